# Optimizing a Trainium2 kernel written in Bass

```python
import math
import jax, jax.numpy as jnp
from jax import lax
import numpy as np

D_MODEL = 1024
BATCH = 8
SEQ = 8192
DEPTH = 2

CHUNK = 64
D_FF = 2816
BRANCH_W = 512
N_BRANCH = 3
CONV_W = 3
H_RET = 4
DK_RET = BRANCH_W // H_RET
DV_RET = BRANCH_W // H_RET
H_ATT = 8
DH_ATT = BRANCH_W // H_ATT
N_PREV_CHUNKS = 8
BAND = (N_PREV_CHUNKS + 1) * CHUNK
REL_CLIP = 128
N_REL = 2 * REL_CLIP + 1
IN_COLS = 3 * BRANCH_W + 4 * BRANCH_W + 3 * BRANCH_W
EPS = 1e-6
NEG_INF = -1e30
ROPE_BASE = 10000.0

kernel_name = "hybrid_gated_conv_retention_chunkattn_macaron"


def _rmsnorm(x, w):
    xf = x.astype(jnp.float32)
    xf = xf * lax.rsqrt(jnp.mean(xf * xf, axis=-1, keepdims=True) + EPS)
    return (xf * w.astype(jnp.float32)).astype(x.dtype)


def _swiglu(h, w_gate, w_up, w_down):
    return (jax.nn.silu(h @ w_gate) * (h @ w_up)) @ w_down


def _short_gated_conv(u, b_gate, c_gate, conv_w):
    z = c_gate * u
    zp = jnp.pad(z, ((0, 0), (CONV_W - 1, 0), (0, 0)))
    s = z.shape[1]
    conv = sum(conv_w[j] * zp[:, j:j + s] for j in range(CONV_W))
    return b_gate * conv


def _rotary(x, cos, sin):
    half = x.shape[-1] // 2
    x1, x2 = x[..., :half], x[..., half:]
    c = cos[None, :, None, :]
    s_ = sin[None, :, None, :]
    return jnp.concatenate([x1 * c - x2 * s_, x1 * s_ + x2 * c], axis=-1).astype(x.dtype)


def _retention(q, k, v, g):
    bsz, s = q.shape[:2]
    nc = s // CHUNK
    log_gamma = jnp.log1p(-jnp.exp2(-5.0 - jnp.arange(H_RET, dtype=jnp.float32)))
    pos = jnp.arange(CHUNK, dtype=jnp.float32)
    d_intra = jnp.exp(log_gamma[:, None, None] * jnp.abs(pos[:, None] - pos[None, :]))
    q_decay = jnp.exp(log_gamma[:, None] * (pos + 1.0))
    k_decay = jnp.exp(log_gamma[:, None] * (CHUNK - 1.0 - pos))
    chunk_decay = jnp.exp(log_gamma * CHUNK)

    def to_chunks(t):
        return t.astype(jnp.float32).reshape(bsz, nc, CHUNK, H_RET, -1).transpose(1, 0, 3, 2, 4)

    qc = to_chunks(q) * (DK_RET ** -0.5)
    kc, vc = to_chunks(k), to_chunks(v)

    def step(state, qkv):
        qb, kb, vb = qkv
        inner = jnp.einsum('bhnk,bhmk->bhnm', qb, kb) * d_intra[None]
        o = jnp.einsum('bhnm,bhmv->bhnv', inner, vb) \
            + jnp.einsum('bhnk,bhkv->bhnv', qb * q_decay[None, :, :, None], state)
        state = state * chunk_decay[None, :, None, None] \
            + jnp.einsum('bhmk,bhmv->bhkv', kb * k_decay[None, :, :, None], vb)
        return state, o

    s0 = jnp.zeros((bsz, H_RET, DK_RET, DV_RET), jnp.float32)
    _, o = lax.scan(step, s0, (qc, kc, vc))
    o = o.transpose(1, 0, 3, 2, 4).reshape(bsz, s, H_RET, DV_RET)
    o = o * lax.rsqrt(jnp.mean(o * o, axis=-1, keepdims=True) + EPS)
    o = o.reshape(bsz, s, BRANCH_W).astype(g.dtype)
    return jax.nn.silu(g) * o


def _chunk_band_attention(q, k, v, rel_bias):
    bsz, s = q.shape[:2]
    nc = s // CHUNK
    pad = N_PREV_CHUNKS * CHUNK
    qc = q.reshape(bsz, nc, CHUNK, H_ATT, DH_ATT).transpose(1, 0, 3, 2, 4)
    kp = jnp.pad(k, ((0, 0), (pad, 0), (0, 0), (0, 0))).transpose(0, 2, 1, 3)
    vp = jnp.pad(v, ((0, 0), (pad, 0), (0, 0), (0, 0))).transpose(0, 2, 1, 3)
    n = jnp.arange(CHUNK)
    m = jnp.arange(BAND)
    dist = (pad + n)[:, None] - m[None, :]
    idx = jnp.clip(dist, -REL_CLIP, REL_CLIP) + REL_CLIP
    bias = rel_bias[:, idx].astype(jnp.float32)
    scale = DH_ATT ** -0.5

    def one_chunk(args):
        c, q_blk = args
        kb = lax.dynamic_slice_in_dim(kp, c * CHUNK, BAND, axis=2)
        vb = lax.dynamic_slice_in_dim(vp, c * CHUNK, BAND, axis=2)
        sc = jnp.einsum('bhnd,bhmd->bhnm', q_blk, kb).astype(jnp.float32) * scale + bias[None]
        valid = m >= (N_PREV_CHUNKS - c) * CHUNK
        sc = jnp.where(valid[None, None, None, :], sc, NEG_INF)
        p = jax.nn.softmax(sc, axis=-1).astype(vb.dtype)
        return jnp.einsum('bhnm,bhmd->bhnd', p, vb)

    out = lax.map(one_chunk, (jnp.arange(nc), qc))
    return out.transpose(1, 0, 3, 2, 4).reshape(bsz, s, BRANCH_W)


def setup_inputs(seed: int = 0) -> dict:
    key = jax.random.key(seed)
    ks = jax.random.split(key, 20)
    f32 = jnp.float32

    def w(k, shape, fan_in):
        return jax.random.normal(k, shape, f32) * (fan_in ** -0.5)

    def gain(k, shape):
        return 1.0 + 0.05 * jax.random.normal(k, shape, f32)

    return {
        "x": jax.random.normal(ks[0], (BATCH, SEQ, D_MODEL), f32),
        "ffn1_norm": gain(ks[1], (DEPTH, D_MODEL)),
        "ffn1_w_gate": w(ks[2], (DEPTH, D_MODEL, D_FF), D_MODEL),
        "ffn1_w_up": w(ks[3], (DEPTH, D_MODEL, D_FF), D_MODEL),
        "ffn1_w_down": w(ks[4], (DEPTH, D_FF, D_MODEL), D_FF),
        "mix_norm": gain(ks[5], (DEPTH, D_MODEL)),
        "w_in": w(ks[6], (DEPTH, D_MODEL, IN_COLS), D_MODEL),
        "conv_w": w(ks[7], (DEPTH, CONV_W, BRANCH_W), CONV_W),
        "rel_bias": 0.5 * jax.random.normal(ks[8], (DEPTH, H_ATT, N_REL), f32),
        "w_branch": w(ks[9], (DEPTH, N_BRANCH, BRANCH_W, D_MODEL), BRANCH_W),
        "w_merge_gate": w(ks[10], (DEPTH, N_BRANCH, D_MODEL, D_MODEL), D_MODEL),
        "w_out": w(ks[11], (DEPTH, D_MODEL, D_MODEL), D_MODEL),
        "ffn2_norm": gain(ks[12], (DEPTH, D_MODEL)),
        "ffn2_w_gate": w(ks[13], (DEPTH, D_MODEL, D_FF), D_MODEL),
        "ffn2_w_up": w(ks[14], (DEPTH, D_MODEL, D_FF), D_MODEL),
        "ffn2_w_down": w(ks[15], (DEPTH, D_FF, D_MODEL), D_FF),
        "final_norm": gain(ks[16], (D_MODEL,)),
    }


def reference(x, ffn1_norm, ffn1_w_gate, ffn1_w_up, ffn1_w_down, mix_norm, w_in, conv_w,
              rel_bias, w_branch, w_merge_gate, w_out, ffn2_norm, ffn2_w_gate, ffn2_w_up,
              ffn2_w_down, final_norm):
    bsz, s, _ = x.shape
    inv_freq = ROPE_BASE ** (-jnp.linspace(0.0, 1.0, DK_RET // 2, dtype=jnp.float32))
    ang = jnp.arange(s, dtype=jnp.float32)[:, None] * inv_freq[None, :]
    cos, sin = jnp.cos(ang), jnp.sin(ang)
    split_pts = [BRANCH_W * i for i in range(1, IN_COLS // BRANCH_W)]

    for l in range(DEPTH):
        x = x + 0.5 * _swiglu(_rmsnorm(x, ffn1_norm[l]), ffn1_w_gate[l], ffn1_w_up[l], ffn1_w_down[l])

        h = _rmsnorm(x, mix_norm[l])
        cols = jnp.split(h @ w_in[l], split_pts, axis=-1)
        cu, cb, cc, rq, rk, rv, rg, aq, ak, av = cols

        y_conv = _short_gated_conv(cu, cb, cc, conv_w[l])

        rq = _rotary(rq.reshape(bsz, s, H_RET, DK_RET), cos, sin)
        rk = _rotary(rk.reshape(bsz, s, H_RET, DK_RET), cos, sin)
        y_ret = _retention(rq, rk, rv.reshape(bsz, s, H_RET, DV_RET), rg)

        y_att = _chunk_band_attention(aq.reshape(bsz, s, H_ATT, DH_ATT),
                                      ak.reshape(bsz, s, H_ATT, DH_ATT),
                                      av.reshape(bsz, s, H_ATT, DH_ATT), rel_bias[l])

        merged = sum(jax.nn.sigmoid(h @ w_merge_gate[l, i]) * (y @ w_branch[l, i])
                     for i, y in enumerate((y_conv, y_ret, y_att)))
        x = x + merged @ w_out[l]

        x = x + 0.5 * _swiglu(_rmsnorm(x, ffn2_norm[l]), ffn2_w_gate[l], ffn2_w_up[l], ffn2_w_down[l])

    return _rmsnorm(x, final_norm)
```

```python
import bisect
from contextlib import ExitStack

import numpy as np
import concourse.bass as bass
import concourse.mybir as mybir
from concourse.bass_utils import run_bass_kernel_spmd

F32 = mybir.dt.float32
BF16 = mybir.dt.bfloat16
AF = mybir.ActivationFunctionType
ALU = mybir.AluOpType

D = 1024
DFF = 2816
BW = 512
L = 2
TT = 512
NCH = D // 128
NJ = DFF // 128
EPS = 1e-6
H_RET = 4
H_ATT = 8
NEG = -30000.0


class Arena:
    def __init__(self, size):
        self.los = [0]
        self.segs = [[0, size, None, {}]]

    def _split(self, pos):
        i = bisect.bisect_right(self.los, pos) - 1
        s = self.segs[i]
        if s[0] < pos < s[1]:
            ns = [pos, s[1], s[2], dict(s[3])]
            s[1] = pos
            self.segs.insert(i + 1, ns)
            self.los.insert(i + 1, pos)

    def access(self, lo, hi, op, write):
        self._split(lo)
        self._split(hi)
        i0 = bisect.bisect_left(self.los, lo)
        i1 = bisect.bisect_left(self.los, hi)
        deps = []
        for i in range(i0, i1):
            s = self.segs[i]
            if s[2] is not None:
                deps.append((s[2], 'w'))
            if write:
                for r in s[3].values():
                    deps.append((r, 'r'))
        if write:
            self.segs[i0:i1] = [[lo, hi, op, {}]]
            self.los[i0:i1] = [lo]
        else:
            key = op.eng if op.dma is None else ('dma', op.uid)
            for i in range(i0, i1):
                self.segs[i][3][key] = op
        return deps


class View:
    def __init__(self, ap, arena, lo, hi):
        self.ap, self.arena, self.lo, self.hi = ap, arena, lo, hi

    def sub(self, i):
        n = self.ap.shape[1]
        step = (self.hi - self.lo) // n
        return View(self.ap[:, i], self.arena, self.lo + i * step, self.lo + (i + 1) * step)

    def cols(self, a, b):
        n = self.ap.shape[-1]
        assert len(self.ap.shape) == 2
        step = (self.hi - self.lo) // n
        return View(self.ap[:, a:b], self.arena, self.lo + a * step, self.lo + b * step)

    def with_ap(self, ap):
        return View(ap, self.arena, self.lo, self.hi)


class Op:
    __slots__ = ('eng', 'fn', 'deps', 'signal', 'sigidx', 'dma', 'uid')


class Prog:
    ENGS = ('pe', 'act', 'dve', 'pool', 'sp')

    def __init__(self, nc):
        self.nc = nc
        self.arenas = {}
        self.ops = {e: [] for e in self.ENGS}
        self.chan_count = {}
        self.uid = 0

    def arena(self, name, size):
        self.arenas[name] = Arena(size)
        return name

    def add(self, eng, fn, reads=(), writes=(), chan=None):
        op = Op()
        op.eng, op.fn, op.signal, op.sigidx = eng, fn, False, None
        op.uid = self.uid
        self.uid += 1
        if chan is not None:
            self.chan_count[chan] = self.chan_count.get(chan, 0) + 16
            op.dma = (chan, self.chan_count[chan])
        else:
            op.dma = None
        raw = []
        for v in reads:
            raw += self.arenas[v.arena].access(v.lo, v.hi, op, v.arena == 'psum')
        for v in writes:
            raw += self.arenas[v.arena].access(v.lo, v.hi, op, True)
        deps = {}
        for d, kind in raw:
            if d is op:
                continue
            if d.dma is None and op.dma is None and d.eng == eng:
                if eng == 'pe' or kind == 'r':
                    continue
            deps[d.uid] = d
            if d.dma is None:
                d.signal = True
        op.deps = list(deps.values())
        self.ops[eng].append(op)
        return op

    def emit(self, block, sems, chan_sems):
        nc = self.nc
        for e in self.ENGS:
            k = 0
            for op in self.ops[e]:
                if op.dma is None and op.signal:
                    k += 1
                    op.sigidx = k
        engobj = {'pe': 'tensor', 'act': 'scalar', 'dve': 'vector', 'pool': 'gpsimd', 'sp': 'sync'}

        def run(e):
            def body(eng):
                seen = {}
                for op in self.ops[e]:
                    for d in op.deps:
                        if d.dma is not None:
                            key, val = ('c', d.dma[0]), d.dma[1]
                            sem = chan_sems[d.dma[0]]
                        else:
                            key, val = ('e', d.eng), d.sigidx
                            sem = sems[d.eng]
                        if seen.get(key, 0) >= val:
                            continue
                        seen[key] = val
                        eng.wait_ge(sem, val)
                    if op.fn is None:
                        continue
                    inst = op.fn(eng)
                    if op.dma is not None:
                        inst.then_inc(chan_sems[op.dma[0]], 16)
                    elif op.signal:
                        inst.then_inc(sems[e], 1)
            return body

        for e in self.ENGS:
            if self.ops[e]:
                getattr(block, engobj[e])(run(e))


def _chunked_vec(v):
    v = np.asarray(v, np.float32)
    lead = v.shape[:-1]
    return np.ascontiguousarray(np.moveaxis(v.reshape(lead + (v.shape[-1] // 128, 128)), -1, 0))


def _const_tables(S):
    inv_freq = (np.float32(10000.0) ** (-np.linspace(0.0, 1.0, 64, dtype=np.float32))).astype(np.float32)
    ang = (np.arange(S, dtype=np.float32)[:, None] * inv_freq[None, :]).astype(np.float32)
    cos = np.cos(ang.astype(np.float64)).astype(np.float32)
    sin = np.sin(ang.astype(np.float64)).astype(np.float32)
    lg = np.log1p(-np.exp2(-5.0 - np.arange(H_RET, dtype=np.float64)))
    p = np.arange(128)[:, None]
    j = np.arange(TT)[None, :]
    scale = 128.0 ** -0.5
    DT = np.zeros((128, H_RET, TT), np.float64)
    for hd in range(H_RET):
        m = np.exp(lg[hd] * np.abs(j - p)) * ((p // 64) <= (j // 64))
        DT[:, hd, :] = m * scale
    qdec = np.exp(lg[:, None] * (np.arange(TT)[None, :] + 1.0)) * scale
    qdecT = np.broadcast_to(qdec[None], (128, H_RET, TT))
    pos = np.arange(4)[None, :, None] * 128 + np.arange(128)[:, None, None]
    kdec = np.exp(lg[None, None, :] * (TT - 1.0 - pos))
    tile_decay = [float(np.exp(lg[hd] * TT)) for hd in range(H_RET)]
    return dict(rope_cos=cos, rope_sin=sin,
                DT=np.ascontiguousarray(DT.reshape(128, -1), np.float32),
                qdecT=np.ascontiguousarray(qdecT.reshape(128, -1), np.float32),
                kdec=np.ascontiguousarray(kdec.reshape(128, -1), np.float32)), tile_decay


def _gather_bias(rel_bias):
    nl = rel_bias.shape[0]
    mp = np.arange(128)[:, None]
    t = np.arange(256)[None, :]
    idx = np.clip(t - mp, -128, 128) + 128
    valid = ((t // 64) - (mp // 64)) >= 0
    G = np.empty((nl, 128, H_ATT, 256), np.float32)
    for l in range(nl):
        for hd in range(H_ATT):
            G[l, :, hd, :] = np.where(valid, rel_bias[l, hd][idx], np.float32(NEG))
    fb = np.broadcast_to(rel_bias[:, :, 256].reshape(1, nl * H_ATT), (128, nl * H_ATT))
    return np.ascontiguousarray(G.reshape(nl, 128, -1)), np.ascontiguousarray(fb, np.float32)


_LG = np.log1p(-np.exp2(-5.0 - np.arange(H_RET, dtype=np.float64)))
TILE_DECAY = [float(np.exp(_LG[hd] * TT)) for hd in range(H_RET)]


class Builder:
    def __init__(self, S, n_layers=L, stages=('ffn1', 'mix', 'ffn2')):
        self.S = S
        self.NT = S // TT
        self.n_layers = n_layers
        self.stages = stages
        self.nc = bass.Bass("TRN2", target_bir_lowering=False)
        self.P = Prog(self.nc)
        self.es = ExitStack()
        self.chans = []
        import os
        self.dbg = set(os.environ.get('MIXDBG', 'conv,ret,att,merge').split(','))
        self.retcut = int(os.environ.get('RETCUT', '99'))

    def din(self, name, shape, dtype=F32):
        t = self.nc.dram_tensor(name, list(shape), dtype, kind="ExternalInput")
        self.P.arena('d_' + name, 1)
        return View(t.ap(), 'd_' + name, 0, 1)

    def dscratch(self, name, shape, dtype):
        t = self.nc.dram_tensor(name, list(shape), dtype, kind="Internal")
        self.P.arena('d_' + name, 1)
        return View(t.ap(), 'd_' + name, 0, 1)

    @staticmethod
    def _shape_ap(ap, free_shape):
        if len(free_shape) == 2:
            ap = ap.rearrange("p (a b) -> p a b", a=free_shape[0])
        elif len(free_shape) == 3:
            ap = ap.rearrange("p (a b c) -> p a b c", a=free_shape[0], b=free_shape[1])
        return ap

    def sb(self, name, free_shape, dtype):
        n = int(np.prod(free_shape))
        t = self.es.enter_context(self.nc.sbuf_tensor(name, [128, n], dtype))
        esz = 2 if dtype == BF16 else 4
        self.P.arena(name, n * esz)
        return View(self._shape_ap(t[:, :], free_shape), name, 0, n * esz)

    def scr(self, off, free_shape, dtype):
        n = int(np.prod(free_shape))
        esz = 2 if dtype == BF16 else 4
        assert off % 4 == 0 and (n * esz) % 4 == 0 and off + n * esz <= self.scr_bytes, (off, n, esz)
        ap = self.scr_t[:, off // 4:(off + n * esz) // 4]
        if dtype == BF16:
            ap = ap.bitcast(BF16)
        return View(self._shape_ap(ap, free_shape), 'scr', off, off + n * esz)

    def chan(self, name):
        self.chans.append(name)
        return name

    def mm(self, out, lhsT, rhs, start, stop, reads, writes):
        self.P.add('pe', lambda e: e.matmul(out, lhsT, rhs, start=start, stop=stop), reads, writes)

    def dma(self, out_v, in_v, chan, out_ap=None, in_ap=None, q='sp'):
        oa = out_v.ap if out_ap is None else out_ap
        ia = in_v.ap if in_ap is None else in_ap
        return self.P.add(q, lambda e: e.dma_start(out=oa, in_=ia), [in_v], [out_v], chan=chan)

    def act_fn(self, out_ap, in_ap, func, reads, writes, **kw):
        self.P.add('act', lambda e: e.activation(out=out_ap, in_=in_ap, func=func, **kw), reads, writes)

    def tt(self, eng, out_ap, in0, in1, op, reads, writes):
        self.P.add(eng, lambda e: e.tensor_tensor(out=out_ap, in0=in0, in1=in1, op=op), reads, writes)

    def copy(self, eng, out_ap, in_ap, reads, writes):
        if eng == 'act':
            self.P.add('act', lambda e: e.copy(out=out_ap, in_=in_ap), reads, writes)
        else:
            self.P.add(eng, lambda e: e.tensor_copy(out=out_ap, in_=in_ap), reads, writes)

    def build(self):
        nc, P = self.nc, self.P
        S, NT, NL = self.S, self.NT, self.n_layers
        K1 = 1024
        self.scr_bytes = 70 * K1
        has_mix = 'mix' in self.stages
        x = self.din("x", [S, D])
        out = self.nc.dram_tensor("out", [S, D], F32, kind="ExternalOutput")
        P.arena('d_out', 1)
        outv = View(out.ap(), 'd_out', 0, 1)
        wf = {}
        for f in (1, 2):
            wf[f] = dict(g=self.din(f"ffn{f}_w_gate", [NL, D, DFF]),
                         u=self.din(f"ffn{f}_w_up", [NL, D, DFF]),
                         d=self.din(f"ffn{f}_w_down", [NL, DFF, D]))
        w_in_d = self.din("w_in", [NL, D, 10 * BW])
        w_br_d = self.din("w_branch", [NL, 3, BW, D])
        w_mg_d = self.din("w_merge_gate", [NL, 3, D, D])
        w_out_d = self.din("w_out", [NL, D, D])
        normw_d = self.din("normw", [128, (3 * NL + 1) * NCH])
        ident_d = self.din("ident", [128, 128])
        convw_d = self.din("convw", [128, NL * 4 * 3])
        cos_d = self.din("rope_cos", [S, 64])
        sin_d = self.din("rope_sin", [S, 64])
        DT_d = self.din("DT", [128, H_RET * TT])
        qdecT_d = self.din("qdecT", [128, H_RET * TT])
        kdec_d = self.din("kdec", [128, 4 * H_RET])
        gtab_d = self.din("gtab", [NL, 128, H_ATT * 256])
        fbias_d = self.din("fbias", [128, NL * H_ATT])

        units = {}
        for l in range(NL):
            for f in (1, 2):
                for u in range(11):
                    units[('gu', l, f, u)] = self.dscratch(f"s_gu_{l}_{f}_{u}", [128, 4096], BF16)
                for cp in range(4):
                    for jh in range(2):
                        units[('dn', l, f, cp, jh)] = self.dscratch(f"s_dn_{l}_{f}_{cp}_{jh}", [128, 2816], BF16)
            for i in range(4):
                units[('cv', l, i)] = self.dscratch(f"s_cv_{l}_{i}", [128, 3072], BF16)
            for blk in range(3, 10):
                units[('win', l, blk)] = self.dscratch(f"s_win_{l}_{blk}", [128, 4096], BF16)
            for cp in range(4):
                for i in range(3):
                    units[('mg', l, cp, i)] = self.dscratch(f"s_mg_{l}_{cp}_{i}", [128, 3072], BF16)
                units[('wo', l, cp)] = self.dscratch(f"s_wo_{l}_{cp}", [128, 2048], BF16)

        xT = self.sb("xT", [NCH, TT], F32)
        h = self.sb("h", [NCH, TT], BF16)
        stdb = self.sb("std", [TT], F32)
        rstd = self.sb("rstd", [TT], F32)
        normw = self.sb("normw_sb", [(3 * NL + 1) * NCH], F32)
        ident = self.sb("ident_sb", [128], F32)
        ident_bf = self.sb("ident_bf", [128], BF16)
        ones_bf = self.sb("ones_bf", [128], BF16)
        NSLOT = 4
        slots = [self.sb(f"wslot{i}", [4096], BF16) for i in range(NSLOT)]
        slot_ch = [self.chan(f"c_slot{i}") for i in range(NSLOT)]
        if has_mix:
            convw = self.sb("convw_sb", [NL * 4 * 3], F32)
            cosT = self.sb("cosT", [4, 64], F32)
            sinT = self.sb("sinT", [4, 64], F32)
            DT = self.sb("DT_sb", [H_RET, TT], F32)
            qdecT = self.sb("qdecT_sb", [H_RET, TT], F32)
            kdec = self.sb("kdec_sb", [4 * H_RET], F32)
            fbias = self.sb("fbias_sb", [NL * H_ATT], F32)
            carry = [self.sb(f"carry{l}", [4, 2], F32) for l in range(NL)]
            state_f = [self.sb(f"state_f{l}", [H_RET, 128], F32) for l in range(NL)]
            state_b = [self.sb(f"state_b{l}", [H_RET, 128], BF16) for l in range(NL)]
            kT_att = [self.sb(f"kT_att{l}", [2, 4, TT], BF16) for l in range(NL)]
            v_att = [self.sb(f"v_att{l}", [2, 4, TT], BF16) for l in range(NL)]
            gt = self.sb("gt", [H_ATT, 256], F32)
        self.scr_t = self.es.enter_context(nc.sbuf_tensor("scr", [128, self.scr_bytes // 4], F32))
        P.arena('scr', self.scr_bytes)
        stg32 = [self.scr(i * 16 * K1, [4096], F32) for i in range(2)]
        stg16 = [self.scr(32 * K1 + i * 8 * K1, [4096], BF16) for i in range(2)]
        stg32_ch = [self.chan(f"c_stg32_{i}") for i in range(2)]
        stg16_ch = [self.chan(f"c_stg16_{i}") for i in range(2)]
        act = self.scr(0, [NJ, TT], BF16)
        sgb = [self.scr(22 * K1 + i * 2 * K1, [TT], F32) for i in range(2)]
        sqr = [self.scr(26 * K1 + i * K1, [TT], BF16) for i in range(2)]
        outT = self.scr(0, [NCH, TT], F32)
        xo = self.scr(16 * K1, [4, D], F32)
        xs = self.scr(32 * K1, [4, D], F32)
        if has_mix:
            yconv = self.scr(0, [4, TT], BF16)
            yret = self.scr(4 * K1, [4, TT], BF16)
            yatt = self.scr(8 * K1, [4, TT], BF16)
            sgate = self.scr(12 * K1, [4, TT], F32)
            cu_sb = self.scr(20 * K1, [TT], F32)
            zb = self.scr(22 * K1, [TT + 2], F32)
            cacc = self.scr(25 * K1, [TT], F32)
            q_tok = self.scr(27 * K1, [4, BW], F32)
            k_tok = self.scr(35 * K1, [4, BW], F32)
            Kd_tok = self.scr(43 * K1, [4, BW], BF16)
            V_tok = self.scr(47 * K1, [4, BW], BF16)
            rt = [self.scr(51 * K1 + i * K1, [4, 64], F32) for i in range(2)]
            QT = [self.scr(53 * K1 + i * K1, [TT], BF16) for i in range(2)]
            QdT = [self.scr(55 * K1 + i * K1, [TT], BF16) for i in range(2)]
            KT = [self.scr(57 * K1 + i * K1, [TT], BF16) for i in range(2)]
            ATb = [self.scr(59 * K1 + i * 2560, [1280], BF16) for i in range(2)]
            rn = self.scr(64 * K1, [TT], F32)
            osq = self.scr(66 * K1, [TT], BF16)
            rgm = self.scr(67 * K1, [TT], F32)
            qT_att = self.scr(20 * K1, [4, TT], BF16)
            stmp = [self.scr(60 * K1 + i * 2 * K1, [TT], F32) for i in range(4)]
            PT = [self.scr(54 * K1 + i * K1, [TT], BF16) for i in range(6)]
            rec = self.scr(32 * K1, [TT], F32)
            sig = [self.scr(34 * K1 + i * 2 * K1, [TT], F32) for i in range(2)]
            prd = [self.scr(38 * K1 + i * 2 * K1, [TT], F32) for i in range(2)]
            macc = [self.scr(42 * K1 + i * 2 * K1, [TT], F32) for i in range(2)]
            mT = self.scr(46 * K1, [NCH, TT], BF16)
        pst = self.es.enter_context(nc.psum_tensor("psum", [128, 8 * 512], F32))
        P.arena('psum', 8 * 2048)
        banks = [View(pst[:, b * 512:(b + 1) * 512], 'psum', b * 2048, (b + 1) * 2048) for b in range(8)]
        self.bank_i = 0

        self.held = set()

        def bank(hold=False):
            while (self.bank_i % 8) in self.held:
                self.bank_i += 1
            k = self.bank_i % 8
            self.bank_i += 1
            if hold:
                self.held.add(k)
            return banks[k]

        def release(bv):
            self.held.discard(bv.lo // 2048)

        def const_load(dst, src):
            self.dma(dst, src, self.chan(f"c_const{len(self.chans)}"))

        const_load(normw, normw_d)
        const_load(ident, ident_d)
        P.add('pool', lambda e: e.memset(ones_bf.ap, 1.0), [], [ones_bf])
        self.copy('dve', ident_bf.ap, ident.ap, [ident], [ident_bf])
        if has_mix:
            const_load(convw, convw_d)
            const_load(DT, DT_d.with_ap(DT_d.ap.rearrange("p (a b) -> p a b", a=H_RET)))
            const_load(qdecT, qdecT_d.with_ap(qdecT_d.ap.rearrange("p (a b) -> p a b", a=H_RET)))
            const_load(kdec, kdec_d)
            const_load(fbias, fbias_d)

        self.pp_i = 0
        cast_engs = ['act', 'dve', 'pool']

        def prepass_unit(unit_v, pieces):
            i = self.pp_i % 2
            self.pp_i += 1
            s32, s16 = stg32[i], stg16[i]
            n_tot = sum(a * b for (_, _, _, a, b) in pieces)
            assert n_tot <= 4096
            for (sv, sap, off, a, b) in pieces:
                dst = s32.ap[:, off:off + a * b].rearrange("p (a b) -> p a b", a=a)
                self.dma(s32, sv, stg32_ch[i], out_ap=dst, in_ap=sap)
            ce = cast_engs[self.pp_i % 3]
            self.copy(ce, s16.ap[:, 0:n_tot], s32.ap[:, 0:n_tot], [s32], [s16])
            self.dma(unit_v, s16, stg16_ch[i], out_ap=unit_v.ap[:, 0:n_tot], in_ap=s16.ap[:, 0:n_tot])

        for l in range(NL):
            for f in (1, 2):
                if f"ffn{f}" not in self.stages:
                    continue
                g, u_, d_ = wf[f]['g'], wf[f]['u'], wf[f]['d']
                gl = g.ap[l].rearrange("(kc p) n -> p kc n", p=128)
                ul = u_.ap[l].rearrange("(kc p) n -> p kc n", p=128)
                dl = d_.ap[l].rearrange("(j p) n -> p j n", p=128)
                for u in range(11):
                    prepass_unit(units[('gu', l, f, u)],
                                 [(g, gl[:, :, u * 256:(u + 1) * 256], 0, 8, 256),
                                  (u_, ul[:, :, u * 256:(u + 1) * 256], 2048, 8, 256)])
                for cp in range(4):
                    for jh in range(2):
                        prepass_unit(units[('dn', l, f, cp, jh)],
                                     [(d_, dl[:, jh * 11:(jh + 1) * 11, cp * 256:(cp + 1) * 256], 0, 11, 256)])
            if has_mix:
                wl = w_in_d.ap[l].rearrange("(kc p) n -> p kc n", p=128)
                for i in range(4):
                    prepass_unit(units[('cv', l, i)],
                                 [(w_in_d, wl[:, :, g_ * BW + i * 128: g_ * BW + (i + 1) * 128], k_ * 1024, 8, 128)
                                  for k_, g_ in enumerate((0, 1, 2))])
                for blk in range(3, 10):
                    prepass_unit(units[('win', l, blk)], [(w_in_d, wl[:, :, blk * BW:(blk + 1) * BW], 0, 8, BW)])
                for cp in range(4):
                    for i in range(3):
                        mgl = w_mg_d.ap[l, i].rearrange("(kc p) n -> p kc n", p=128)
                        brl = w_br_d.ap[l, i].rearrange("(kk p) n -> p kk n", p=128)
                        prepass_unit(units[('mg', l, cp, i)],
                                     [(w_mg_d, mgl[:, :, cp * 256:(cp + 1) * 256], 0, 8, 256),
                                      (w_br_d, brl[:, :, cp * 256:(cp + 1) * 256], 2048, 4, 256)])
                    wol = w_out_d.ap[l].rearrange("(kc p) n -> p kc n", p=128)
                    prepass_unit(units[('wo', l, cp)], [(w_out_d, wol[:, :, cp * 256:(cp + 1) * 256], 0, 8, 256)])

        self.slot_i = 0

        def load_unit(key, n):
            i = self.slot_i % NSLOT
            self.slot_i += 1
            sl = slots[i]
            uv = units[key]
            self.dma(sl, uv, slot_ch[i], out_ap=sl.ap[:, 0:n], in_ap=uv.ap[:, 0:n])
            return sl

        def unit_seq():
            seq = []
            for l in range(NL):
                for st in self.stages:
                    if st in ('ffn1', 'ffn2'):
                        f = 1 if st == 'ffn1' else 2
                        for u in range(11):
                            seq.append((('gu', l, f, u), 4096))
                        for cp in range(4):
                            for jh in range(2):
                                seq.append((('dn', l, f, cp, jh), 2816))
                    elif st == 'mix':
                        if 'conv' in self.dbg:
                            for i in range(4):
                                seq.append((('cv', l, i), 3072))
                        if 'ret' in self.dbg:
                            for blk in range(3, 7):
                                seq.append((('win', l, blk), 4096))
                        if 'att' in self.dbg:
                            for blk in range(7, 10):
                                seq.append((('win', l, blk), 4096))
                        if 'merge' in self.dbg:
                            for cp in range(4):
                                for i in range(3):
                                    seq.append((('mg', l, cp, i), 3072))
                            for cp in range(4):
                                seq.append((('wo', l, cp), 2048))
            return seq

        useq = unit_seq() * NT
        self.u_next = 0
        self.u_cons = 0
        self.loaded = {}
        PREFETCH = NSLOT - 1

        def prefetch_to(k):
            while self.u_next < min(len(useq), k + 1):
                key, n = useq[self.u_next]
                self.loaded[self.u_next] = load_unit(key, n)
                self.u_next += 1

        def next_unit(expect_kind):
            k = self.u_cons
            assert useq[k][0][0] == expect_kind, (useq[k], expect_kind)
            prefetch_to(k)
            sl = self.loaded.pop(k)
            self.u_cons += 1
            return sl

        def after_unit():
            prefetch_to(self.u_cons + PREFETCH - 1)

        self.nps = None
        self.nk = 0

        def norm_chunk(c):
            if self.nk == 0:
                self.nps = bank(hold=True)
            sqv = sqr[self.nk % 2]
            xc = xT.sub(c)
            self.act_fn(sqv.ap, xc.ap, AF.Square, [xc], [sqv])
            self.mm(self.nps.ap, ones_bf.ap, sqv.ap, self.nk == 0, self.nk == NCH - 1, [ones_bf, sqv], [self.nps])
            self.nk += 1

        def rmsnorm(widx, out_bf16=True):
            assert self.nk == NCH
            ps = self.nps
            self.nk = 0
            self.act_fn(stdb.ap, ps.ap, AF.Ln, [ps], [stdb], scale=1.0 / D, bias=EPS)
            release(ps)
            self.act_fn(rstd.ap, stdb.ap, AF.Exp, [stdb], [rstd], scale=-0.5)
            for c in range(NCH):
                xc = xT.sub(c)
                dst = h.sub(c) if out_bf16 else outT.sub(c)
                wap = normw.ap[:, widx * NCH + c: widx * NCH + c + 1]
                P.add('dve', lambda e, xc=xc, dst=dst, wap=wap: e.scalar_tensor_tensor(
                    out=dst.ap, in0=xc.ap, scalar=wap, in1=rstd.ap, op0=ALU.mult, op1=ALU.mult),
                    [xc, rstd, normw], [dst])

        def proj_fm(sl, wv, ncol0, ps, kchunks=NCH, rhs_src=None):
            src = h if rhs_src is None else rhs_src
            for kc in range(kchunks):
                sk = src.sub(kc)
                self.mm(ps.ap, wv[:, kc, ncol0:ncol0 + 128], sk.ap, kc == 0, kc == kchunks - 1, [sl, sk], [ps])

        def proj_tm(sl, wv, tb, ps):
            for kc in range(NCH):
                hk = h.sub(kc)
                self.mm(ps.ap, hk.ap[:, tb * 128:(tb + 1) * 128], wv[:, kc, :], kc == 0, kc == NCH - 1,
                        [sl, hk], [ps])

        def ffn(l, f):
            rmsnorm(3 * l + (0 if f == 1 else 2))
            for u in range(11):
                sl = next_unit('gu')
                w = sl.ap.rearrange("p (g kc n) -> p g kc n", g=2, kc=8)
                for jj in range(2):
                    j = 2 * u + jj
                    pg, pu = bank(), bank()
                    proj_fm(sl, w[:, 0], jj * 128, pg)
                    proj_fm(sl, w[:, 1], jj * 128, pu)
                    sg = sgb[j % 2]
                    aj = act.sub(j)
                    self.act_fn(sg.ap, pg.ap, AF.Silu, [pg], [sg])
                    self.tt('dve', aj.ap, pu.ap, sg.ap, ALU.mult, [pu, sg], [aj])
                after_unit()
            for cp in range(4):
                pcs = [bank(), bank()]
                for jh in range(2):
                    sl = next_unit('dn')
                    w = sl.ap[:, 0:2816].rearrange("p (j n) -> p j n", j=11)
                    for jj in range(11):
                        j = jh * 11 + jj
                        aj = act.sub(j)
                        for cc in range(2):
                            self.mm(pcs[cc].ap, w[:, jj, cc * 128:(cc + 1) * 128], aj.ap, j == 0, j == NJ - 1,
                                    [sl, aj], [pcs[cc]])
                    after_unit()
                for cc in range(2):
                    xc = xT.sub(cp * 2 + cc)
                    pc = pcs[cc]
                    P.add('dve', lambda e, xc=xc, pc=pc: e.scalar_tensor_tensor(
                        out=xc.ap, in0=pc.ap, scalar=0.5, in1=xc.ap, op0=ALU.mult, op1=ALU.add),
                        [pc, xc], [xc])
                    norm_chunk(cp * 2 + cc)

        def mixer(l, t):
            rmsnorm(3 * l + 1)
            self.dma(gt, gtab_d, c_gt, in_ap=gtab_d.ap[l].rearrange("p (a b) -> p a b", a=H_ATT))
            cur, prv = t % 2, (t + 1) % 2
            if 'conv' not in self.dbg:
                P.add('pool', lambda e: e.memset(yconv.ap, 0.0), [], [yconv])
            for i in (range(4) if 'conv' in self.dbg else []):
                sl = next_unit('cv')
                w = sl.ap[:, 0:3072].rearrange("p (g kc n) -> p g kc n", g=3, kc=8)
                pu, pb, pc = bank(), bank(), bank()
                proj_fm(sl, w[:, 0], 0, pu)
                proj_fm(sl, w[:, 2], 0, pc)
                proj_fm(sl, w[:, 1], 0, pb)
                after_unit()
                self.copy('act', cu_sb.ap, pu.ap, [pu], [cu_sb])
                ci = carry[l].sub(i)
                if t == 0:
                    P.add('pool', lambda e: e.memset(zb.ap[:, 0:2], 0.0), [], [zb])
                else:
                    self.copy('pool', zb.ap[:, 0:2], ci.ap, [ci], [zb])
                self.tt('dve', zb.ap[:, 2:TT + 2], pc.ap, cu_sb.ap, ALU.mult, [pc, cu_sb, zb], [zb])
                self.copy('pool', ci.ap, zb.ap[:, TT:TT + 2], [zb], [ci])
                wi = (l * 4 + i) * 3
                w0, w1, w2 = (convw.ap[:, wi + k:wi + k + 1] for k in range(3))
                P.add('dve', lambda e, w0=w0: e.tensor_scalar(
                    out=cacc.ap, in0=zb.ap[:, 0:TT], scalar1=w0, scalar2=None, op0=ALU.mult), [zb, convw], [cacc])
                P.add('dve', lambda e, w1=w1: e.scalar_tensor_tensor(
                    out=cacc.ap, in0=zb.ap[:, 1:TT + 1], scalar=w1, in1=cacc.ap, op0=ALU.mult, op1=ALU.add),
                    [zb, convw, cacc], [cacc])
                P.add('dve', lambda e, w2=w2: e.scalar_tensor_tensor(
                    out=cacc.ap, in0=zb.ap[:, 2:TT + 2], scalar=w2, in1=cacc.ap, op0=ALU.mult, op1=ALU.add),
                    [zb, convw, cacc], [cacc])
                yc = yconv.sub(i)
                self.tt('dve', yc.ap, pb.ap, cacc.ap, ALU.mult, [pb, cacc], [yc])
            def _sec_ret():
                for which, dst_tok in ((0, q_tok), (1, k_tok)):
                    sl = next_unit('win')
                    w = sl.ap.rearrange("p (kc n) -> p kc n", kc=8)
                    for tb in range(4):
                        ps = bank()
                        proj_tm(sl, w, tb, ps)
                        psv = ps.ap.rearrange("p (h two f) -> p h two f", h=H_RET, two=2)
                        x1, x2 = psv[:, :, 0, :], psv[:, :, 1, :]
                        dt_ = dst_tok.sub(tb)
                        dv = dt_.ap.rearrange("p (h two f) -> p h two f", h=H_RET, two=2)
                        cb_ = cosT.ap[:, tb, :].unsqueeze(1).broadcast_to([128, H_RET, 64])
                        sb_ = sinT.ap[:, tb, :].unsqueeze(1).broadcast_to([128, H_RET, 64])
                        t1, t2 = rt[0], rt[1]
                        self.tt('dve', t1.ap, x1, cb_, ALU.mult, [ps, cosT], [t1])
                        self.tt('dve', t2.ap, x2, sb_, ALU.mult, [ps, sinT], [t2])
                        self.tt('pool', dv[:, :, 0, :], t1.ap, t2.ap, ALU.subtract, [t1, t2], [dt_])
                        self.tt('dve', t1.ap, x1, sb_, ALU.mult, [ps, sinT], [t1])
                        self.tt('dve', t2.ap, x2, cb_, ALU.mult, [ps, cosT], [t2])
                        self.tt('pool', dv[:, :, 1, :], t1.ap, t2.ap, ALU.add, [t1, t2], [dt_])
                    after_unit()
                if self.retcut == 1:
                    for _ in range(2):
                        next_unit('win'); after_unit()
                    P.add('pool', lambda e: e.memset(yret.ap, 0.0), [], [yret])
                    return
                for tb in range(4):
                    kt_, kd_ = k_tok.sub(tb), Kd_tok.sub(tb)
                    for hd in range(H_RET):
                        sc = kdec.ap[:, tb * H_RET + hd: tb * H_RET + hd + 1]
                        P.add('act', lambda e, kt_=kt_, kd_=kd_, hd=hd, sc=sc: e.activation(
                            out=kd_.ap[:, hd * 128:(hd + 1) * 128], in_=kt_.ap[:, hd * 128:(hd + 1) * 128],
                            func=AF.Copy, scale=sc), [kt_, kdec], [kd_])
                if self.retcut == 2:
                    for _ in range(2):
                        next_unit('win'); after_unit()
                    P.add('pool', lambda e: e.memset(yret.ap, 0.0), [], [yret])
                    return
                sl = next_unit('win')
                w = sl.ap.rearrange("p (kc n) -> p kc n", kc=8)
                for tb in range(4):
                    ps = bank()
                    proj_tm(sl, w, tb, ps)
                    vt = V_tok.sub(tb)
                    self.copy('act', vt.ap, ps.ap, [ps], [vt])
                after_unit()
                if self.retcut == 3:
                    for _ in range(1):
                        next_unit('win'); after_unit()
                    P.add('pool', lambda e: e.memset(yret.ap, 0.0), [], [yret])
                    return
                sl = next_unit('win')
                w = sl.ap.rearrange("p (kc n) -> p kc n", kc=8)
                for i in range(4):
                    ps = bank()
                    proj_fm(sl, w, i * 128, ps)
                    sgi = sgate.sub(i)
                    self.act_fn(sgi.ap, ps.ap, AF.Silu, [ps], [sgi])
                after_unit()
                if self.retcut == 4:
                    for _ in range(0):
                        next_unit('win'); after_unit()
                    P.add('pool', lambda e: e.memset(yret.ap, 0.0), [], [yret])
                    return
                offs = [0, 512, 896, 1152]
                ops_ = {}

                def st_T(hd):
                    hs = slice(hd * 128, (hd + 1) * 128)
                    qt_, qd_, kt_ = QT[hd % 2], QdT[hd % 2], KT[hd % 2]
                    psq, psk = bank(), bank()
                    for tb in range(4):
                        qs, ks = q_tok.sub(tb), k_tok.sub(tb)
                        P.add('pe', lambda e, psq=psq, qs=qs, tb=tb, hs=hs: e.transpose(
                            out=psq.ap[:, tb * 128:(tb + 1) * 128], in_=qs.ap[:, hs], identity=ident.ap),
                            [qs, ident], [psq])
                        P.add('pe', lambda e, psk=psk, ks=ks, tb=tb, hs=hs: e.transpose(
                            out=psk.ap[:, tb * 128:(tb + 1) * 128], in_=ks.ap[:, hs], identity=ident.ap),
                            [ks, ident], [psk])
                    self.copy('act', qt_.ap, psq.ap, [psq], [qt_])
                    self.tt('dve', qd_.ap, psq.ap, qdecT.ap[:, hd, :], ALU.mult, [psq, qdecT], [qd_])
                    self.copy('act', kt_.ap, psk.ap, [psk], [kt_])

                def st_A(hd):
                    qt_, qd_, kt_ = QT[hd % 2], QdT[hd % 2], KT[hd % 2]
                    at = ATb[hd % 2]
                    for b in range(4):
                        n0, wd = 128 * b, TT - 128 * b
                        ps = bank()
                        self.mm(ps.ap[:, 0:wd], kt_.ap[:, b * 128:(b + 1) * 128], qt_.ap[:, n0:TT], True, True,
                                [kt_, qt_], [ps])
                        self.tt('dve', at.ap[:, offs[b]:offs[b] + wd], ps.ap[:, 0:wd], DT.ap[:, hd, 0:wd], ALU.mult,
                                [ps, DT], [at])
                    o_ps = bank(hold=True)
                    ops_[hd] = o_ps
                    if t > 0:
                        sb_h = state_b[l].sub(hd)
                        self.mm(o_ps.ap, sb_h.ap, qd_.ap, True, False, [sb_h, qd_], [o_ps])

                def st_O(hd):
                    hs = slice(hd * 128, (hd + 1) * 128)
                    at = ATb[hd % 2]
                    o_ps = ops_[hd]
                    first = (t == 0)
                    for b in range(4):
                        n0, wd = 128 * b, TT - 128 * b
                        vt = V_tok.sub(b)
                        self.mm(o_ps.ap[:, n0:TT], vt.ap[:, hs], at.ap[:, offs[b]:offs[b] + wd], first, b == 3,
                                [vt, at], [o_ps])
                        first = False
                    self.act_fn(osq.ap, o_ps.ap, AF.Square, [o_ps], [osq])
                    if t < NT - 1:
                        st_ps = bank()
                        for b in range(4):
                            kd_, vt = Kd_tok.sub(b), V_tok.sub(b)
                            self.mm(st_ps.ap[:, 0:128], kd_.ap[:, hs], vt.ap[:, hs], b == 0, b == 3, [kd_, vt], [st_ps])
                        sf, sbh = state_f[l].sub(hd), state_b[l].sub(hd)
                        if t == 0:
                            self.copy('dve', sf.ap, st_ps.ap[:, 0:128], [st_ps], [sf])
                        else:
                            P.add('dve', lambda e, sf=sf, st_ps=st_ps, hd=hd: e.scalar_tensor_tensor(
                                out=sf.ap, in0=sf.ap, scalar=TILE_DECAY[hd], in1=st_ps.ap[:, 0:128],
                                op0=ALU.mult, op1=ALU.add), [sf, st_ps], [sf])
                        self.copy('pool', sbh.ap, sf.ap, [sf], [sbh])

                def st_N(hd):
                    o_ps = ops_.pop(hd)
                    ss = bank()
                    self.mm(ss.ap, ones_bf.ap, osq.ap, True, True, [ones_bf, osq], [ss])
                    self.act_fn(rn.ap, ss.ap, AF.Ln, [ss], [rn], scale=1.0 / 128, bias=EPS)
                    self.act_fn(rn.ap, rn.ap, AF.Exp, [rn], [rn], scale=-0.5)
                    sgi = sgate.sub(hd)
                    self.tt('pool', rgm.ap, rn.ap, sgi.ap, ALU.mult, [rn, sgi], [rgm])
                    yr = yret.sub(hd)
                    self.tt('dve', yr.ap, o_ps.ap, rgm.ap, ALU.mult, [o_ps, rgm], [yr])
                    release(o_ps)

                for fn_, hd_ in ((st_T, 0), (st_T, 1), (st_A, 0), (st_T, 2), (st_O, 0), (st_A, 1), (st_T, 3),
                                 (st_N, 0), (st_O, 1), (st_A, 2), (st_N, 1), (st_O, 2), (st_A, 3), (st_N, 2),
                                 (st_O, 3), (st_N, 3)):
                    fn_(hd_)
            if 'ret' in self.dbg:
                _sec_ret()
                if self.retcut in (5, 6, 7, 8, 51, 52):
                    P.add('pool', lambda e: e.memset(yret.ap, 0.0), [], [yret])
            else:
                P.add('pool', lambda e: e.memset(yret.ap, 0.0), [], [yret])
            def _sec_att():
                sl = next_unit('win')
                w = sl.ap.rearrange("p (kc n) -> p kc n", kc=8)
                for i in range(4):
                    ps = bank()
                    proj_fm(sl, w, i * 128, ps)
                    qi = qT_att.sub(i)
                    P.add('act', lambda e, qi=qi, ps=ps: e.mul(out=qi.ap, in_=ps.ap, mul=0.125), [ps], [qi])
                after_unit()
                sl = next_unit('win')
                w = sl.ap.rearrange("p (kc n) -> p kc n", kc=8)
                kcur = kT_att[l].sub(cur)
                for i in range(4):
                    ps = bank()
                    proj_fm(sl, w, i * 128, ps)
                    ki = kcur.sub(i)
                    self.copy('dve' if i % 2 else 'act', ki.ap, ps.ap, [ps], [ki])
                after_unit()
                sl = next_unit('win')
                w = sl.ap.rearrange("p (kc n) -> p kc n", kc=8)
                vcur = v_att[l].sub(cur)
                for tb in range(4):
                    ps = bank()
                    proj_tm(sl, w, tb, ps)
                    vi = vcur.sub(tb)
                    self.copy('dve' if tb % 2 else 'act', vi.ap, ps.ap, [ps], [vi])
                after_unit()
                blocks = [4, 5, 6, 7] + ([0, 1, 2, 3] if t > 0 else [])
                LA = 3
                units_ = [(hp, bi, b, hh) for hp in range(4) for bi, b in enumerate(blocks) for hh in range(2)]
                acc = {}
                st_ = {}

                def stage_A(ui):
                    hp, bi, b, hh = units_[ui]
                    if hp not in acc:
                        acc[hp] = (bank(hold=True), bank(hold=True))
                    qh = qT_att.sub(hp)
                    n0 = 64 * max(0, 2 * b - 8)
                    n1 = 64 * (min(7, 2 * b + 1) + 1)
                    wd = n1 - n0
                    half = cur if b >= 4 else prv
                    bb = b % 4
                    kh = kT_att[l].sub(half).sub(hp)
                    head = 2 * hp + hh
                    r0 = hh * 64
                    s_ps = bank()
                    self.mm(s_ps.ap[:, 0:wd], kh.ap[r0:r0 + 64, bb * 128:(bb + 1) * 128],
                            qh.ap[r0:r0 + 64, n0:n1], True, True, [kh, qh], [s_ps])
                    pt = PT[ui % len(PT)]
                    fb = fbias.ap[:, l * H_ATT + head: l * H_ATT + head + 1]
                    t0 = n0 + 512 - 128 * b
                    nw = max(0, min(256 - t0, wd))
                    if nw > 0:
                        tmp = stmp[ui % len(stmp)]
                        self.tt('dve', tmp.ap[:, 0:nw], s_ps.ap[:, 0:nw], gt.ap[:, head, t0:t0 + nw], ALU.add,
                                [s_ps, gt], [tmp])
                        self.act_fn(pt.ap[:, 0:nw], tmp.ap[:, 0:nw], AF.Exp, [tmp], [pt])
                    if wd > nw:
                        self.act_fn(pt.ap[:, nw:wd], s_ps.ap[:, nw:wd], AF.Exp, [s_ps, fbias], [pt], bias=fb)
                    if b < 4:
                        P.add('pool', lambda e, pt=pt, wd=wd: e.memset(pt.ap[0:64, wd - 64:wd], 0.0), [], [pt])
                    st_[ui] = (pt, n0, n1, wd, bb, half, head, r0)

                def stage_C(ui):
                    hp, bi, b, hh = units_[ui]
                    pt, n0, n1, wd, bb, half, head, r0 = st_.pop(ui)
                    o_ps, den_ps = acc[hp]
                    vh = v_att[l].sub(half).sub(bb)
                    self.mm(o_ps.ap[r0:r0 + 64, n0:n1], vh.ap[:, head * 64:(head + 1) * 64], pt.ap[:, 0:wd],
                            bi == 0, bi == len(blocks) - 1, [vh, pt], [o_ps])
                    self.mm(den_ps.ap[r0:r0 + 64, n0:n1], ones_bf.ap[:, 0:64], pt.ap[:, 0:wd],
                            bi == 0, bi == len(blocks) - 1, [ones_bf, pt], [den_ps])
                    if bi == len(blocks) - 1 and hh == 1:
                        self.act_fn(rec.ap, den_ps.ap, AF.Ln, [den_ps], [rec])
                        self.act_fn(rec.ap, rec.ap, AF.Exp, [rec], [rec], scale=-1.0)
                        ya = yatt.sub(hp)
                        self.tt('dve', ya.ap, o_ps.ap, rec.ap, ALU.mult, [o_ps, rec], [ya])
                        release(o_ps)
                        release(den_ps)

                nu = len(units_)
                for i in range(nu + LA):
                    if i < nu:
                        stage_A(i)
                    if i - LA >= 0:
                        stage_C(i - LA)
            if 'att' in self.dbg:
                _sec_att()
            else:
                P.add('pool', lambda e: e.memset(yatt.ap, 0.0), [], [yatt])
            def _sec_merge():
                ys = (yconv, yret, yatt)
                for cp in range(4):
                    for i in range(3):
                        sl = next_unit('mg')
                        wg = sl.ap[:, 0:2048].rearrange("p (kc n) -> p kc n", kc=8)
                        wb = sl.ap[:, 2048:3072].rearrange("p (kk n) -> p kk n", kk=4)
                        for cc in range(2):
                            pg, pb = bank(), bank()
                            proj_fm(sl, wg, cc * 128, pg)
                            proj_fm(sl, wb, cc * 128, pb, kchunks=4, rhs_src=ys[i])
                            sg_, pr_, ma_ = sig[cc], prd[cc], macc[cc]
                            self.act_fn(sg_.ap, pg.ap, AF.Sigmoid, [pg], [sg_])
                            if i == 0:
                                self.tt('dve', ma_.ap, pb.ap, sg_.ap, ALU.mult, [pb, sg_], [ma_])
                            else:
                                self.tt('dve', pr_.ap, pb.ap, sg_.ap, ALU.mult, [pb, sg_], [pr_])
                                if i == 1:
                                    self.tt('pool', ma_.ap, ma_.ap, pr_.ap, ALU.add, [ma_, pr_], [ma_])
                                else:
                                    mc = mT.sub(cp * 2 + cc)
                                    self.tt('pool', mc.ap, ma_.ap, pr_.ap, ALU.add, [ma_, pr_], [mc])
                        after_unit()
                for cp in range(4):
                    sl = next_unit('wo')
                    w = sl.ap[:, 0:2048].rearrange("p (kc n) -> p kc n", kc=8)
                    for cc in range(2):
                        ps = bank()
                        proj_fm(sl, w, cc * 128, ps, rhs_src=mT)
                        xc = xT.sub(cp * 2 + cc)
                        self.tt('dve', xc.ap, ps.ap, xc.ap, ALU.add, [ps, xc], [xc])
                        norm_chunk(cp * 2 + cc)
                    after_unit()
            if 'merge' in self.dbg:
                _sec_merge()

        c_xin = self.chan("c_xin")
        c_out = self.chan("c_out")
        c_gt = self.chan("c_gt")
        c_rope = [self.chan("c_cos"), self.chan("c_sin")]

        def load_x(t):
            self.dma(xs, x, c_xin, in_ap=x.ap[t * TT:(t + 1) * TT].rearrange("(tb p) d -> p tb d", p=128))

        load_x(0)
        for t in range(NT):
            if has_mix:
                self.dma(cosT, cos_d, c_rope[0], in_ap=cos_d.ap[t * TT:(t + 1) * TT].rearrange("(tb p) f -> p tb f", p=128))
                self.dma(sinT, sin_d, c_rope[1], in_ap=sin_d.ap[t * TT:(t + 1) * TT].rearrange("(tb p) f -> p tb f", p=128))
            for c in range(NCH):
                ps = bank()
                for tb in range(4):
                    P.add('pe', lambda e, ps=ps, tb=tb, c=c: e.transpose(
                        out=ps.ap[:, tb * 128:(tb + 1) * 128], in_=xs.ap[:, tb, c * 128:(c + 1) * 128],
                        identity=ident.ap), [xs, ident], [ps])
                xc = xT.sub(c)
                self.copy('dve' if c % 2 else 'act', xc.ap, ps.ap, [ps], [xc])
                norm_chunk(c)
            for l in range(NL):
                for si, st in enumerate(self.stages):
                    if l == NL - 1 and si == len(self.stages) - 1 and t + 1 < NT:
                        load_x(t + 1)
                    if st == 'ffn1':
                        ffn(l, 1)
                    elif st == 'ffn2':
                        ffn(l, 2)
                    elif st == 'mix':
                        mixer(l, t)
            rmsnorm(3 * NL, out_bf16=False)
            for tb in range(4):
                for half in range(2):
                    ps = bank()
                    for cc in range(4):
                        oc = outT.sub(half * 4 + cc)
                        P.add('pe', lambda e, ps=ps, cc=cc, oc=oc, tb=tb: e.transpose(
                            out=ps.ap[:, cc * 128:(cc + 1) * 128], in_=oc.ap[:, tb * 128:(tb + 1) * 128],
                            identity=ident.ap), [oc, ident], [ps])
                    dst = xo.ap[:, tb, half * 512:(half + 1) * 512]
                    self.copy('dve' if half else 'act', dst, ps.ap, [ps], [xo])
            self.dma(outv, xo, c_out,
                     out_ap=out.ap()[t * TT:(t + 1) * TT].rearrange("(tb p) d -> p tb d", p=128))
        P.add('sp', None, [outv], [])

    def finish(self):
        nc, P = self.nc, self.P
        sems = {e: self.es.enter_context(nc.semaphore("sem_" + e)) for e in Prog.ENGS}
        chan_sems = {c: self.es.enter_context(nc.semaphore(c)) for c in self.chans}
        block = self.es.enter_context(nc.Block())
        P.emit(block, sems, chan_sems)
        self.es.close()
        return nc


def build_program(S, n_layers=L, stages=('ffn1', 'mix', 'ffn2')):
    b = Builder(S, n_layers, stages)
    b.build()
    return b.finish()


def host_inputs(inp, S, n_layers=L):
    nw = []
    for l in range(n_layers):
        nw += [inp["ffn1_norm"][l], inp["mix_norm"][l], inp["ffn2_norm"][l]]
    nw.append(inp["final_norm"])
    normw = _chunked_vec(np.stack(nw)).reshape(128, -1)
    cw = np.asarray(inp["conv_w"], np.float32)[:n_layers]
    convw = cw.reshape(n_layers, 3, 4, 128).transpose(3, 0, 2, 1).reshape(128, -1)
    tabs, _ = _const_tables(S)
    G, fb = _gather_bias(np.asarray(inp["rel_bias"], np.float32)[:n_layers])
    shared = {
        "normw": np.ascontiguousarray(normw, np.float32),
        "ident": np.eye(128, dtype=np.float32),
        "convw": np.ascontiguousarray(convw, np.float32),
        "gtab": G, "fbias": fb,
    }
    shared.update(tabs)
    for k in ("ffn1_w_gate", "ffn1_w_up", "ffn1_w_down", "ffn2_w_gate", "ffn2_w_up", "ffn2_w_down",
              "w_in", "w_branch", "w_merge_gate", "w_out"):
        shared[k] = np.ascontiguousarray(np.asarray(inp[k], np.float32)[:n_layers])
    return shared


_NC_CACHE = {}


def kernel(**inputs):
    x = np.asarray(inputs["x"], np.float32)
    B, S, _ = x.shape
    key = (S,)
    if key not in _NC_CACHE:
        _NC_CACHE[key] = build_program(S)
    nc = _NC_CACHE[key]
    shared = host_inputs(inputs, S)
    in_maps = []
    for b in range(B):
        m = dict(shared)
        m["x"] = np.ascontiguousarray(x[b])
        in_maps.append(m)
    res = run_bass_kernel_spmd(nc, in_maps, core_ids=list(range(B)))
    return np.stack([np.asarray(r["out"], np.float32) for r in res.results], axis=0)
```

```python
import bisect
from contextlib import ExitStack

import numpy as np
import concourse.bass as bass
import concourse.mybir as mybir
from concourse.bass_utils import run_bass_kernel_spmd

F32 = mybir.dt.float32
BF16 = mybir.dt.bfloat16
AF = mybir.ActivationFunctionType
ALU = mybir.AluOpType

D = 1024
DFF = 2816
BW = 512
L = 2
TT = 512
NCH = D // 128
NJ = DFF // 128
EPS = 1e-6
H_RET = 4
H_ATT = 8
NEG = -30000.0


class Arena:
    def __init__(self, size):
        self.los = [0]
        self.segs = [[0, size, None, {}]]

    def _split(self, pos):
        i = bisect.bisect_right(self.los, pos) - 1
        s = self.segs[i]
        if s[0] < pos < s[1]:
            ns = [pos, s[1], s[2], dict(s[3])]
            s[1] = pos
            self.segs.insert(i + 1, ns)
            self.los.insert(i + 1, pos)

    def access(self, lo, hi, op, write):
        self._split(lo)
        self._split(hi)
        i0 = bisect.bisect_left(self.los, lo)
        i1 = bisect.bisect_left(self.los, hi)
        deps = []
        for i in range(i0, i1):
            s = self.segs[i]
            if s[2] is not None:
                deps.append((s[2], 'w'))
            if write:
                for r in s[3].values():
                    deps.append((r, 'r'))
        if write:
            self.segs[i0:i1] = [[lo, hi, op, {}]]
            self.los[i0:i1] = [lo]
        else:
            key = op.eng if op.dma is None else ('dma', op.uid)
            for i in range(i0, i1):
                self.segs[i][3][key] = op
        return deps


class View:
    def __init__(self, ap, arena, lo, hi):
        self.ap, self.arena, self.lo, self.hi = ap, arena, lo, hi

    def sub(self, i):
        n = self.ap.shape[1]
        step = (self.hi - self.lo) // n
        return View(self.ap[:, i], self.arena, self.lo + i * step, self.lo + (i + 1) * step)

    def cols(self, a, b):
        n = self.ap.shape[-1]
        assert len(self.ap.shape) == 2
        step = (self.hi - self.lo) // n
        return View(self.ap[:, a:b], self.arena, self.lo + a * step, self.lo + b * step)

    def with_ap(self, ap):
        return View(ap, self.arena, self.lo, self.hi)


class Op:
    __slots__ = ('eng', 'fn', 'deps', 'signal', 'sigidx', 'dma', 'uid')


class Prog:
    ENGS = ('pe', 'act', 'dve', 'pool', 'sp')

    def __init__(self, nc):
        self.nc = nc
        self.arenas = {}
        self.ops = {e: [] for e in self.ENGS}
        self.chan_count = {}
        self.uid = 0

    def arena(self, name, size):
        self.arenas[name] = Arena(size)
        return name

    def add(self, eng, fn, reads=(), writes=(), chan=None):
        op = Op()
        op.eng, op.fn, op.signal, op.sigidx = eng, fn, False, None
        op.uid = self.uid
        self.uid += 1
        if chan is not None:
            self.chan_count[chan] = self.chan_count.get(chan, 0) + 16
            op.dma = (chan, self.chan_count[chan])
        else:
            op.dma = None
        raw = []
        for v in reads:
            raw += self.arenas[v.arena].access(v.lo, v.hi, op, v.arena == 'psum')
        for v in writes:
            raw += self.arenas[v.arena].access(v.lo, v.hi, op, True)
        deps = {}
        for d, kind in raw:
            if d is op:
                continue
            if d.dma is None and op.dma is None and d.eng == eng:
                if eng == 'pe' or kind == 'r':
                    continue
            deps[d.uid] = d
            if d.dma is None:
                d.signal = True
        op.deps = list(deps.values())
        self.ops[eng].append(op)
        return op

    def emit(self, block, sems, chan_sems):
        nc = self.nc
        for e in self.ENGS:
            k = 0
            for op in self.ops[e]:
                if op.dma is None and op.signal:
                    k += 1
                    op.sigidx = k
        engobj = {'pe': 'tensor', 'act': 'scalar', 'dve': 'vector', 'pool': 'gpsimd', 'sp': 'sync'}

        def run(e):
            def body(eng):
                seen = {}
                for op in self.ops[e]:
                    for d in op.deps:
                        if d.dma is not None:
                            key, val = ('c', d.dma[0]), d.dma[1]
                            sem = chan_sems[d.dma[0]]
                        else:
                            key, val = ('e', d.eng), d.sigidx
                            sem = sems[d.eng]
                        if seen.get(key, 0) >= val:
                            continue
                        seen[key] = val
                        eng.wait_ge(sem, val)
                    if op.fn is None:
                        continue
                    inst = op.fn(eng)
                    if op.dma is not None:
                        inst.then_inc(chan_sems[op.dma[0]], 16)
                    elif op.signal:
                        inst.then_inc(sems[e], 1)
            return body

        for e in self.ENGS:
            if self.ops[e]:
                getattr(block, engobj[e])(run(e))


def _chunked_vec(v):
    v = np.asarray(v, np.float32)
    lead = v.shape[:-1]
    return np.ascontiguousarray(np.moveaxis(v.reshape(lead + (v.shape[-1] // 128, 128)), -1, 0))


def _const_tables(S):
    inv_freq = (np.float32(10000.0) ** (-np.linspace(0.0, 1.0, 64, dtype=np.float32))).astype(np.float32)
    ang = (np.arange(S, dtype=np.float32)[:, None] * inv_freq[None, :]).astype(np.float32)
    cos = np.cos(ang.astype(np.float64)).astype(np.float32)
    sin = np.sin(ang.astype(np.float64)).astype(np.float32)
    lg = np.log1p(-np.exp2(-5.0 - np.arange(H_RET, dtype=np.float64)))
    p = np.arange(128)[:, None]
    j = np.arange(TT)[None, :]
    scale = 128.0 ** -0.5
    DT = np.zeros((128, H_RET, TT), np.float64)
    for hd in range(H_RET):
        m = np.exp(lg[hd] * np.abs(j - p)) * ((p // 64) <= (j // 64))
        DT[:, hd, :] = m * scale
    qdec = np.exp(lg[:, None] * (np.arange(TT)[None, :] + 1.0)) * scale
    qdecT = np.broadcast_to(qdec[None], (128, H_RET, TT))
    pos = np.arange(4)[None, :, None] * 128 + np.arange(128)[:, None, None]
    kdec = np.exp(lg[None, None, :] * (TT - 1.0 - pos))
    tile_decay = [float(np.exp(lg[hd] * TT)) for hd in range(H_RET)]
    return dict(rope_cos=cos, rope_sin=sin,
                DT=np.ascontiguousarray(DT.reshape(128, -1), np.float32),
                qdecT=np.ascontiguousarray(qdecT.reshape(128, -1), np.float32),
                kdec=np.ascontiguousarray(kdec.reshape(128, -1), np.float32)), tile_decay


def _gather_bias(rel_bias):
    nl = rel_bias.shape[0]
    mp = np.arange(128)[:, None]
    t = np.arange(256)[None, :]
    idx = np.clip(t - mp, -128, 128) + 128
    valid = ((t // 64) - (mp // 64)) >= 0
    G = np.empty((nl, 128, H_ATT, 256), np.float32)
    for l in range(nl):
        for hd in range(H_ATT):
            G[l, :, hd, :] = np.where(valid, rel_bias[l, hd][idx], np.float32(NEG))
    fb = np.broadcast_to(rel_bias[:, :, 256].reshape(1, nl * H_ATT), (128, nl * H_ATT))
    return np.ascontiguousarray(G.reshape(nl, 128, -1)), np.ascontiguousarray(fb, np.float32)


_LG = np.log1p(-np.exp2(-5.0 - np.arange(H_RET, dtype=np.float64)))
TILE_DECAY = [float(np.exp(_LG[hd] * TT)) for hd in range(H_RET)]


class Builder:
    def __init__(self, S, n_layers=L, stages=('ffn1', 'mix', 'ffn2')):
        self.S = S
        self.NT = S // TT
        self.n_layers = n_layers
        self.stages = stages
        self.nc = bass.Bass("TRN2", target_bir_lowering=False)
        self.P = Prog(self.nc)
        self.es = ExitStack()
        self.chans = []
        import os
        self.dbg = set(os.environ.get('MIXDBG', 'conv,ret,att,merge').split(','))
        self.retcut = int(os.environ.get('RETCUT', '99'))

    def din(self, name, shape, dtype=F32):
        t = self.nc.dram_tensor(name, list(shape), dtype, kind="ExternalInput")
        self.P.arena('d_' + name, 1)
        return View(t.ap(), 'd_' + name, 0, 1)

    def dscratch(self, name, shape, dtype):
        t = self.nc.dram_tensor(name, list(shape), dtype, kind="Internal")
        self.P.arena('d_' + name, 1)
        return View(t.ap(), 'd_' + name, 0, 1)

    @staticmethod
    def _shape_ap(ap, free_shape):
        if len(free_shape) == 2:
            ap = ap.rearrange("p (a b) -> p a b", a=free_shape[0])
        elif len(free_shape) == 3:
            ap = ap.rearrange("p (a b c) -> p a b c", a=free_shape[0], b=free_shape[1])
        return ap

    def sb(self, name, free_shape, dtype):
        n = int(np.prod(free_shape))
        t = self.es.enter_context(self.nc.sbuf_tensor(name, [128, n], dtype))
        esz = 2 if dtype == BF16 else 4
        self.P.arena(name, n * esz)
        return View(self._shape_ap(t[:, :], free_shape), name, 0, n * esz)

    def scr(self, off, free_shape, dtype):
        n = int(np.prod(free_shape))
        esz = 2 if dtype == BF16 else 4
        assert off % 4 == 0 and (n * esz) % 4 == 0 and off + n * esz <= self.scr_bytes, (off, n, esz)
        ap = self.scr_t[:, off // 4:(off + n * esz) // 4]
        if dtype == BF16:
            ap = ap.bitcast(BF16)
        return View(self._shape_ap(ap, free_shape), 'scr', off, off + n * esz)

    def chan(self, name):
        self.chans.append(name)
        return name

    def mm(self, out, lhsT, rhs, start, stop, reads, writes):
        self.P.add('pe', lambda e: e.matmul(out, lhsT, rhs, start=start, stop=stop), reads, writes)

    def dma(self, out_v, in_v, chan, out_ap=None, in_ap=None, q='sp'):
        oa = out_v.ap if out_ap is None else out_ap
        ia = in_v.ap if in_ap is None else in_ap
        return self.P.add(q, lambda e: e.dma_start(out=oa, in_=ia), [in_v], [out_v], chan=chan)

    def act_fn(self, out_ap, in_ap, func, reads, writes, **kw):
        self.P.add('act', lambda e: e.activation(out=out_ap, in_=in_ap, func=func, **kw), reads, writes)

    def tt(self, eng, out_ap, in0, in1, op, reads, writes):
        self.P.add(eng, lambda e: e.tensor_tensor(out=out_ap, in0=in0, in1=in1, op=op), reads, writes)

    def copy(self, eng, out_ap, in_ap, reads, writes):
        if eng == 'act':
            self.P.add('act', lambda e: e.copy(out=out_ap, in_=in_ap), reads, writes)
        else:
            self.P.add(eng, lambda e: e.tensor_copy(out=out_ap, in_=in_ap), reads, writes)

    def build(self):
        nc, P = self.nc, self.P
        S, NT, NL = self.S, self.NT, self.n_layers
        K1 = 1024
        self.scr_bytes = 64 * K1
        has_mix = 'mix' in self.stages
        x = self.din("x", [S, D])
        out = self.nc.dram_tensor("out", [S, D], F32, kind="ExternalOutput")
        P.arena('d_out', 1)
        outv = View(out.ap(), 'd_out', 0, 1)
        wf = {}
        for f in (1, 2):
            wf[f] = dict(g=self.din(f"ffn{f}_w_gate", [NL, D, DFF]),
                         u=self.din(f"ffn{f}_w_up", [NL, D, DFF]),
                         d=self.din(f"ffn{f}_w_down", [NL, DFF, D]))
        w_in_d = self.din("w_in", [NL, D, 10 * BW])
        w_br_d = self.din("w_branch", [NL, 3, BW, D])
        w_mg_d = self.din("w_merge_gate", [NL, 3, D, D])
        w_out_d = self.din("w_out", [NL, D, D])
        normw_d = self.din("normw", [128, (3 * NL + 1) * NCH])
        ident_d = self.din("ident", [128, 128])
        convw_d = self.din("convw", [128, NL * 4 * 3])
        cos_d = self.din("rope_cos", [S, 64])
        sin_d = self.din("rope_sin", [S, 64])
        DT_d = self.din("DT", [128, H_RET * TT])
        qdecT_d = self.din("qdecT", [128, H_RET * TT])
        kdec_d = self.din("kdec", [128, 4 * H_RET])
        gtab_d = self.din("gtab", [NL, 128, H_ATT * 256])
        fbias_d = self.din("fbias", [128, NL * H_ATT])

        units = {}
        for l in range(NL):
            for f in (1, 2):
                for u in range(11):
                    units[('gu', l, f, u)] = self.dscratch(f"s_gu_{l}_{f}_{u}", [128, 4096], BF16)
                for cp in range(4):
                    for jh in range(2):
                        units[('dn', l, f, cp, jh)] = self.dscratch(f"s_dn_{l}_{f}_{cp}_{jh}", [128, 2816], BF16)
            for i in range(4):
                units[('cv', l, i)] = self.dscratch(f"s_cv_{l}_{i}", [128, 3072], BF16)
            for blk in range(3, 10):
                units[('win', l, blk)] = self.dscratch(f"s_win_{l}_{blk}", [128, 4096], BF16)
            for cp in range(4):
                for i in range(3):
                    units[('mg', l, cp, i)] = self.dscratch(f"s_mg_{l}_{cp}_{i}", [128, 3072], BF16)
                units[('wo', l, cp)] = self.dscratch(f"s_wo_{l}_{cp}", [128, 2048], BF16)

        xT = self.sb("xT", [NCH, TT], F32)
        h = self.sb("h", [NCH, TT], BF16)
        stdb = self.sb("std", [TT], F32)
        rstd = self.sb("rstd", [TT], F32)
        normw = self.sb("normw_sb", [(3 * NL + 1) * NCH], F32)
        ident = self.sb("ident_sb", [128], F32)
        ident_bf = self.sb("ident_bf", [128], BF16)
        ones_bf = self.sb("ones_bf", [128], BF16)
        NSLOT = 4
        slots = [self.sb(f"wslot{i}", [4096], BF16) for i in range(NSLOT)]
        slot_ch = [self.chan(f"c_slot{i}") for i in range(NSLOT)]
        if has_mix:
            convw = self.sb("convw_sb", [NL * 4 * 3], F32)
            cosT = self.sb("cosT", [4, 64], F32)
            sinT = self.sb("sinT", [4, 64], F32)
            DT = self.sb("DT_sb", [H_RET, TT], F32)
            qdecT = self.sb("qdecT_sb", [H_RET, TT], F32)
            kdec = self.sb("kdec_sb", [4 * H_RET], F32)
            fbias = self.sb("fbias_sb", [NL * H_ATT], F32)
            carry = [self.sb(f"carry{l}", [4, 2], F32) for l in range(NL)]
            state_f = [self.sb(f"state_f{l}", [H_RET, 128], F32) for l in range(NL)]
            state_b = [self.sb(f"state_b{l}", [H_RET, 128], BF16) for l in range(NL)]
            kT_att = [self.sb(f"kT_att{l}", [2, 4, TT], BF16) for l in range(NL)]
            v_att = [self.sb(f"v_att{l}", [2, 4, TT], BF16) for l in range(NL)]
            gt = self.sb("gt", [H_ATT, 256], F32)
        self.scr_t = self.es.enter_context(nc.sbuf_tensor("scr", [128, self.scr_bytes // 4], F32))
        P.arena('scr', self.scr_bytes)
        NSTG = 4
        stg32 = [self.sb(f"stg32_{i}", [1024], F32) for i in range(NSTG)]
        stg32_ch = [self.chan(f"c_stg32_{i}") for i in range(NSTG)]
        slotst_ch = [self.chan(f"c_slotst{i}") for i in range(NSLOT)]
        act = self.scr(0, [NJ, TT], BF16)
        sgb = [self.scr(22 * K1 + i * 2 * K1, [TT], F32) for i in range(2)]
        sqr = [self.scr(26 * K1 + i * K1, [TT], BF16) for i in range(4)]
        outT = self.scr(0, [NCH, TT], F32)
        xo = self.scr(16 * K1, [4, D], F32)
        xs = self.scr(32 * K1, [4, D], F32)
        if has_mix:
            yconv = self.scr(0, [4, TT], BF16)
            yret = self.scr(4 * K1, [4, TT], BF16)
            yatt = self.scr(8 * K1, [4, TT], BF16)
            sgate = self.scr(12 * K1, [4, TT], F32)
            cu_sb = self.scr(20 * K1, [TT], F32)
            zb = self.scr(22 * K1, [TT + 2], F32)
            cacc = self.scr(25 * K1, [TT], F32)
            q_tok = self.scr(27 * K1, [4, BW], F32)
            k_tok = self.scr(35 * K1, [4, BW], F32)
            Kd_tok = self.scr(43 * K1, [4, BW], BF16)
            V_tok = self.scr(47 * K1, [4, BW], BF16)
            rt = [self.scr(51 * K1 + i * K1, [4, 64], F32) for i in range(2)]
            QT = [self.scr(53 * K1 + i * K1, [TT], BF16) for i in range(2)]
            QdT = [self.scr(55 * K1 + i * K1, [TT], BF16) for i in range(2)]
            KT = [self.scr(57 * K1 + i * K1, [TT], BF16) for i in range(2)]
            ATb = [self.scr(59 * K1 + i * 2560, [1280], BF16) for i in range(2)]
            rn = self.scr(20 * K1, [TT], F32)
            osq = self.scr(22 * K1, [TT], BF16)
            rgm = self.scr(23 * K1, [TT], F32)
            qT_att = self.scr(20 * K1, [4, TT], BF16)
            stmp = [self.scr(24 * K1 + i * 2 * K1, [TT], F32) for i in range(4)]
            PT = [self.scr(54 * K1 + i * K1, [TT], BF16) for i in range(6)]
            rec = self.scr(32 * K1, [TT], F32)
            sig = [self.scr(34 * K1 + i * 2 * K1, [TT], F32) for i in range(2)]
            prd = [self.scr(38 * K1 + i * 2 * K1, [TT], F32) for i in range(2)]
            macc = [self.scr(42 * K1 + i * 2 * K1, [TT], F32) for i in range(2)]
            mT = self.scr(46 * K1, [NCH, TT], BF16)
        pst = self.es.enter_context(nc.psum_tensor("psum", [128, 8 * 512], F32))
        P.arena('psum', 8 * 2048)
        banks = [View(pst[:, b * 512:(b + 1) * 512], 'psum', b * 2048, (b + 1) * 2048) for b in range(8)]
        self.bank_i = 0

        self.held = set()

        def bank(hold=False):
            while (self.bank_i % 8) in self.held:
                self.bank_i += 1
            k = self.bank_i % 8
            self.bank_i += 1
            if hold:
                self.held.add(k)
            return banks[k]

        def release(bv):
            self.held.discard(bv.lo // 2048)

        def const_load(dst, src):
            self.dma(dst, src, self.chan(f"c_const{len(self.chans)}"))

        const_load(normw, normw_d)
        const_load(ident, ident_d)
        P.add('pool', lambda e: e.memset(ones_bf.ap, 1.0), [], [ones_bf])
        self.copy('dve', ident_bf.ap, ident.ap, [ident], [ident_bf])
        if has_mix:
            const_load(convw, convw_d)
            const_load(DT, DT_d.with_ap(DT_d.ap.rearrange("p (a b) -> p a b", a=H_RET)))
            const_load(qdecT, qdecT_d.with_ap(qdecT_d.ap.rearrange("p (a b) -> p a b", a=H_RET)))
            const_load(kdec, kdec_d)
            const_load(fbias, fbias_d)

        unit_src = {}
        ukey = {id(v): k for k, v in units.items()}

        def prepass_unit(unit_v, pieces):
            unit_src[ukey[id(unit_v)]] = pieces

        for l in range(NL):
            for f in (1, 2):
                if f"ffn{f}" not in self.stages:
                    continue
                g, u_, d_ = wf[f]['g'], wf[f]['u'], wf[f]['d']
                gl = g.ap[l].rearrange("(kc p) n -> p kc n", p=128)
                ul = u_.ap[l].rearrange("(kc p) n -> p kc n", p=128)
                dl = d_.ap[l].rearrange("(j p) n -> p j n", p=128)
                for u in range(11):
                    prepass_unit(units[('gu', l, f, u)],
                                 [(g, gl[:, :, u * 256:(u + 1) * 256], 0, 8, 256),
                                  (u_, ul[:, :, u * 256:(u + 1) * 256], 2048, 8, 256)])
                for cp in range(4):
                    for jh in range(2):
                        prepass_unit(units[('dn', l, f, cp, jh)],
                                     [(d_, dl[:, jh * 11:(jh + 1) * 11, cp * 256:(cp + 1) * 256], 0, 11, 256)])
            if has_mix:
                wl = w_in_d.ap[l].rearrange("(kc p) n -> p kc n", p=128)
                for i in range(4):
                    prepass_unit(units[('cv', l, i)],
                                 [(w_in_d, wl[:, :, g_ * BW + i * 128: g_ * BW + (i + 1) * 128], k_ * 1024, 8, 128)
                                  for k_, g_ in enumerate((0, 1, 2))])
                for blk in range(3, 10):
                    prepass_unit(units[('win', l, blk)], [(w_in_d, wl[:, :, blk * BW:(blk + 1) * BW], 0, 8, BW)])
                for cp in range(4):
                    for i in range(3):
                        mgl = w_mg_d.ap[l, i].rearrange("(kc p) n -> p kc n", p=128)
                        brl = w_br_d.ap[l, i].rearrange("(kk p) n -> p kk n", p=128)
                        prepass_unit(units[('mg', l, cp, i)],
                                     [(w_mg_d, mgl[:, :, cp * 256:(cp + 1) * 256], 0, 8, 256),
                                      (w_br_d, brl[:, :, cp * 256:(cp + 1) * 256], 2048, 4, 256)])
                    wol = w_out_d.ap[l].rearrange("(kc p) n -> p kc n", p=128)
                    prepass_unit(units[('wo', l, cp)], [(w_out_d, wol[:, :, cp * 256:(cp + 1) * 256], 0, 8, 256)])

        self.slot_i = 0

        self.stg_i = 0
        cast_engs = ['act', 'dve', 'pool']

        def load_unit(key, n, first):
            i = self.slot_i % NSLOT
            self.slot_i += 1
            sl = slots[i]
            uv = units[key]
            if not first:
                self.dma(sl, uv, slot_ch[i], out_ap=sl.ap[:, 0:n], in_ap=uv.ap[:, 0:n])
                return sl
            for (sv, sap, off, a, b) in unit_src[key]:
                step = max(1, 1024 // b)
                for a0 in range(0, a, step):
                    a1 = min(a, a0 + step)
                    cnt = (a1 - a0) * b
                    k = self.stg_i % NSTG
                    self.stg_i += 1
                    st = stg32[k]
                    self.dma(st, sv, stg32_ch[k], out_ap=st.ap[:, 0:cnt].rearrange("p (a b) -> p a b", a=a1 - a0),
                             in_ap=sap[:, a0:a1, :])
                    dst = sl.cols(off + a0 * b, off + a0 * b + cnt)
                    self.copy(cast_engs[self.stg_i % 3], dst.ap, st.ap[:, 0:cnt], [st], [dst])
            if NT > 1:
                self.dma(uv, sl, slotst_ch[i], out_ap=uv.ap[:, 0:n], in_ap=sl.ap[:, 0:n])
            return sl

        def unit_seq():
            seq = []
            for l in range(NL):
                for st in self.stages:
                    if st in ('ffn1', 'ffn2'):
                        f = 1 if st == 'ffn1' else 2
                        for u in range(11):
                            seq.append((('gu', l, f, u), 4096))
                        for cp in range(4):
                            for jh in range(2):
                                seq.append((('dn', l, f, cp, jh), 2816))
                    elif st == 'mix':
                        if 'conv' in self.dbg:
                            for i in range(4):
                                seq.append((('cv', l, i), 3072))
                        if 'ret' in self.dbg:
                            for blk in range(3, 7):
                                seq.append((('win', l, blk), 4096))
                        if 'att' in self.dbg:
                            for blk in range(7, 10):
                                seq.append((('win', l, blk), 4096))
                        if 'merge' in self.dbg:
                            for cp in range(4):
                                for i in range(3):
                                    seq.append((('mg', l, cp, i), 3072))
                            for cp in range(4):
                                seq.append((('wo', l, cp), 2048))
            return seq

        useq = unit_seq() * NT
        self.u_next = 0
        self.u_cons = 0
        self.loaded = {}
        PREFETCH = NSLOT - 1

        def prefetch_to(k):
            while self.u_next < min(len(useq), k + 1):
                key, n = useq[self.u_next]
                self.loaded[self.u_next] = load_unit(key, n, self.u_next < len(useq) // NT)
                self.u_next += 1

        def next_unit(expect_kind):
            k = self.u_cons
            assert useq[k][0][0] == expect_kind, (useq[k], expect_kind)
            prefetch_to(k)
            sl = self.loaded.pop(k)
            self.u_cons += 1
            return sl

        def after_unit():
            prefetch_to(self.u_cons + PREFETCH - 1)

        self.nps = None
        self.nk = 0

        def norm_chunk(c):
            if self.nk == 0:
                self.nps = bank(hold=True)
            norm_flush(keep=len(sqr) - 1)
            sqv = sqr[self.nk % len(sqr)]
            xc = xT.sub(c)
            self.act_fn(sqv.ap, xc.ap, AF.Square, [xc], [sqv])
            self.npend.append((sqv, self.nk == 0, self.nk == NCH - 1))
            self.nk += 1

        self.npend = []

        def norm_flush(keep=0):
            while len(self.npend) > keep:
                sqv, st, sp = self.npend.pop(0)
                self.mm(self.nps.ap, ones_bf.ap, sqv.ap, st, sp, [ones_bf, sqv], [self.nps])

        def rmsnorm(widx, out_bf16=True):
            assert self.nk == NCH
            norm_flush()
            ps = self.nps
            self.nk = 0
            self.act_fn(stdb.ap, ps.ap, AF.Ln, [ps], [stdb], scale=1.0 / D, bias=EPS)
            release(ps)
            self.act_fn(rstd.ap, stdb.ap, AF.Exp, [stdb], [rstd], scale=-0.5)
            for c in range(NCH):
                xc = xT.sub(c)
                dst = h.sub(c) if out_bf16 else outT.sub(c)
                wap = normw.ap[:, widx * NCH + c: widx * NCH + c + 1]
                P.add('dve', lambda e, xc=xc, dst=dst, wap=wap: e.scalar_tensor_tensor(
                    out=dst.ap, in0=xc.ap, scalar=wap, in1=rstd.ap, op0=ALU.mult, op1=ALU.mult),
                    [xc, rstd, normw], [dst])

        def proj_fm(sl, wv, ncol0, ps, kchunks=NCH, rhs_src=None):
            src = h if rhs_src is None else rhs_src
            for kc in range(kchunks):
                sk = src.sub(kc)
                self.mm(ps.ap, wv[:, kc, ncol0:ncol0 + 128], sk.ap, kc == 0, kc == kchunks - 1, [sl, sk], [ps])

        def proj_tm(sl, wv, tb, ps):
            for kc in range(NCH):
                hk = h.sub(kc)
                self.mm(ps.ap, hk.ap[:, tb * 128:(tb + 1) * 128], wv[:, kc, :], kc == 0, kc == NCH - 1,
                        [sl, hk], [ps])

        def ffn(l, f):
            rmsnorm(3 * l + (0 if f == 1 else 2))
            for u in range(11):
                sl = next_unit('gu')
                w = sl.ap.rearrange("p (g kc n) -> p g kc n", g=2, kc=8)
                for jj in range(2):
                    j = 2 * u + jj
                    pg, pu = bank(), bank()
                    proj_fm(sl, w[:, 0], jj * 128, pg)
                    proj_fm(sl, w[:, 1], jj * 128, pu)
                    sg = sgb[j % 2]
                    aj = act.sub(j)
                    self.act_fn(sg.ap, pg.ap, AF.Silu, [pg], [sg])
                    self.tt('dve', aj.ap, pu.ap, sg.ap, ALU.mult, [pu, sg], [aj])
                after_unit()
            for cp in range(4):
                pcs = [bank(), bank()]
                for jh in range(2):
                    if jh == 1:
                        norm_flush()
                    sl = next_unit('dn')
                    w = sl.ap[:, 0:2816].rearrange("p (j n) -> p j n", j=11)
                    for jj in range(11):
                        j = jh * 11 + jj
                        aj = act.sub(j)
                        for cc in range(2):
                            self.mm(pcs[cc].ap, w[:, jj, cc * 128:(cc + 1) * 128], aj.ap, j == 0, j == NJ - 1,
                                    [sl, aj], [pcs[cc]])
                    after_unit()
                for cc in range(2):
                    xc = xT.sub(cp * 2 + cc)
                    pc = pcs[cc]
                    P.add('dve', lambda e, xc=xc, pc=pc: e.scalar_tensor_tensor(
                        out=xc.ap, in0=pc.ap, scalar=0.5, in1=xc.ap, op0=ALU.mult, op1=ALU.add),
                        [pc, xc], [xc])
                    norm_chunk(cp * 2 + cc)

        def mixer(l, t):
            rmsnorm(3 * l + 1)
            self.dma(gt, gtab_d, c_gt, in_ap=gtab_d.ap[l].rearrange("p (a b) -> p a b", a=H_ATT))
            cur, prv = t % 2, (t + 1) % 2
            if 'conv' not in self.dbg:
                P.add('pool', lambda e: e.memset(yconv.ap, 0.0), [], [yconv])
            for i in (range(4) if 'conv' in self.dbg else []):
                sl = next_unit('cv')
                w = sl.ap[:, 0:3072].rearrange("p (g kc n) -> p g kc n", g=3, kc=8)
                pu, pb, pc = bank(), bank(), bank()
                proj_fm(sl, w[:, 0], 0, pu)
                proj_fm(sl, w[:, 2], 0, pc)
                proj_fm(sl, w[:, 1], 0, pb)
                after_unit()
                self.copy('act', cu_sb.ap, pu.ap, [pu], [cu_sb])
                ci = carry[l].sub(i)
                if t == 0:
                    P.add('pool', lambda e: e.memset(zb.ap[:, 0:2], 0.0), [], [zb])
                else:
                    self.copy('pool', zb.ap[:, 0:2], ci.ap, [ci], [zb])
                self.tt('dve', zb.ap[:, 2:TT + 2], pc.ap, cu_sb.ap, ALU.mult, [pc, cu_sb, zb], [zb])
                self.copy('pool', ci.ap, zb.ap[:, TT:TT + 2], [zb], [ci])
                wi = (l * 4 + i) * 3
                w0, w1, w2 = (convw.ap[:, wi + k:wi + k + 1] for k in range(3))
                P.add('dve', lambda e, w0=w0: e.tensor_scalar(
                    out=cacc.ap, in0=zb.ap[:, 0:TT], scalar1=w0, scalar2=None, op0=ALU.mult), [zb, convw], [cacc])
                P.add('dve', lambda e, w1=w1: e.scalar_tensor_tensor(
                    out=cacc.ap, in0=zb.ap[:, 1:TT + 1], scalar=w1, in1=cacc.ap, op0=ALU.mult, op1=ALU.add),
                    [zb, convw, cacc], [cacc])
                P.add('dve', lambda e, w2=w2: e.scalar_tensor_tensor(
                    out=cacc.ap, in0=zb.ap[:, 2:TT + 2], scalar=w2, in1=cacc.ap, op0=ALU.mult, op1=ALU.add),
                    [zb, convw, cacc], [cacc])
                yc = yconv.sub(i)
                self.tt('dve', yc.ap, pb.ap, cacc.ap, ALU.mult, [pb, cacc], [yc])
            def _sec_ret():
                for which, dst_tok in ((0, q_tok), (1, k_tok)):
                    sl = next_unit('win')
                    w = sl.ap.rearrange("p (kc n) -> p kc n", kc=8)
                    for tb in range(4):
                        ps = bank()
                        proj_tm(sl, w, tb, ps)
                        psv = ps.ap.rearrange("p (h two f) -> p h two f", h=H_RET, two=2)
                        x1, x2 = psv[:, :, 0, :], psv[:, :, 1, :]
                        dt_ = dst_tok.sub(tb)
                        dv = dt_.ap.rearrange("p (h two f) -> p h two f", h=H_RET, two=2)
                        cb_ = cosT.ap[:, tb, :].unsqueeze(1).broadcast_to([128, H_RET, 64])
                        sb_ = sinT.ap[:, tb, :].unsqueeze(1).broadcast_to([128, H_RET, 64])
                        t1, t2 = rt[0], rt[1]
                        self.tt('dve', t1.ap, x1, cb_, ALU.mult, [ps, cosT], [t1])
                        self.tt('dve', t2.ap, x2, sb_, ALU.mult, [ps, sinT], [t2])
                        self.tt('pool', dv[:, :, 0, :], t1.ap, t2.ap, ALU.subtract, [t1, t2], [dt_])
                        self.tt('dve', t1.ap, x1, sb_, ALU.mult, [ps, sinT], [t1])
                        self.tt('dve', t2.ap, x2, cb_, ALU.mult, [ps, cosT], [t2])
                        self.tt('pool', dv[:, :, 1, :], t1.ap, t2.ap, ALU.add, [t1, t2], [dt_])
                    after_unit()
                if self.retcut == 1:
                    for _ in range(2):
                        next_unit('win'); after_unit()
                    P.add('pool', lambda e: e.memset(yret.ap, 0.0), [], [yret])
                    return
                for tb in range(4):
                    kt_, kd_ = k_tok.sub(tb), Kd_tok.sub(tb)
                    for hd in range(H_RET):
                        sc = kdec.ap[:, tb * H_RET + hd: tb * H_RET + hd + 1]
                        P.add('act', lambda e, kt_=kt_, kd_=kd_, hd=hd, sc=sc: e.activation(
                            out=kd_.ap[:, hd * 128:(hd + 1) * 128], in_=kt_.ap[:, hd * 128:(hd + 1) * 128],
                            func=AF.Copy, scale=sc), [kt_, kdec], [kd_])
                if self.retcut == 2:
                    for _ in range(2):
                        next_unit('win'); after_unit()
                    P.add('pool', lambda e: e.memset(yret.ap, 0.0), [], [yret])
                    return
                sl = next_unit('win')
                w = sl.ap.rearrange("p (kc n) -> p kc n", kc=8)
                for tb in range(4):
                    ps = bank()
                    proj_tm(sl, w, tb, ps)
                    vt = V_tok.sub(tb)
                    self.copy('act', vt.ap, ps.ap, [ps], [vt])
                after_unit()
                if self.retcut == 3:
                    for _ in range(1):
                        next_unit('win'); after_unit()
                    P.add('pool', lambda e: e.memset(yret.ap, 0.0), [], [yret])
                    return
                sl = next_unit('win')
                w = sl.ap.rearrange("p (kc n) -> p kc n", kc=8)
                for i in range(4):
                    ps = bank()
                    proj_fm(sl, w, i * 128, ps)
                    sgi = sgate.sub(i)
                    self.act_fn(sgi.ap, ps.ap, AF.Silu, [ps], [sgi])
                after_unit()
                if self.retcut == 4:
                    for _ in range(0):
                        next_unit('win'); after_unit()
                    P.add('pool', lambda e: e.memset(yret.ap, 0.0), [], [yret])
                    return
                offs = [0, 512, 896, 1152]
                ops_ = {}

                def st_T(hd):
                    hs = slice(hd * 128, (hd + 1) * 128)
                    qt_, qd_, kt_ = QT[hd % 2], QdT[hd % 2], KT[hd % 2]
                    psq, psk = bank(), bank()
                    for tb in range(4):
                        qs, ks = q_tok.sub(tb), k_tok.sub(tb)
                        P.add('pe', lambda e, psq=psq, qs=qs, tb=tb, hs=hs: e.transpose(
                            out=psq.ap[:, tb * 128:(tb + 1) * 128], in_=qs.ap[:, hs], identity=ident.ap),
                            [qs, ident], [psq])
                        P.add('pe', lambda e, psk=psk, ks=ks, tb=tb, hs=hs: e.transpose(
                            out=psk.ap[:, tb * 128:(tb + 1) * 128], in_=ks.ap[:, hs], identity=ident.ap),
                            [ks, ident], [psk])
                    self.copy('act', qt_.ap, psq.ap, [psq], [qt_])
                    self.tt('dve', qd_.ap, psq.ap, qdecT.ap[:, hd, :], ALU.mult, [psq, qdecT], [qd_])
                    self.copy('act', kt_.ap, psk.ap, [psk], [kt_])

                def st_A(hd):
                    qt_, qd_, kt_ = QT[hd % 2], QdT[hd % 2], KT[hd % 2]
                    at = ATb[hd % 2]
                    for b in range(4):
                        n0, wd = 128 * b, TT - 128 * b
                        ps = bank()
                        self.mm(ps.ap[:, 0:wd], kt_.ap[:, b * 128:(b + 1) * 128], qt_.ap[:, n0:TT], True, True,
                                [kt_, qt_], [ps])
                        self.tt('dve', at.ap[:, offs[b]:offs[b] + wd], ps.ap[:, 0:wd], DT.ap[:, hd, 0:wd], ALU.mult,
                                [ps, DT], [at])
                    o_ps = bank(hold=True)
                    ops_[hd] = o_ps
                    if t > 0:
                        sb_h = state_b[l].sub(hd)
                        self.mm(o_ps.ap, sb_h.ap, qd_.ap, True, False, [sb_h, qd_], [o_ps])

                def st_O(hd):
                    hs = slice(hd * 128, (hd + 1) * 128)
                    at = ATb[hd % 2]
                    o_ps = ops_[hd]
                    first = (t == 0)
                    for b in range(4):
                        n0, wd = 128 * b, TT - 128 * b
                        vt = V_tok.sub(b)
                        self.mm(o_ps.ap[:, n0:TT], vt.ap[:, hs], at.ap[:, offs[b]:offs[b] + wd], first, b == 3,
                                [vt, at], [o_ps])
                        first = False
                    self.act_fn(osq.ap, o_ps.ap, AF.Square, [o_ps], [osq])
                    if t < NT - 1:
                        st_ps = bank()
                        for b in range(4):
                            kd_, vt = Kd_tok.sub(b), V_tok.sub(b)
                            self.mm(st_ps.ap[:, 0:128], kd_.ap[:, hs], vt.ap[:, hs], b == 0, b == 3, [kd_, vt], [st_ps])
                        sf, sbh = state_f[l].sub(hd), state_b[l].sub(hd)
                        if t == 0:
                            self.copy('dve', sf.ap, st_ps.ap[:, 0:128], [st_ps], [sf])
                        else:
                            P.add('dve', lambda e, sf=sf, st_ps=st_ps, hd=hd: e.scalar_tensor_tensor(
                                out=sf.ap, in0=sf.ap, scalar=TILE_DECAY[hd], in1=st_ps.ap[:, 0:128],
                                op0=ALU.mult, op1=ALU.add), [sf, st_ps], [sf])
                        self.copy('pool', sbh.ap, sf.ap, [sf], [sbh])

                def st_N(hd):
                    o_ps = ops_.pop(hd)
                    ss = bank()
                    self.mm(ss.ap, ones_bf.ap, osq.ap, True, True, [ones_bf, osq], [ss])
                    self.act_fn(rn.ap, ss.ap, AF.Ln, [ss], [rn], scale=1.0 / 128, bias=EPS)
                    self.act_fn(rn.ap, rn.ap, AF.Exp, [rn], [rn], scale=-0.5)
                    sgi = sgate.sub(hd)
                    self.tt('pool', rgm.ap, rn.ap, sgi.ap, ALU.mult, [rn, sgi], [rgm])
                    yr = yret.sub(hd)
                    self.tt('dve', yr.ap, o_ps.ap, rgm.ap, ALU.mult, [o_ps, rgm], [yr])
                    release(o_ps)

                for fn_, hd_ in ((st_T, 0), (st_T, 1), (st_A, 0), (st_T, 2), (st_O, 0), (st_A, 1), (st_T, 3),
                                 (st_N, 0), (st_O, 1), (st_A, 2), (st_N, 1), (st_O, 2), (st_A, 3), (st_N, 2),
                                 (st_O, 3), (st_N, 3)):
                    fn_(hd_)
            if 'ret' in self.dbg:
                _sec_ret()
                if self.retcut in (5, 6, 7, 8, 51, 52):
                    P.add('pool', lambda e: e.memset(yret.ap, 0.0), [], [yret])
            else:
                P.add('pool', lambda e: e.memset(yret.ap, 0.0), [], [yret])
            def _sec_att():
                sl = next_unit('win')
                w = sl.ap.rearrange("p (kc n) -> p kc n", kc=8)
                for i in range(4):
                    ps = bank()
                    proj_fm(sl, w, i * 128, ps)
                    qi = qT_att.sub(i)
                    P.add('act', lambda e, qi=qi, ps=ps: e.mul(out=qi.ap, in_=ps.ap, mul=0.125), [ps], [qi])
                after_unit()
                sl = next_unit('win')
                w = sl.ap.rearrange("p (kc n) -> p kc n", kc=8)
                kcur = kT_att[l].sub(cur)
                for i in range(4):
                    ps = bank()
                    proj_fm(sl, w, i * 128, ps)
                    ki = kcur.sub(i)
                    self.copy('dve' if i % 2 else 'act', ki.ap, ps.ap, [ps], [ki])
                after_unit()
                sl = next_unit('win')
                w = sl.ap.rearrange("p (kc n) -> p kc n", kc=8)
                vcur = v_att[l].sub(cur)
                for tb in range(4):
                    ps = bank()
                    proj_tm(sl, w, tb, ps)
                    vi = vcur.sub(tb)
                    self.copy('dve' if tb % 2 else 'act', vi.ap, ps.ap, [ps], [vi])
                after_unit()
                blocks = [4, 5, 6, 7] + ([0, 1, 2, 3] if t > 0 else [])
                LA = 3
                units_ = [(hp, bi, b, hh) for hp in range(4) for bi, b in enumerate(blocks) for hh in range(2)]
                acc = {}
                st_ = {}

                def stage_A(ui):
                    hp, bi, b, hh = units_[ui]
                    if hp not in acc:
                        acc[hp] = (bank(hold=True), bank(hold=True))
                    qh = qT_att.sub(hp)
                    n0 = 64 * max(0, 2 * b - 8)
                    n1 = 64 * (min(7, 2 * b + 1) + 1)
                    wd = n1 - n0
                    half = cur if b >= 4 else prv
                    bb = b % 4
                    kh = kT_att[l].sub(half).sub(hp)
                    head = 2 * hp + hh
                    r0 = hh * 64
                    s_ps = bank()
                    self.mm(s_ps.ap[:, 0:wd], kh.ap[r0:r0 + 64, bb * 128:(bb + 1) * 128],
                            qh.ap[r0:r0 + 64, n0:n1], True, True, [kh, qh], [s_ps])
                    pt = PT[ui % len(PT)]
                    fb = fbias.ap[:, l * H_ATT + head: l * H_ATT + head + 1]
                    t0 = n0 + 512 - 128 * b
                    nw = max(0, min(256 - t0, wd))
                    if nw > 0:
                        tmp = stmp[ui % len(stmp)]
                        self.tt('dve', tmp.ap[:, 0:nw], s_ps.ap[:, 0:nw], gt.ap[:, head, t0:t0 + nw], ALU.add,
                                [s_ps, gt], [tmp])
                        self.act_fn(pt.ap[:, 0:nw], tmp.ap[:, 0:nw], AF.Exp, [tmp], [pt])
                    if wd > nw:
                        self.act_fn(pt.ap[:, nw:wd], s_ps.ap[:, nw:wd], AF.Exp, [s_ps, fbias], [pt], bias=fb)
                    if b < 4:
                        P.add('pool', lambda e, pt=pt, wd=wd: e.memset(pt.ap[0:64, wd - 64:wd], 0.0), [], [pt])
                    st_[ui] = (pt, n0, n1, wd, bb, half, head, r0)

                def stage_C(ui):
                    hp, bi, b, hh = units_[ui]
                    pt, n0, n1, wd, bb, half, head, r0 = st_.pop(ui)
                    o_ps, den_ps = acc[hp]
                    vh = v_att[l].sub(half).sub(bb)
                    self.mm(o_ps.ap[r0:r0 + 64, n0:n1], vh.ap[:, head * 64:(head + 1) * 64], pt.ap[:, 0:wd],
                            bi == 0, bi == len(blocks) - 1, [vh, pt], [o_ps])
                    self.mm(den_ps.ap[r0:r0 + 64, n0:n1], ones_bf.ap[:, 0:64], pt.ap[:, 0:wd],
                            bi == 0, bi == len(blocks) - 1, [ones_bf, pt], [den_ps])
                    if bi == len(blocks) - 1 and hh == 1:
                        self.act_fn(rec.ap, den_ps.ap, AF.Ln, [den_ps], [rec])
                        self.act_fn(rec.ap, rec.ap, AF.Exp, [rec], [rec], scale=-1.0)
                        ya = yatt.sub(hp)
                        self.tt('dve', ya.ap, o_ps.ap, rec.ap, ALU.mult, [o_ps, rec], [ya])
                        release(o_ps)
                        release(den_ps)

                nu = len(units_)
                for i in range(nu + LA):
                    if i < nu:
                        stage_A(i)
                    if i - LA >= 0:
                        stage_C(i - LA)
            if 'att' in self.dbg:
                _sec_att()
            else:
                P.add('pool', lambda e: e.memset(yatt.ap, 0.0), [], [yatt])
            def _sec_merge():
                ys = (yconv, yret, yatt)
                for cp in range(4):
                    for i in range(3):
                        sl = next_unit('mg')
                        wg = sl.ap[:, 0:2048].rearrange("p (kc n) -> p kc n", kc=8)
                        wb = sl.ap[:, 2048:3072].rearrange("p (kk n) -> p kk n", kk=4)
                        for cc in range(2):
                            pg, pb = bank(), bank()
                            proj_fm(sl, wg, cc * 128, pg)
                            proj_fm(sl, wb, cc * 128, pb, kchunks=4, rhs_src=ys[i])
                            sg_, pr_, ma_ = sig[cc], prd[cc], macc[cc]
                            self.act_fn(sg_.ap, pg.ap, AF.Sigmoid, [pg], [sg_])
                            if i == 0:
                                self.tt('dve', ma_.ap, pb.ap, sg_.ap, ALU.mult, [pb, sg_], [ma_])
                            else:
                                self.tt('dve', pr_.ap, pb.ap, sg_.ap, ALU.mult, [pb, sg_], [pr_])
                                if i == 1:
                                    self.tt('pool', ma_.ap, ma_.ap, pr_.ap, ALU.add, [ma_, pr_], [ma_])
                                else:
                                    mc = mT.sub(cp * 2 + cc)
                                    self.tt('pool', mc.ap, ma_.ap, pr_.ap, ALU.add, [ma_, pr_], [mc])
                        after_unit()
                for cp in range(4):
                    sl = next_unit('wo')
                    w = sl.ap[:, 0:2048].rearrange("p (kc n) -> p kc n", kc=8)
                    for cc in range(2):
                        ps = bank()
                        proj_fm(sl, w, cc * 128, ps, rhs_src=mT)
                        norm_flush(keep=1)
                        xc = xT.sub(cp * 2 + cc)
                        self.tt('dve', xc.ap, ps.ap, xc.ap, ALU.add, [ps, xc], [xc])
                        norm_chunk(cp * 2 + cc)
                    after_unit()
            if 'merge' in self.dbg:
                _sec_merge()

        c_xin = self.chan("c_xin")
        c_out = self.chan("c_out")
        c_gt = self.chan("c_gt")
        c_rope = [self.chan("c_cos"), self.chan("c_sin")]

        def load_x(t):
            self.dma(xs, x, c_xin, in_ap=x.ap[t * TT:(t + 1) * TT].rearrange("(tb p) d -> p tb d", p=128))

        load_x(0)
        for t in range(NT):
            if has_mix:
                self.dma(cosT, cos_d, c_rope[0], in_ap=cos_d.ap[t * TT:(t + 1) * TT].rearrange("(tb p) f -> p tb f", p=128))
                self.dma(sinT, sin_d, c_rope[1], in_ap=sin_d.ap[t * TT:(t + 1) * TT].rearrange("(tb p) f -> p tb f", p=128))
            for c in range(NCH):
                ps = bank()
                for tb in range(4):
                    P.add('pe', lambda e, ps=ps, tb=tb, c=c: e.transpose(
                        out=ps.ap[:, tb * 128:(tb + 1) * 128], in_=xs.ap[:, tb, c * 128:(c + 1) * 128],
                        identity=ident.ap), [xs, ident], [ps])
                xc = xT.sub(c)
                self.copy('dve' if c % 2 else 'act', xc.ap, ps.ap, [ps], [xc])
                norm_flush(keep=1)
                norm_chunk(c)
            for l in range(NL):
                for si, st in enumerate(self.stages):
                    if l == NL - 1 and si == len(self.stages) - 1 and t + 1 < NT:
                        load_x(t + 1)
                    if st == 'ffn1':
                        ffn(l, 1)
                    elif st == 'ffn2':
                        ffn(l, 2)
                    elif st == 'mix':
                        mixer(l, t)
            rmsnorm(3 * NL, out_bf16=False)
            for tb in range(4):
                for half in range(2):
                    ps = bank()
                    for cc in range(4):
                        oc = outT.sub(half * 4 + cc)
                        P.add('pe', lambda e, ps=ps, cc=cc, oc=oc, tb=tb: e.transpose(
                            out=ps.ap[:, cc * 128:(cc + 1) * 128], in_=oc.ap[:, tb * 128:(tb + 1) * 128],
                            identity=ident.ap), [oc, ident], [ps])
                    dst = xo.ap[:, tb, half * 512:(half + 1) * 512]
                    self.copy('dve' if half else 'act', dst, ps.ap, [ps], [xo])
            self.dma(outv, xo, c_out,
                     out_ap=out.ap()[t * TT:(t + 1) * TT].rearrange("(tb p) d -> p tb d", p=128))
        P.add('sp', None, [outv], [])

    def finish(self):
        nc, P = self.nc, self.P
        sems = {e: self.es.enter_context(nc.semaphore("sem_" + e)) for e in Prog.ENGS}
        chan_sems = {c: self.es.enter_context(nc.semaphore(c)) for c in self.chans}
        block = self.es.enter_context(nc.Block())
        P.emit(block, sems, chan_sems)
        self.es.close()
        return nc


def build_program(S, n_layers=L, stages=('ffn1', 'mix', 'ffn2')):
    b = Builder(S, n_layers, stages)
    b.build()
    return b.finish()


def host_inputs(inp, S, n_layers=L):
    nw = []
    for l in range(n_layers):
        nw += [inp["ffn1_norm"][l], inp["mix_norm"][l], inp["ffn2_norm"][l]]
    nw.append(inp["final_norm"])
    normw = _chunked_vec(np.stack(nw)).reshape(128, -1)
    cw = np.asarray(inp["conv_w"], np.float32)[:n_layers]
    convw = cw.reshape(n_layers, 3, 4, 128).transpose(3, 0, 2, 1).reshape(128, -1)
    tabs, _ = _const_tables(S)
    G, fb = _gather_bias(np.asarray(inp["rel_bias"], np.float32)[:n_layers])
    shared = {
        "normw": np.ascontiguousarray(normw, np.float32),
        "ident": np.eye(128, dtype=np.float32),
        "convw": np.ascontiguousarray(convw, np.float32),
        "gtab": G, "fbias": fb,
    }
    shared.update(tabs)
    for k in ("ffn1_w_gate", "ffn1_w_up", "ffn1_w_down", "ffn2_w_gate", "ffn2_w_up", "ffn2_w_down",
              "w_in", "w_branch", "w_merge_gate", "w_out"):
        shared[k] = np.ascontiguousarray(np.asarray(inp[k], np.float32)[:n_layers])
    return shared


_NC_CACHE = {}


def kernel(**inputs):
    x = np.asarray(inputs["x"], np.float32)
    B, S, _ = x.shape
    key = (S,)
    if key not in _NC_CACHE:
        _NC_CACHE[key] = build_program(S)
    nc = _NC_CACHE[key]
    shared = host_inputs(inputs, S)
    in_maps = []
    for b in range(B):
        m = dict(shared)
        m["x"] = np.ascontiguousarray(x[b])
        in_maps.append(m)
    res = run_bass_kernel_spmd(nc, in_maps, core_ids=list(range(B)))
    return np.stack([np.asarray(r["out"], np.float32) for r in res.results], axis=0)
```

```python
import bisect
from contextlib import ExitStack

import numpy as np
import concourse.bass as bass
import concourse.mybir as mybir
from concourse.bass_utils import run_bass_kernel_spmd

F32 = mybir.dt.float32
BF16 = mybir.dt.bfloat16
AF = mybir.ActivationFunctionType
ALU = mybir.AluOpType

D = 1024
DFF = 2816
BW = 512
L = 2
TT = 512
NCH = D // 128
NJ = DFF // 128
EPS = 1e-6
H_RET = 4
H_ATT = 8
NEG = -30000.0


class Arena:
    def __init__(self, size):
        self.los = [0]
        self.segs = [[0, size, None, {}]]

    def _split(self, pos):
        i = bisect.bisect_right(self.los, pos) - 1
        s = self.segs[i]
        if s[0] < pos < s[1]:
            ns = [pos, s[1], s[2], dict(s[3])]
            s[1] = pos
            self.segs.insert(i + 1, ns)
            self.los.insert(i + 1, pos)

    def access(self, lo, hi, op, write):
        self._split(lo)
        self._split(hi)
        i0 = bisect.bisect_left(self.los, lo)
        i1 = bisect.bisect_left(self.los, hi)
        deps = []
        for i in range(i0, i1):
            s = self.segs[i]
            if s[2] is not None:
                deps.append((s[2], 'w'))
            if write:
                for r in s[3].values():
                    deps.append((r, 'r'))
        if write:
            self.segs[i0:i1] = [[lo, hi, op, {}]]
            self.los[i0:i1] = [lo]
        else:
            key = op.eng if op.dma is None else ('dma', op.uid)
            for i in range(i0, i1):
                self.segs[i][3][key] = op
        return deps


class View:
    def __init__(self, ap, arena, lo, hi):
        self.ap, self.arena, self.lo, self.hi = ap, arena, lo, hi

    def sub(self, i):
        n = self.ap.shape[1]
        step = (self.hi - self.lo) // n
        return View(self.ap[:, i], self.arena, self.lo + i * step, self.lo + (i + 1) * step)

    def cols(self, a, b):
        n = self.ap.shape[-1]
        assert len(self.ap.shape) == 2
        step = (self.hi - self.lo) // n
        return View(self.ap[:, a:b], self.arena, self.lo + a * step, self.lo + b * step)

    def with_ap(self, ap):
        return View(ap, self.arena, self.lo, self.hi)


class Op:
    __slots__ = ('eng', 'fn', 'deps', 'signal', 'sigidx', 'dma', 'uid')


class Prog:
    ENGS = ('pe', 'act', 'dve', 'pool', 'sp')

    def __init__(self, nc):
        self.nc = nc
        self.arenas = {}
        self.ops = {e: [] for e in self.ENGS}
        self.chan_count = {}
        self.uid = 0

    def arena(self, name, size):
        self.arenas[name] = Arena(size)
        return name

    def add(self, eng, fn, reads=(), writes=(), chan=None):
        op = Op()
        op.eng, op.fn, op.signal, op.sigidx = eng, fn, False, None
        op.uid = self.uid
        self.uid += 1
        if chan is not None:
            self.chan_count[chan] = self.chan_count.get(chan, 0) + 16
            op.dma = (chan, self.chan_count[chan])
        else:
            op.dma = None
        raw = []
        for v in reads:
            raw += self.arenas[v.arena].access(v.lo, v.hi, op, v.arena == 'psum')
        for v in writes:
            raw += self.arenas[v.arena].access(v.lo, v.hi, op, True)
        deps = {}
        for d, kind in raw:
            if d is op:
                continue
            if d.dma is None and op.dma is None and d.eng == eng:
                if eng == 'pe' or kind == 'r':
                    continue
            deps[d.uid] = d
            if d.dma is None:
                d.signal = True
        op.deps = list(deps.values())
        self.ops[eng].append(op)
        return op

    def emit(self, block, sems, chan_sems):
        nc = self.nc
        for e in self.ENGS:
            k = 0
            for op in self.ops[e]:
                if op.dma is None and op.signal:
                    k += 1
                    op.sigidx = k
        engobj = {'pe': 'tensor', 'act': 'scalar', 'dve': 'vector', 'pool': 'gpsimd', 'sp': 'sync'}

        def run(e):
            def body(eng):
                seen = {}
                for op in self.ops[e]:
                    for d in op.deps:
                        if d.dma is not None:
                            key, val = ('c', d.dma[0]), d.dma[1]
                            sem = chan_sems[d.dma[0]]
                        else:
                            key, val = ('e', d.eng), d.sigidx
                            sem = sems[d.eng]
                        if seen.get(key, 0) >= val:
                            continue
                        seen[key] = val
                        eng.wait_ge(sem, val)
                    if op.fn is None:
                        continue
                    inst = op.fn(eng)
                    if op.dma is not None:
                        inst.then_inc(chan_sems[op.dma[0]], 16)
                    elif op.signal:
                        inst.then_inc(sems[e], 1)
            return body

        for e in self.ENGS:
            if self.ops[e]:
                getattr(block, engobj[e])(run(e))


def _chunked_vec(v):
    v = np.asarray(v, np.float32)
    lead = v.shape[:-1]
    return np.ascontiguousarray(np.moveaxis(v.reshape(lead + (v.shape[-1] // 128, 128)), -1, 0))


def _const_tables(S):
    inv_freq = (np.float32(10000.0) ** (-np.linspace(0.0, 1.0, 64, dtype=np.float32))).astype(np.float32)
    ang = (np.arange(S, dtype=np.float32)[:, None] * inv_freq[None, :]).astype(np.float32)
    cos = np.cos(ang.astype(np.float64)).astype(np.float32)
    sin = np.sin(ang.astype(np.float64)).astype(np.float32)
    lg = np.log1p(-np.exp2(-5.0 - np.arange(H_RET, dtype=np.float64)))
    p = np.arange(128)[:, None]
    j = np.arange(TT)[None, :]
    scale = 128.0 ** -0.5
    DT = np.zeros((128, H_RET, TT), np.float64)
    for hd in range(H_RET):
        m = np.exp(lg[hd] * np.abs(j - p)) * ((p // 64) <= (j // 64))
        DT[:, hd, :] = m * scale
    qdec = np.exp(lg[:, None] * (np.arange(TT)[None, :] + 1.0)) * scale
    qdecT = np.broadcast_to(qdec[None], (128, H_RET, TT))
    pos = np.arange(4)[None, :, None] * 128 + np.arange(128)[:, None, None]
    kdec = np.exp(lg[None, None, :] * (TT - 1.0 - pos))
    tile_decay = [float(np.exp(lg[hd] * TT)) for hd in range(H_RET)]
    return dict(rope_cos=cos, rope_sin=sin,
                DT=np.ascontiguousarray(DT.reshape(128, -1), np.float32),
                qdecT=np.ascontiguousarray(qdecT.reshape(128, -1), np.float32),
                kdec=np.ascontiguousarray(kdec.reshape(128, -1), np.float32)), tile_decay


def _gather_bias(rel_bias):
    nl = rel_bias.shape[0]
    mp = np.arange(128)[:, None]
    t = np.arange(256)[None, :]
    idx = np.clip(t - mp, -128, 128) + 128
    valid = ((t // 64) - (mp // 64)) >= 0
    G = np.empty((nl, 128, H_ATT, 256), np.float32)
    for l in range(nl):
        for hd in range(H_ATT):
            G[l, :, hd, :] = np.where(valid, rel_bias[l, hd][idx], np.float32(NEG))
    fb = np.broadcast_to(rel_bias[:, :, 256].reshape(1, nl * H_ATT), (128, nl * H_ATT))
    return np.ascontiguousarray(G.reshape(nl, 128, -1)), np.ascontiguousarray(fb, np.float32)


_LG = np.log1p(-np.exp2(-5.0 - np.arange(H_RET, dtype=np.float64)))
TILE_DECAY = [float(np.exp(_LG[hd] * TT)) for hd in range(H_RET)]


class Builder:
    def __init__(self, S, n_layers=L, stages=('ffn1', 'mix', 'ffn2')):
        self.S = S
        self.NT = S // TT
        self.n_layers = n_layers
        self.stages = stages
        self.nc = bass.Bass("TRN2", target_bir_lowering=False)
        self.P = Prog(self.nc)
        self.es = ExitStack()
        self.chans = []
        import os
        self.dbg = set(os.environ.get('MIXDBG', 'conv,ret,att,merge').split(','))
        self.retcut = int(os.environ.get('RETCUT', '99'))

    def din(self, name, shape, dtype=F32):
        t = self.nc.dram_tensor(name, list(shape), dtype, kind="ExternalInput")
        self.P.arena('d_' + name, 1)
        return View(t.ap(), 'd_' + name, 0, 1)

    def dscratch(self, name, shape, dtype):
        t = self.nc.dram_tensor(name, list(shape), dtype, kind="Internal")
        self.P.arena('d_' + name, 1)
        return View(t.ap(), 'd_' + name, 0, 1)

    @staticmethod
    def _shape_ap(ap, free_shape):
        if len(free_shape) == 2:
            ap = ap.rearrange("p (a b) -> p a b", a=free_shape[0])
        elif len(free_shape) == 3:
            ap = ap.rearrange("p (a b c) -> p a b c", a=free_shape[0], b=free_shape[1])
        return ap

    def sb(self, name, free_shape, dtype):
        n = int(np.prod(free_shape))
        t = self.es.enter_context(self.nc.sbuf_tensor(name, [128, n], dtype))
        esz = 2 if dtype == BF16 else 4
        self.P.arena(name, n * esz)
        return View(self._shape_ap(t[:, :], free_shape), name, 0, n * esz)

    def scr(self, off, free_shape, dtype):
        n = int(np.prod(free_shape))
        esz = 2 if dtype == BF16 else 4
        assert off % 4 == 0 and (n * esz) % 4 == 0 and off + n * esz <= self.scr_bytes, (off, n, esz)
        ap = self.scr_t[:, off // 4:(off + n * esz) // 4]
        if dtype == BF16:
            ap = ap.bitcast(BF16)
        return View(self._shape_ap(ap, free_shape), 'scr', off, off + n * esz)

    def chan(self, name):
        self.chans.append(name)
        return name

    def mm(self, out, lhsT, rhs, start, stop, reads, writes):
        self.P.add('pe', lambda e: e.matmul(out, lhsT, rhs, start=start, stop=stop), reads, writes)

    def dma(self, out_v, in_v, chan, out_ap=None, in_ap=None, q='sp'):
        oa = out_v.ap if out_ap is None else out_ap
        ia = in_v.ap if in_ap is None else in_ap
        return self.P.add(q, lambda e: e.dma_start(out=oa, in_=ia), [in_v], [out_v], chan=chan)

    def act_fn(self, out_ap, in_ap, func, reads, writes, **kw):
        self.P.add('act', lambda e: e.activation(out=out_ap, in_=in_ap, func=func, **kw), reads, writes)

    def tt(self, eng, out_ap, in0, in1, op, reads, writes):
        self.P.add(eng, lambda e: e.tensor_tensor(out=out_ap, in0=in0, in1=in1, op=op), reads, writes)

    def copy(self, eng, out_ap, in_ap, reads, writes):
        if eng == 'act':
            self.P.add('act', lambda e: e.copy(out=out_ap, in_=in_ap), reads, writes)
        else:
            self.P.add(eng, lambda e: e.tensor_copy(out=out_ap, in_=in_ap), reads, writes)

    def build(self):
        nc, P = self.nc, self.P
        S, NT, NL = self.S, self.NT, self.n_layers
        K1 = 1024
        self.scr_bytes = 64 * K1
        has_mix = 'mix' in self.stages
        x = self.din("x", [S, D])
        out = self.nc.dram_tensor("out", [S, D], F32, kind="ExternalOutput")
        P.arena('d_out', 1)
        outv = View(out.ap(), 'd_out', 0, 1)
        wf = {}
        for f in (1, 2):
            wf[f] = dict(g=self.din(f"ffn{f}_w_gate", [NL, D, DFF]),
                         u=self.din(f"ffn{f}_w_up", [NL, D, DFF]),
                         d=self.din(f"ffn{f}_w_down", [NL, DFF, D]))
        w_in_d = self.din("w_in", [NL, D, 10 * BW])
        w_br_d = self.din("w_branch", [NL, 3, BW, D])
        w_mg_d = self.din("w_merge_gate", [NL, 3, D, D])
        w_out_d = self.din("w_out", [NL, D, D])
        normw_d = self.din("normw", [128, (3 * NL + 1) * NCH])
        ident_d = self.din("ident", [128, 128])
        convw_d = self.din("convw", [128, NL * 4 * 3])
        cos_d = self.din("rope_cos", [S, 64])
        sin_d = self.din("rope_sin", [S, 64])
        DT_d = self.din("DT", [128, H_RET * TT])
        qdecT_d = self.din("qdecT", [128, H_RET * TT])
        kdec_d = self.din("kdec", [128, 4 * H_RET])
        gtab_d = self.din("gtab", [NL, 128, H_ATT * 256])
        fbias_d = self.din("fbias", [128, NL * H_ATT])

        units = {}
        for l in range(NL):
            for f in (1, 2):
                for u in range(11):
                    units[('gu', l, f, u)] = self.dscratch(f"s_gu_{l}_{f}_{u}", [128, 4096], BF16)
                for cp in range(4):
                    for jh in range(2):
                        units[('dn', l, f, cp, jh)] = self.dscratch(f"s_dn_{l}_{f}_{cp}_{jh}", [128, 2816], BF16)
            for i in range(4):
                units[('cv', l, i)] = self.dscratch(f"s_cv_{l}_{i}", [128, 3072], BF16)
            for blk in range(3, 10):
                units[('win', l, blk)] = self.dscratch(f"s_win_{l}_{blk}", [128, 4096], BF16)
            for cp in range(4):
                for i in range(3):
                    units[('mg', l, cp, i)] = self.dscratch(f"s_mg_{l}_{cp}_{i}", [128, 3072], BF16)
                units[('wo', l, cp)] = self.dscratch(f"s_wo_{l}_{cp}", [128, 2048], BF16)

        xT = self.sb("xT", [NCH, TT], F32)
        h = self.sb("h", [NCH, TT], BF16)
        stdb = self.sb("std", [TT], F32)
        rstd = self.sb("rstd", [TT], F32)
        normw = self.sb("normw_sb", [(3 * NL + 1) * NCH], F32)
        ident = self.sb("ident_sb", [128], F32)
        ident_bf = self.sb("ident_bf", [128], BF16)
        ones_bf = self.sb("ones_bf", [128], BF16)
        NSLOT = 4
        slots = [self.sb(f"wslot{i}", [4096], BF16) for i in range(NSLOT)]
        slot_ch = [self.chan(f"c_slot{i}") for i in range(NSLOT)]
        if has_mix:
            convw = self.sb("convw_sb", [NL * 4 * 3], F32)
            cosT = self.sb("cosT", [4, 64], F32)
            sinT = self.sb("sinT", [4, 64], F32)
            DT = self.sb("DT_sb", [H_RET, TT], F32)
            qdecT = self.sb("qdecT_sb", [H_RET, TT], F32)
            kdec = self.sb("kdec_sb", [4 * H_RET], F32)
            fbias = self.sb("fbias_sb", [NL * H_ATT], F32)
            carry = [self.sb(f"carry{l}", [4, 2], F32) for l in range(NL)]
            state_f = [self.sb(f"state_f{l}", [H_RET, 128], F32) for l in range(NL)]
            state_b = [self.sb(f"state_b{l}", [H_RET, 128], BF16) for l in range(NL)]
            kT_att = [self.sb(f"kT_att{l}", [2, 4, TT], BF16) for l in range(NL)]
            v_att = [self.sb(f"v_att{l}", [2, 4, TT], BF16) for l in range(NL)]
            gt = self.sb("gt", [H_ATT, 256], F32)
        self.scr_t = self.es.enter_context(nc.sbuf_tensor("scr", [128, self.scr_bytes // 4], F32))
        P.arena('scr', self.scr_bytes)
        NSTG = 4
        stg32 = [self.sb(f"stg32_{i}", [1024], F32) for i in range(NSTG)]
        stg32_ch = [self.chan(f"c_stg32_{i}") for i in range(NSTG)]
        slotst_ch = [self.chan(f"c_slotst{i}") for i in range(NSLOT)]
        act = self.scr(0, [NJ, TT], BF16)
        sgb = [self.scr(22 * K1 + i * 2 * K1, [TT], F32) for i in range(2)]
        sqr = [self.scr(26 * K1 + i * K1, [TT], BF16) for i in range(4)]
        outT = self.scr(0, [NCH, TT], F32)
        xo = self.scr(16 * K1, [4, D], F32)
        xs = self.scr(32 * K1, [4, D], F32)
        if has_mix:
            yconv = self.scr(0, [4, TT], BF16)
            yret = self.scr(4 * K1, [4, TT], BF16)
            yatt = self.scr(8 * K1, [4, TT], BF16)
            sgate = self.scr(12 * K1, [4, TT], F32)
            cu_sb = self.scr(20 * K1, [TT], F32)
            zb = self.scr(22 * K1, [TT + 2], F32)
            cacc = self.scr(25 * K1, [TT], F32)
            q_tok = self.scr(27 * K1, [4, BW], F32)
            k_tok = self.scr(35 * K1, [4, BW], F32)
            Kd_tok = self.scr(43 * K1, [4, BW], BF16)
            V_tok = self.scr(47 * K1, [4, BW], BF16)
            rt = [self.scr(51 * K1 + i * K1, [4, 64], F32) for i in range(2)]
            QT = [self.scr(53 * K1 + i * K1, [TT], BF16) for i in range(2)]
            QdT = [self.scr(55 * K1 + i * K1, [TT], BF16) for i in range(2)]
            KT = [self.scr(57 * K1 + i * K1, [TT], BF16) for i in range(2)]
            ATb = [self.scr(59 * K1 + i * 2560, [1280], BF16) for i in range(2)]
            rn = self.scr(20 * K1, [TT], F32)
            osq = self.scr(22 * K1, [TT], BF16)
            rgm = self.scr(23 * K1, [TT], F32)
            qm = self.scr(20 * K1, [4, 2, TT], BF16)
            stmp = [self.scr(28 * K1 + i * 2 * K1, [TT], F32) for i in range(4)]
            Vm = [self.scr(38 * K1 + i * 256, [128], BF16) for i in range(8)]
            onesm = [self.scr(40 * K1 + i * 256, [128], BF16) for i in range(2)]
            PT = [self.scr(54 * K1 + i * K1, [TT], BF16) for i in range(6)]
            rec = self.scr(36 * K1, [TT], F32)
            sig = [self.scr(34 * K1 + i * 2 * K1, [TT], F32) for i in range(2)]
            prd = [self.scr(38 * K1 + i * 2 * K1, [TT], F32) for i in range(2)]
            macc = [self.scr(42 * K1 + i * 2 * K1, [TT], F32) for i in range(2)]
            mT = self.scr(46 * K1, [NCH, TT], BF16)
        pst = self.es.enter_context(nc.psum_tensor("psum", [128, 8 * 512], F32))
        P.arena('psum', 8 * 2048)
        banks = [View(pst[:, b * 512:(b + 1) * 512], 'psum', b * 2048, (b + 1) * 2048) for b in range(8)]
        self.bank_i = 0

        self.held = set()

        def bank(hold=False):
            while (self.bank_i % 8) in self.held:
                self.bank_i += 1
            k = self.bank_i % 8
            self.bank_i += 1
            if hold:
                self.held.add(k)
            return banks[k]

        def release(bv):
            self.held.discard(bv.lo // 2048)

        def const_load(dst, src):
            self.dma(dst, src, self.chan(f"c_const{len(self.chans)}"))

        const_load(normw, normw_d)
        const_load(ident, ident_d)
        P.add('pool', lambda e: e.memset(ones_bf.ap, 1.0), [], [ones_bf])
        self.copy('dve', ident_bf.ap, ident.ap, [ident], [ident_bf])
        if has_mix:
            const_load(convw, convw_d)
            const_load(DT, DT_d.with_ap(DT_d.ap.rearrange("p (a b) -> p a b", a=H_RET)))
            const_load(qdecT, qdecT_d.with_ap(qdecT_d.ap.rearrange("p (a b) -> p a b", a=H_RET)))
            const_load(kdec, kdec_d)
            const_load(fbias, fbias_d)

        unit_src = {}
        ukey = {id(v): k for k, v in units.items()}

        def prepass_unit(unit_v, pieces):
            unit_src[ukey[id(unit_v)]] = pieces

        for l in range(NL):
            for f in (1, 2):
                if f"ffn{f}" not in self.stages:
                    continue
                g, u_, d_ = wf[f]['g'], wf[f]['u'], wf[f]['d']
                gl = g.ap[l].rearrange("(kc p) n -> p kc n", p=128)
                ul = u_.ap[l].rearrange("(kc p) n -> p kc n", p=128)
                dl = d_.ap[l].rearrange("(j p) n -> p j n", p=128)
                for u in range(11):
                    prepass_unit(units[('gu', l, f, u)],
                                 [(g, gl[:, :, u * 256:(u + 1) * 256], 0, 8, 256),
                                  (u_, ul[:, :, u * 256:(u + 1) * 256], 2048, 8, 256)])
                for cp in range(4):
                    for jh in range(2):
                        prepass_unit(units[('dn', l, f, cp, jh)],
                                     [(d_, dl[:, jh * 11:(jh + 1) * 11, cp * 256:(cp + 1) * 256], 0, 11, 256)])
            if has_mix:
                wl = w_in_d.ap[l].rearrange("(kc p) n -> p kc n", p=128)
                for i in range(4):
                    prepass_unit(units[('cv', l, i)],
                                 [(w_in_d, wl[:, :, g_ * BW + i * 128: g_ * BW + (i + 1) * 128], k_ * 1024, 8, 128)
                                  for k_, g_ in enumerate((0, 1, 2))])
                for blk in range(3, 10):
                    prepass_unit(units[('win', l, blk)], [(w_in_d, wl[:, :, blk * BW:(blk + 1) * BW], 0, 8, BW)])
                for cp in range(4):
                    for i in range(3):
                        mgl = w_mg_d.ap[l, i].rearrange("(kc p) n -> p kc n", p=128)
                        brl = w_br_d.ap[l, i].rearrange("(kk p) n -> p kk n", p=128)
                        prepass_unit(units[('mg', l, cp, i)],
                                     [(w_mg_d, mgl[:, :, cp * 256:(cp + 1) * 256], 0, 8, 256),
                                      (w_br_d, brl[:, :, cp * 256:(cp + 1) * 256], 2048, 4, 256)])
                    wol = w_out_d.ap[l].rearrange("(kc p) n -> p kc n", p=128)
                    prepass_unit(units[('wo', l, cp)], [(w_out_d, wol[:, :, cp * 256:(cp + 1) * 256], 0, 8, 256)])

        self.slot_i = 0

        self.stg_i = 0
        cast_engs = ['act', 'dve', 'pool']

        def load_unit(key, n, first):
            i = self.slot_i % NSLOT
            self.slot_i += 1
            sl = slots[i]
            uv = units[key]
            if not first:
                self.dma(sl, uv, slot_ch[i], out_ap=sl.ap[:, 0:n], in_ap=uv.ap[:, 0:n])
                return sl
            for (sv, sap, off, a, b) in unit_src[key]:
                step = max(1, 1024 // b)
                for a0 in range(0, a, step):
                    a1 = min(a, a0 + step)
                    cnt = (a1 - a0) * b
                    k = self.stg_i % NSTG
                    self.stg_i += 1
                    st = stg32[k]
                    self.dma(st, sv, stg32_ch[k], out_ap=st.ap[:, 0:cnt].rearrange("p (a b) -> p a b", a=a1 - a0),
                             in_ap=sap[:, a0:a1, :])
                    dst = sl.cols(off + a0 * b, off + a0 * b + cnt)
                    self.copy(cast_engs[self.stg_i % 3], dst.ap, st.ap[:, 0:cnt], [st], [dst])
            if NT > 1:
                self.dma(uv, sl, slotst_ch[i], out_ap=uv.ap[:, 0:n], in_ap=sl.ap[:, 0:n])
            return sl

        def unit_seq():
            seq = []
            for l in range(NL):
                for st in self.stages:
                    if st in ('ffn1', 'ffn2'):
                        f = 1 if st == 'ffn1' else 2
                        for u in range(11):
                            seq.append((('gu', l, f, u), 4096))
                        for cp in range(4):
                            for jh in range(2):
                                seq.append((('dn', l, f, cp, jh), 2816))
                    elif st == 'mix':
                        if 'conv' in self.dbg:
                            for i in range(4):
                                seq.append((('cv', l, i), 3072))
                        if 'ret' in self.dbg:
                            for blk in range(3, 7):
                                seq.append((('win', l, blk), 4096))
                        if 'att' in self.dbg:
                            for blk in range(7, 10):
                                seq.append((('win', l, blk), 4096))
                        if 'merge' in self.dbg:
                            for cp in range(4):
                                for i in range(3):
                                    seq.append((('mg', l, cp, i), 3072))
                            for cp in range(4):
                                seq.append((('wo', l, cp), 2048))
            return seq

        useq = unit_seq() * NT
        self.u_next = 0
        self.u_cons = 0
        self.loaded = {}
        PREFETCH = NSLOT - 1

        def prefetch_to(k):
            while self.u_next < min(len(useq), k + 1):
                key, n = useq[self.u_next]
                self.loaded[self.u_next] = load_unit(key, n, self.u_next < len(useq) // NT)
                self.u_next += 1

        def next_unit(expect_kind):
            k = self.u_cons
            assert useq[k][0][0] == expect_kind, (useq[k], expect_kind)
            prefetch_to(k)
            sl = self.loaded.pop(k)
            self.u_cons += 1
            return sl

        def after_unit():
            prefetch_to(self.u_cons + PREFETCH - 1)

        self.nps = None
        self.nk = 0

        def norm_chunk(c):
            if self.nk == 0:
                self.nps = bank(hold=True)
            norm_flush(keep=len(sqr) - 1)
            sqv = sqr[self.nk % len(sqr)]
            xc = xT.sub(c)
            self.act_fn(sqv.ap, xc.ap, AF.Square, [xc], [sqv])
            self.npend.append((sqv, self.nk == 0, self.nk == NCH - 1))
            self.nk += 1

        self.npend = []

        def norm_flush(keep=0):
            while len(self.npend) > keep:
                sqv, st, sp = self.npend.pop(0)
                self.mm(self.nps.ap, ones_bf.ap, sqv.ap, st, sp, [ones_bf, sqv], [self.nps])

        def rmsnorm(widx, out_bf16=True):
            assert self.nk == NCH
            norm_flush()
            ps = self.nps
            self.nk = 0
            self.act_fn(stdb.ap, ps.ap, AF.Ln, [ps], [stdb], scale=1.0 / D, bias=EPS)
            release(ps)
            self.act_fn(rstd.ap, stdb.ap, AF.Exp, [stdb], [rstd], scale=-0.5)
            for c in range(NCH):
                xc = xT.sub(c)
                dst = h.sub(c) if out_bf16 else outT.sub(c)
                wap = normw.ap[:, widx * NCH + c: widx * NCH + c + 1]
                P.add('dve', lambda e, xc=xc, dst=dst, wap=wap: e.scalar_tensor_tensor(
                    out=dst.ap, in0=xc.ap, scalar=wap, in1=rstd.ap, op0=ALU.mult, op1=ALU.mult),
                    [xc, rstd, normw], [dst])

        def proj_fm(sl, wv, ncol0, ps, kchunks=NCH, rhs_src=None):
            src = h if rhs_src is None else rhs_src
            for kc in range(kchunks):
                sk = src.sub(kc)
                self.mm(ps.ap, wv[:, kc, ncol0:ncol0 + 128], sk.ap, kc == 0, kc == kchunks - 1, [sl, sk], [ps])

        def proj_tm(sl, wv, tb, ps):
            for kc in range(NCH):
                hk = h.sub(kc)
                self.mm(ps.ap, hk.ap[:, tb * 128:(tb + 1) * 128], wv[:, kc, :], kc == 0, kc == NCH - 1,
                        [sl, hk], [ps])

        def ffn(l, f):
            rmsnorm(3 * l + (0 if f == 1 else 2))
            for u in range(11):
                sl = next_unit('gu')
                w = sl.ap.rearrange("p (g kc n) -> p g kc n", g=2, kc=8)
                for jj in range(2):
                    j = 2 * u + jj
                    pg, pu = bank(), bank()
                    proj_fm(sl, w[:, 0], jj * 128, pg)
                    proj_fm(sl, w[:, 1], jj * 128, pu)
                    sg = sgb[j % 2]
                    aj = act.sub(j)
                    self.act_fn(sg.ap, pg.ap, AF.Silu, [pg], [sg])
                    self.tt('dve', aj.ap, pu.ap, sg.ap, ALU.mult, [pu, sg], [aj])
                after_unit()
            for cp in range(4):
                pcs = [bank(), bank()]
                for jh in range(2):
                    if jh == 1:
                        norm_flush()
                    sl = next_unit('dn')
                    w = sl.ap[:, 0:2816].rearrange("p (j n) -> p j n", j=11)
                    for jj in range(11):
                        j = jh * 11 + jj
                        aj = act.sub(j)
                        for cc in range(2):
                            self.mm(pcs[cc].ap, w[:, jj, cc * 128:(cc + 1) * 128], aj.ap, j == 0, j == NJ - 1,
                                    [sl, aj], [pcs[cc]])
                    after_unit()
                for cc in range(2):
                    xc = xT.sub(cp * 2 + cc)
                    pc = pcs[cc]
                    P.add('dve', lambda e, xc=xc, pc=pc: e.scalar_tensor_tensor(
                        out=xc.ap, in0=pc.ap, scalar=0.5, in1=xc.ap, op0=ALU.mult, op1=ALU.add),
                        [pc, xc], [xc])
                    norm_chunk(cp * 2 + cc)

        def mixer(l, t):
            rmsnorm(3 * l + 1)
            self.dma(gt, gtab_d, c_gt, in_ap=gtab_d.ap[l].rearrange("p (a b) -> p a b", a=H_ATT))
            cur, prv = t % 2, (t + 1) % 2
            if 'conv' not in self.dbg:
                P.add('pool', lambda e: e.memset(yconv.ap, 0.0), [], [yconv])
            for i in (range(4) if 'conv' in self.dbg else []):
                sl = next_unit('cv')
                w = sl.ap[:, 0:3072].rearrange("p (g kc n) -> p g kc n", g=3, kc=8)
                pu, pb, pc = bank(), bank(), bank()
                proj_fm(sl, w[:, 0], 0, pu)
                proj_fm(sl, w[:, 2], 0, pc)
                proj_fm(sl, w[:, 1], 0, pb)
                after_unit()
                self.copy('act', cu_sb.ap, pu.ap, [pu], [cu_sb])
                ci = carry[l].sub(i)
                if t == 0:
                    P.add('pool', lambda e: e.memset(zb.ap[:, 0:2], 0.0), [], [zb])
                else:
                    self.copy('pool', zb.ap[:, 0:2], ci.ap, [ci], [zb])
                self.tt('dve', zb.ap[:, 2:TT + 2], pc.ap, cu_sb.ap, ALU.mult, [pc, cu_sb, zb], [zb])
                self.copy('pool', ci.ap, zb.ap[:, TT:TT + 2], [zb], [ci])
                wi = (l * 4 + i) * 3
                w0, w1, w2 = (convw.ap[:, wi + k:wi + k + 1] for k in range(3))
                P.add('dve', lambda e, w0=w0: e.tensor_scalar(
                    out=cacc.ap, in0=zb.ap[:, 0:TT], scalar1=w0, scalar2=None, op0=ALU.mult), [zb, convw], [cacc])
                P.add('dve', lambda e, w1=w1: e.scalar_tensor_tensor(
                    out=cacc.ap, in0=zb.ap[:, 1:TT + 1], scalar=w1, in1=cacc.ap, op0=ALU.mult, op1=ALU.add),
                    [zb, convw, cacc], [cacc])
                P.add('dve', lambda e, w2=w2: e.scalar_tensor_tensor(
                    out=cacc.ap, in0=zb.ap[:, 2:TT + 2], scalar=w2, in1=cacc.ap, op0=ALU.mult, op1=ALU.add),
                    [zb, convw, cacc], [cacc])
                yc = yconv.sub(i)
                self.tt('dve', yc.ap, pb.ap, cacc.ap, ALU.mult, [pb, cacc], [yc])
            def _sec_ret():
                for which, dst_tok in ((0, q_tok), (1, k_tok)):
                    sl = next_unit('win')
                    w = sl.ap.rearrange("p (kc n) -> p kc n", kc=8)
                    for tb in range(4):
                        ps = bank()
                        proj_tm(sl, w, tb, ps)
                        psv = ps.ap.rearrange("p (h two f) -> p h two f", h=H_RET, two=2)
                        x1, x2 = psv[:, :, 0, :], psv[:, :, 1, :]
                        dt_ = dst_tok.sub(tb)
                        dv = dt_.ap.rearrange("p (h two f) -> p h two f", h=H_RET, two=2)
                        cb_ = cosT.ap[:, tb, :].unsqueeze(1).broadcast_to([128, H_RET, 64])
                        sb_ = sinT.ap[:, tb, :].unsqueeze(1).broadcast_to([128, H_RET, 64])
                        t1, t2 = rt[0], rt[1]
                        self.tt('dve', t1.ap, x1, cb_, ALU.mult, [ps, cosT], [t1])
                        self.tt('dve', t2.ap, x2, sb_, ALU.mult, [ps, sinT], [t2])
                        self.tt('pool', dv[:, :, 0, :], t1.ap, t2.ap, ALU.subtract, [t1, t2], [dt_])
                        self.tt('dve', t1.ap, x1, sb_, ALU.mult, [ps, sinT], [t1])
                        self.tt('dve', t2.ap, x2, cb_, ALU.mult, [ps, cosT], [t2])
                        self.tt('pool', dv[:, :, 1, :], t1.ap, t2.ap, ALU.add, [t1, t2], [dt_])
                    after_unit()
                if self.retcut == 1:
                    for _ in range(2):
                        next_unit('win'); after_unit()
                    P.add('pool', lambda e: e.memset(yret.ap, 0.0), [], [yret])
                    return
                for tb in range(4):
                    kt_, kd_ = k_tok.sub(tb), Kd_tok.sub(tb)
                    for hd in range(H_RET):
                        sc = kdec.ap[:, tb * H_RET + hd: tb * H_RET + hd + 1]
                        P.add('act', lambda e, kt_=kt_, kd_=kd_, hd=hd, sc=sc: e.activation(
                            out=kd_.ap[:, hd * 128:(hd + 1) * 128], in_=kt_.ap[:, hd * 128:(hd + 1) * 128],
                            func=AF.Copy, scale=sc), [kt_, kdec], [kd_])
                if self.retcut == 2:
                    for _ in range(2):
                        next_unit('win'); after_unit()
                    P.add('pool', lambda e: e.memset(yret.ap, 0.0), [], [yret])
                    return
                sl = next_unit('win')
                w = sl.ap.rearrange("p (kc n) -> p kc n", kc=8)
                for tb in range(4):
                    ps = bank()
                    proj_tm(sl, w, tb, ps)
                    vt = V_tok.sub(tb)
                    self.copy('act', vt.ap, ps.ap, [ps], [vt])
                after_unit()
                if self.retcut == 3:
                    for _ in range(1):
                        next_unit('win'); after_unit()
                    P.add('pool', lambda e: e.memset(yret.ap, 0.0), [], [yret])
                    return
                sl = next_unit('win')
                w = sl.ap.rearrange("p (kc n) -> p kc n", kc=8)
                for i in range(4):
                    ps = bank()
                    proj_fm(sl, w, i * 128, ps)
                    sgi = sgate.sub(i)
                    self.act_fn(sgi.ap, ps.ap, AF.Silu, [ps], [sgi])
                after_unit()
                if self.retcut == 4:
                    for _ in range(0):
                        next_unit('win'); after_unit()
                    P.add('pool', lambda e: e.memset(yret.ap, 0.0), [], [yret])
                    return
                offs = [0, 512, 896, 1152]
                ops_ = {}

                def st_T(hd):
                    hs = slice(hd * 128, (hd + 1) * 128)
                    qt_, qd_, kt_ = QT[hd % 2], QdT[hd % 2], KT[hd % 2]
                    psq, psk = bank(), bank()
                    for tb in range(4):
                        qs, ks = q_tok.sub(tb), k_tok.sub(tb)
                        P.add('pe', lambda e, psq=psq, qs=qs, tb=tb, hs=hs: e.transpose(
                            out=psq.ap[:, tb * 128:(tb + 1) * 128], in_=qs.ap[:, hs], identity=ident.ap),
                            [qs, ident], [psq])
                        P.add('pe', lambda e, psk=psk, ks=ks, tb=tb, hs=hs: e.transpose(
                            out=psk.ap[:, tb * 128:(tb + 1) * 128], in_=ks.ap[:, hs], identity=ident.ap),
                            [ks, ident], [psk])
                    self.copy('act', qt_.ap, psq.ap, [psq], [qt_])
                    self.tt('dve', qd_.ap, psq.ap, qdecT.ap[:, hd, :], ALU.mult, [psq, qdecT], [qd_])
                    self.copy('act', kt_.ap, psk.ap, [psk], [kt_])

                def st_A(hd):
                    qt_, qd_, kt_ = QT[hd % 2], QdT[hd % 2], KT[hd % 2]
                    at = ATb[hd % 2]
                    for b in range(4):
                        n0, wd = 128 * b, TT - 128 * b
                        ps = bank()
                        self.mm(ps.ap[:, 0:wd], kt_.ap[:, b * 128:(b + 1) * 128], qt_.ap[:, n0:TT], True, True,
                                [kt_, qt_], [ps])
                        self.tt('dve', at.ap[:, offs[b]:offs[b] + wd], ps.ap[:, 0:wd], DT.ap[:, hd, 0:wd], ALU.mult,
                                [ps, DT], [at])
                    o_ps = bank(hold=True)
                    ops_[hd] = o_ps
                    if t > 0:
                        sb_h = state_b[l].sub(hd)
                        self.mm(o_ps.ap, sb_h.ap, qd_.ap, True, False, [sb_h, qd_], [o_ps])

                def st_O(hd):
                    hs = slice(hd * 128, (hd + 1) * 128)
                    at = ATb[hd % 2]
                    o_ps = ops_[hd]
                    first = (t == 0)
                    for b in range(4):
                        n0, wd = 128 * b, TT - 128 * b
                        vt = V_tok.sub(b)
                        self.mm(o_ps.ap[:, n0:TT], vt.ap[:, hs], at.ap[:, offs[b]:offs[b] + wd], first, b == 3,
                                [vt, at], [o_ps])
                        first = False
                    self.act_fn(osq.ap, o_ps.ap, AF.Square, [o_ps], [osq])
                    if t < NT - 1:
                        st_ps = bank()
                        for b in range(4):
                            kd_, vt = Kd_tok.sub(b), V_tok.sub(b)
                            self.mm(st_ps.ap[:, 0:128], kd_.ap[:, hs], vt.ap[:, hs], b == 0, b == 3, [kd_, vt], [st_ps])
                        sf, sbh = state_f[l].sub(hd), state_b[l].sub(hd)
                        if t == 0:
                            self.copy('dve', sf.ap, st_ps.ap[:, 0:128], [st_ps], [sf])
                        else:
                            P.add('dve', lambda e, sf=sf, st_ps=st_ps, hd=hd: e.scalar_tensor_tensor(
                                out=sf.ap, in0=sf.ap, scalar=TILE_DECAY[hd], in1=st_ps.ap[:, 0:128],
                                op0=ALU.mult, op1=ALU.add), [sf, st_ps], [sf])
                        self.copy('pool', sbh.ap, sf.ap, [sf], [sbh])

                def st_N(hd):
                    o_ps = ops_.pop(hd)
                    ss = bank()
                    self.mm(ss.ap, ones_bf.ap, osq.ap, True, True, [ones_bf, osq], [ss])
                    self.act_fn(rn.ap, ss.ap, AF.Ln, [ss], [rn], scale=1.0 / 128, bias=EPS)
                    self.act_fn(rn.ap, rn.ap, AF.Exp, [rn], [rn], scale=-0.5)
                    sgi = sgate.sub(hd)
                    self.tt('pool', rgm.ap, rn.ap, sgi.ap, ALU.mult, [rn, sgi], [rgm])
                    yr = yret.sub(hd)
                    self.tt('dve', yr.ap, o_ps.ap, rgm.ap, ALU.mult, [o_ps, rgm], [yr])
                    release(o_ps)

                for fn_, hd_ in ((st_T, 0), (st_T, 1), (st_A, 0), (st_T, 2), (st_O, 0), (st_A, 1), (st_T, 3),
                                 (st_N, 0), (st_O, 1), (st_A, 2), (st_N, 1), (st_O, 2), (st_A, 3), (st_N, 2),
                                 (st_O, 3), (st_N, 3)):
                    fn_(hd_)
            if 'ret' in self.dbg:
                _sec_ret()
                if self.retcut in (5, 6, 7, 8, 51, 52):
                    P.add('pool', lambda e: e.memset(yret.ap, 0.0), [], [yret])
            else:
                P.add('pool', lambda e: e.memset(yret.ap, 0.0), [], [yret])
            def _sec_att():
                sl = next_unit('win')
                w = sl.ap.rearrange("p (kc n) -> p kc n", kc=8)
                P.add('pool', lambda e: e.memset(qm.ap[64:128, :, 0, :], 0.0), [], [qm])
                P.add('pool', lambda e: e.memset(qm.ap[0:64, :, 1, :], 0.0), [], [qm])
                vmall = View(self.scr_t[:, 38 * K1 // 4: (40 * K1 + 512) // 4].bitcast(BF16), 'scr', 38 * K1, 40 * K1 + 512)
                P.add('pool', lambda e: e.memset(vmall.ap, 0.0), [], [vmall])
                P.add('pool', lambda e: e.memset(onesm[0].ap[:, 0:64], 1.0), [], [onesm[0]])
                P.add('pool', lambda e: e.memset(onesm[1].ap[:, 64:128], 1.0), [], [onesm[1]])
                for i in range(4):
                    ps = bank()
                    proj_fm(sl, w, i * 128, ps)
                    qi = qm.sub(i)
                    P.add('act', lambda e, qi=qi, ps=ps: e.mul(out=qi.ap[0:64, 0, :], in_=ps.ap[0:64, :], mul=0.125),
                          [ps], [qi])
                    P.add('act', lambda e, qi=qi, ps=ps: e.mul(out=qi.ap[64:128, 1, :], in_=ps.ap[64:128, :], mul=0.125),
                          [ps], [qi])
                after_unit()
                sl = next_unit('win')
                w = sl.ap.rearrange("p (kc n) -> p kc n", kc=8)
                kcur = kT_att[l].sub(cur)
                for i in range(4):
                    ps = bank()
                    proj_fm(sl, w, i * 128, ps)
                    ki = kcur.sub(i)
                    self.copy('dve' if i % 2 else 'act', ki.ap, ps.ap, [ps], [ki])
                after_unit()
                sl = next_unit('win')
                w = sl.ap.rearrange("p (kc n) -> p kc n", kc=8)
                vcur = v_att[l].sub(cur)
                for tb in range(4):
                    ps = bank()
                    proj_tm(sl, w, tb, ps)
                    vi = vcur.sub(tb)
                    self.copy('dve' if tb % 2 else 'act', vi.ap, ps.ap, [ps], [vi])
                after_unit()
                blocks = [4, 5, 6, 7] + ([0, 1, 2, 3] if t > 0 else [])
                LA = 3
                units_ = [(hp, bi, b, hh) for hp in range(4) for bi, b in enumerate(blocks) for hh in range(2)]
                acc = {}
                st_ = {}

                def stage_A(ui):
                    hp, bi, b, hh = units_[ui]
                    if hp not in acc:
                        acc[hp] = (bank(hold=True), bank(hold=True))
                    qh = qm.sub(hp).sub(hh)
                    n0 = 64 * max(0, 2 * b - 8)
                    n1 = 64 * (min(7, 2 * b + 1) + 1)
                    wd = n1 - n0
                    half = cur if b >= 4 else prv
                    bb = b % 4
                    kh = kT_att[l].sub(half).sub(hp)
                    head = 2 * hp + hh
                    r0 = hh * 64
                    s_ps = bank()
                    self.mm(s_ps.ap[:, 0:wd], kh.ap[:, bb * 128:(bb + 1) * 128],
                            qh.ap[:, n0:n1], True, True, [kh, qh], [s_ps])
                    vm = Vm[(ui // 2 % 4) * 2 + hh]
                    vh_ = v_att[l].sub(half).sub(bb)
                    P.add('pool', lambda e, vm=vm, vh_=vh_, head=head, r0=r0: e.tensor_copy(
                        out=vm.ap[:, r0:r0 + 64], in_=vh_.ap[:, head * 64:(head + 1) * 64]), [vh_], [vm])
                    pt = PT[ui % len(PT)]
                    fb = fbias.ap[:, l * H_ATT + head: l * H_ATT + head + 1]
                    t0 = n0 + 512 - 128 * b
                    nw = max(0, min(256 - t0, wd))
                    if nw > 0:
                        tmp = stmp[ui % len(stmp)]
                        self.tt('dve', tmp.ap[:, 0:nw], s_ps.ap[:, 0:nw], gt.ap[:, head, t0:t0 + nw], ALU.add,
                                [s_ps, gt], [tmp])
                        self.act_fn(pt.ap[:, 0:nw], tmp.ap[:, 0:nw], AF.Exp, [tmp], [pt])
                    if wd > nw:
                        self.act_fn(pt.ap[:, nw:wd], s_ps.ap[:, nw:wd], AF.Exp, [s_ps, fbias], [pt], bias=fb)
                    if b < 4:
                        P.add('pool', lambda e, pt=pt, wd=wd: e.memset(pt.ap[0:64, wd - 64:wd], 0.0), [], [pt])
                    st_[ui] = (pt, n0, n1, wd, bb, half, head, r0, vm)

                def stage_C(ui):
                    hp, bi, b, hh = units_[ui]
                    pt, n0, n1, wd, bb, half, head, r0, vm = st_.pop(ui)
                    o_ps, den_ps = acc[hp]
                    first_ = (bi == 0 and hh == 0)
                    last_ = (bi == len(blocks) - 1 and hh == 1)
                    self.mm(o_ps.ap[:, n0:n1], vm.ap, pt.ap[:, 0:wd], first_, last_, [vm, pt], [o_ps])
                    om = onesm[hh]
                    self.mm(den_ps.ap[:, n0:n1], om.ap, pt.ap[:, 0:wd], first_, last_, [om, pt], [den_ps])
                    if bi == len(blocks) - 1 and hh == 1:
                        self.act_fn(rec.ap, den_ps.ap, AF.Ln, [den_ps], [rec])
                        self.act_fn(rec.ap, rec.ap, AF.Exp, [rec], [rec], scale=-1.0)
                        ya = yatt.sub(hp)
                        self.tt('dve', ya.ap, o_ps.ap, rec.ap, ALU.mult, [o_ps, rec], [ya])
                        release(o_ps)
                        release(den_ps)

                nu = len(units_)
                for i in range(nu + LA):
                    if i < nu:
                        stage_A(i)
                    if i - LA >= 0:
                        stage_C(i - LA)
            if 'att' in self.dbg:
                _sec_att()
            else:
                P.add('pool', lambda e: e.memset(yatt.ap, 0.0), [], [yatt])
            def _sec_merge():
                ys = (yconv, yret, yatt)
                for cp in range(4):
                    for i in range(3):
                        sl = next_unit('mg')
                        wg = sl.ap[:, 0:2048].rearrange("p (kc n) -> p kc n", kc=8)
                        wb = sl.ap[:, 2048:3072].rearrange("p (kk n) -> p kk n", kk=4)
                        for cc in range(2):
                            pg, pb = bank(), bank()
                            proj_fm(sl, wg, cc * 128, pg)
                            proj_fm(sl, wb, cc * 128, pb, kchunks=4, rhs_src=ys[i])
                            sg_, pr_, ma_ = sig[cc], prd[cc], macc[cc]
                            self.act_fn(sg_.ap, pg.ap, AF.Sigmoid, [pg], [sg_])
                            if i == 0:
                                self.tt('dve', ma_.ap, pb.ap, sg_.ap, ALU.mult, [pb, sg_], [ma_])
                            else:
                                self.tt('dve', pr_.ap, pb.ap, sg_.ap, ALU.mult, [pb, sg_], [pr_])
                                if i == 1:
                                    self.tt('pool', ma_.ap, ma_.ap, pr_.ap, ALU.add, [ma_, pr_], [ma_])
                                else:
                                    mc = mT.sub(cp * 2 + cc)
                                    self.tt('pool', mc.ap, ma_.ap, pr_.ap, ALU.add, [ma_, pr_], [mc])
                        after_unit()
                for cp in range(4):
                    sl = next_unit('wo')
                    w = sl.ap[:, 0:2048].rearrange("p (kc n) -> p kc n", kc=8)
                    for cc in range(2):
                        ps = bank()
                        proj_fm(sl, w, cc * 128, ps, rhs_src=mT)
                        norm_flush(keep=1)
                        xc = xT.sub(cp * 2 + cc)
                        self.tt('dve', xc.ap, ps.ap, xc.ap, ALU.add, [ps, xc], [xc])
                        norm_chunk(cp * 2 + cc)
                    after_unit()
            if 'merge' in self.dbg:
                _sec_merge()

        c_xin = self.chan("c_xin")
        c_out = self.chan("c_out")
        c_gt = self.chan("c_gt")
        c_rope = [self.chan("c_cos"), self.chan("c_sin")]

        def load_x(t):
            self.dma(xs, x, c_xin, in_ap=x.ap[t * TT:(t + 1) * TT].rearrange("(tb p) d -> p tb d", p=128))

        load_x(0)
        for t in range(NT):
            if has_mix:
                self.dma(cosT, cos_d, c_rope[0], in_ap=cos_d.ap[t * TT:(t + 1) * TT].rearrange("(tb p) f -> p tb f", p=128))
                self.dma(sinT, sin_d, c_rope[1], in_ap=sin_d.ap[t * TT:(t + 1) * TT].rearrange("(tb p) f -> p tb f", p=128))
            for c in range(NCH):
                ps = bank()
                for tb in range(4):
                    P.add('pe', lambda e, ps=ps, tb=tb, c=c: e.transpose(
                        out=ps.ap[:, tb * 128:(tb + 1) * 128], in_=xs.ap[:, tb, c * 128:(c + 1) * 128],
                        identity=ident.ap), [xs, ident], [ps])
                xc = xT.sub(c)
                self.copy('dve' if c % 2 else 'act', xc.ap, ps.ap, [ps], [xc])
                norm_flush(keep=1)
                norm_chunk(c)
            for l in range(NL):
                for si, st in enumerate(self.stages):
                    if l == NL - 1 and si == len(self.stages) - 1 and t + 1 < NT:
                        load_x(t + 1)
                    if st == 'ffn1':
                        ffn(l, 1)
                    elif st == 'ffn2':
                        ffn(l, 2)
                    elif st == 'mix':
                        mixer(l, t)
            rmsnorm(3 * NL, out_bf16=False)
            for tb in range(4):
                for half in range(2):
                    ps = bank()
                    for cc in range(4):
                        oc = outT.sub(half * 4 + cc)
                        P.add('pe', lambda e, ps=ps, cc=cc, oc=oc, tb=tb: e.transpose(
                            out=ps.ap[:, cc * 128:(cc + 1) * 128], in_=oc.ap[:, tb * 128:(tb + 1) * 128],
                            identity=ident.ap), [oc, ident], [ps])
                    dst = xo.ap[:, tb, half * 512:(half + 1) * 512]
                    self.copy('dve' if half else 'act', dst, ps.ap, [ps], [xo])
            self.dma(outv, xo, c_out,
                     out_ap=out.ap()[t * TT:(t + 1) * TT].rearrange("(tb p) d -> p tb d", p=128))
        P.add('sp', None, [outv], [])

    def finish(self):
        nc, P = self.nc, self.P
        sems = {e: self.es.enter_context(nc.semaphore("sem_" + e)) for e in Prog.ENGS}
        chan_sems = {c: self.es.enter_context(nc.semaphore(c)) for c in self.chans}
        block = self.es.enter_context(nc.Block())
        P.emit(block, sems, chan_sems)
        self.es.close()
        return nc


def build_program(S, n_layers=L, stages=('ffn1', 'mix', 'ffn2')):
    b = Builder(S, n_layers, stages)
    b.build()
    return b.finish()


def host_inputs(inp, S, n_layers=L):
    nw = []
    for l in range(n_layers):
        nw += [inp["ffn1_norm"][l], inp["mix_norm"][l], inp["ffn2_norm"][l]]
    nw.append(inp["final_norm"])
    normw = _chunked_vec(np.stack(nw)).reshape(128, -1)
    cw = np.asarray(inp["conv_w"], np.float32)[:n_layers]
    convw = cw.reshape(n_layers, 3, 4, 128).transpose(3, 0, 2, 1).reshape(128, -1)
    tabs, _ = _const_tables(S)
    G, fb = _gather_bias(np.asarray(inp["rel_bias"], np.float32)[:n_layers])
    shared = {
        "normw": np.ascontiguousarray(normw, np.float32),
        "ident": np.eye(128, dtype=np.float32),
        "convw": np.ascontiguousarray(convw, np.float32),
        "gtab": G, "fbias": fb,
    }
    shared.update(tabs)
    for k in ("ffn1_w_gate", "ffn1_w_up", "ffn1_w_down", "ffn2_w_gate", "ffn2_w_up", "ffn2_w_down",
              "w_in", "w_branch", "w_merge_gate", "w_out"):
        shared[k] = np.ascontiguousarray(np.asarray(inp[k], np.float32)[:n_layers])
    return shared


_NC_CACHE = {}


def kernel(**inputs):
    x = np.asarray(inputs["x"], np.float32)
    B, S, _ = x.shape
    key = (S,)
    if key not in _NC_CACHE:
        _NC_CACHE[key] = build_program(S)
    nc = _NC_CACHE[key]
    shared = host_inputs(inputs, S)
    in_maps = []
    for b in range(B):
        m = dict(shared)
        m["x"] = np.ascontiguousarray(x[b])
        in_maps.append(m)
    res = run_bass_kernel_spmd(nc, in_maps, core_ids=list(range(B)))
    return np.stack([np.asarray(r["out"], np.float32) for r in res.results], axis=0)
```

```python
import bisect
from contextlib import ExitStack

import numpy as np
import concourse.bass as bass
import concourse.mybir as mybir
from concourse.bass_utils import run_bass_kernel_spmd

F32 = mybir.dt.float32
BF16 = mybir.dt.bfloat16
AF = mybir.ActivationFunctionType
ALU = mybir.AluOpType

D = 1024
DFF = 2816
BW = 512
L = 2
TT = 512
NCH = D // 128
NJ = DFF // 128
EPS = 1e-6
H_RET = 4
H_ATT = 8
NEG = -30000.0


class Arena:
    def __init__(self, size):
        self.los = [0]
        self.segs = [[0, size, None, {}]]

    def _split(self, pos):
        i = bisect.bisect_right(self.los, pos) - 1
        s = self.segs[i]
        if s[0] < pos < s[1]:
            ns = [pos, s[1], s[2], dict(s[3])]
            s[1] = pos
            self.segs.insert(i + 1, ns)
            self.los.insert(i + 1, pos)

    def access(self, lo, hi, op, write):
        self._split(lo)
        self._split(hi)
        i0 = bisect.bisect_left(self.los, lo)
        i1 = bisect.bisect_left(self.los, hi)
        deps = []
        for i in range(i0, i1):
            s = self.segs[i]
            if s[2] is not None:
                deps.append((s[2], 'w'))
            if write:
                for r in s[3].values():
                    deps.append((r, 'r'))
        if write:
            self.segs[i0:i1] = [[lo, hi, op, {}]]
            self.los[i0:i1] = [lo]
        else:
            key = op.eng if op.dma is None else ('dma', op.uid)
            for i in range(i0, i1):
                self.segs[i][3][key] = op
        return deps


class View:
    def __init__(self, ap, arena, lo, hi):
        self.ap, self.arena, self.lo, self.hi = ap, arena, lo, hi

    def sub(self, i):
        n = self.ap.shape[1]
        step = (self.hi - self.lo) // n
        return View(self.ap[:, i], self.arena, self.lo + i * step, self.lo + (i + 1) * step)

    def cols(self, a, b):
        n = self.ap.shape[-1]
        assert len(self.ap.shape) == 2
        step = (self.hi - self.lo) // n
        return View(self.ap[:, a:b], self.arena, self.lo + a * step, self.lo + b * step)

    def with_ap(self, ap):
        return View(ap, self.arena, self.lo, self.hi)


class Op:
    __slots__ = ('eng', 'fn', 'deps', 'signal', 'sigidx', 'dma', 'uid')


class Prog:
    ENGS = ('pe', 'act', 'dve', 'pool', 'sp')

    def __init__(self, nc):
        self.nc = nc
        self.arenas = {}
        self.ops = {e: [] for e in self.ENGS}
        self.chan_count = {}
        self.uid = 0

    def arena(self, name, size):
        self.arenas[name] = Arena(size)
        return name

    def add(self, eng, fn, reads=(), writes=(), chan=None):
        op = Op()
        op.eng, op.fn, op.signal, op.sigidx = eng, fn, False, None
        op.uid = self.uid
        self.uid += 1
        if chan is not None:
            self.chan_count[chan] = self.chan_count.get(chan, 0) + 16
            op.dma = (chan, self.chan_count[chan])
        else:
            op.dma = None
        raw = []
        for v in reads:
            raw += self.arenas[v.arena].access(v.lo, v.hi, op, v.arena == 'psum')
        for v in writes:
            raw += self.arenas[v.arena].access(v.lo, v.hi, op, True)
        deps = {}
        for d, kind in raw:
            if d is op:
                continue
            if d.dma is None and op.dma is None and d.eng == eng:
                if eng == 'pe' or kind == 'r':
                    continue
            deps[d.uid] = d
            if d.dma is None:
                d.signal = True
        op.deps = list(deps.values())
        self.ops[eng].append(op)
        return op

    def emit(self, block, sems, chan_sems):
        nc = self.nc
        for e in self.ENGS:
            k = 0
            for op in self.ops[e]:
                if op.dma is None and op.signal:
                    k += 1
                    op.sigidx = k
        engobj = {'pe': 'tensor', 'act': 'scalar', 'dve': 'vector', 'pool': 'gpsimd', 'sp': 'sync'}

        def run(e):
            def body(eng):
                seen = {}
                for op in self.ops[e]:
                    for d in op.deps:
                        if d.dma is not None:
                            key, val = ('c', d.dma[0]), d.dma[1]
                            sem = chan_sems[d.dma[0]]
                        else:
                            key, val = ('e', d.eng), d.sigidx
                            sem = sems[d.eng]
                        if seen.get(key, 0) >= val:
                            continue
                        seen[key] = val
                        eng.wait_ge(sem, val)
                    if op.fn is None:
                        continue
                    inst = op.fn(eng)
                    if op.dma is not None:
                        inst.then_inc(chan_sems[op.dma[0]], 16)
                    elif op.signal:
                        inst.then_inc(sems[e], 1)
            return body

        for e in self.ENGS:
            if self.ops[e]:
                getattr(block, engobj[e])(run(e))


def _chunked_vec(v):
    v = np.asarray(v, np.float32)
    lead = v.shape[:-1]
    return np.ascontiguousarray(np.moveaxis(v.reshape(lead + (v.shape[-1] // 128, 128)), -1, 0))


def _const_tables(S):
    inv_freq = (np.float32(10000.0) ** (-np.linspace(0.0, 1.0, 64, dtype=np.float32))).astype(np.float32)
    ang = (np.arange(S, dtype=np.float32)[:, None] * inv_freq[None, :]).astype(np.float32)
    cos = np.cos(ang.astype(np.float64)).astype(np.float32)
    sin = np.sin(ang.astype(np.float64)).astype(np.float32)
    lg = np.log1p(-np.exp2(-5.0 - np.arange(H_RET, dtype=np.float64)))
    p = np.arange(128)[:, None]
    j = np.arange(TT)[None, :]
    scale = 128.0 ** -0.5
    DT = np.zeros((128, H_RET, TT), np.float64)
    for hd in range(H_RET):
        m = np.exp(lg[hd] * np.abs(j - p)) * ((p // 64) <= (j // 64))
        DT[:, hd, :] = m * scale
    qdec = np.exp(lg[:, None] * (np.arange(TT)[None, :] + 1.0)) * scale
    qdecT = np.broadcast_to(qdec[None], (128, H_RET, TT))
    pos = np.arange(4)[None, :, None] * 128 + np.arange(128)[:, None, None]
    kdec = np.exp(lg[None, None, :] * (TT - 1.0 - pos))
    tile_decay = [float(np.exp(lg[hd] * TT)) for hd in range(H_RET)]
    return dict(rope_cos=cos, rope_sin=sin,
                DT=np.ascontiguousarray(DT.reshape(128, -1), np.float32),
                qdecT=np.ascontiguousarray(qdecT.reshape(128, -1), np.float32),
                kdec=np.ascontiguousarray(kdec.reshape(128, -1), np.float32)), tile_decay


def _gather_bias(rel_bias):
    nl = rel_bias.shape[0]
    mp = np.arange(128)[:, None]
    t = np.arange(256)[None, :]
    idx = np.clip(t - mp, -128, 128) + 128
    valid = ((t // 64) - (mp // 64)) >= 0
    G = np.empty((nl, 128, H_ATT, 256), np.float32)
    for l in range(nl):
        for hd in range(H_ATT):
            G[l, :, hd, :] = np.where(valid, rel_bias[l, hd][idx], np.float32(NEG))
    fb = np.broadcast_to(rel_bias[:, :, 256].reshape(1, nl * H_ATT), (128, nl * H_ATT))
    return np.ascontiguousarray(G.reshape(nl, 128, -1)), np.ascontiguousarray(fb, np.float32)


_LG = np.log1p(-np.exp2(-5.0 - np.arange(H_RET, dtype=np.float64)))
TILE_DECAY = [float(np.exp(_LG[hd] * TT)) for hd in range(H_RET)]


class Builder:
    def __init__(self, S, n_layers=L, stages=('ffn1', 'mix', 'ffn2')):
        self.S = S
        self.NT = S // TT
        self.n_layers = n_layers
        self.stages = stages
        self.nc = bass.Bass("TRN2", target_bir_lowering=False)
        self.P = Prog(self.nc)
        self.es = ExitStack()
        self.chans = []
        import os
        self.dbg = set(os.environ.get('MIXDBG', 'conv,ret,att,merge').split(','))
        self.retcut = int(os.environ.get('RETCUT', '99'))

    def din(self, name, shape, dtype=F32):
        t = self.nc.dram_tensor(name, list(shape), dtype, kind="ExternalInput")
        self.P.arena('d_' + name, 1)
        return View(t.ap(), 'd_' + name, 0, 1)

    def dscratch(self, name, shape, dtype):
        t = self.nc.dram_tensor(name, list(shape), dtype, kind="Internal")
        self.P.arena('d_' + name, 1)
        return View(t.ap(), 'd_' + name, 0, 1)

    @staticmethod
    def _shape_ap(ap, free_shape):
        if len(free_shape) == 2:
            ap = ap.rearrange("p (a b) -> p a b", a=free_shape[0])
        elif len(free_shape) == 3:
            ap = ap.rearrange("p (a b c) -> p a b c", a=free_shape[0], b=free_shape[1])
        return ap

    def sb(self, name, free_shape, dtype):
        n = int(np.prod(free_shape))
        t = self.es.enter_context(self.nc.sbuf_tensor(name, [128, n], dtype))
        esz = 2 if dtype == BF16 else 4
        self.P.arena(name, n * esz)
        return View(self._shape_ap(t[:, :], free_shape), name, 0, n * esz)

    def scr(self, off, free_shape, dtype):
        n = int(np.prod(free_shape))
        esz = 2 if dtype == BF16 else 4
        assert off % 4 == 0 and (n * esz) % 4 == 0 and off + n * esz <= self.scr_bytes, (off, n, esz)
        ap = self.scr_t[:, off // 4:(off + n * esz) // 4]
        if dtype == BF16:
            ap = ap.bitcast(BF16)
        return View(self._shape_ap(ap, free_shape), 'scr', off, off + n * esz)

    def chan(self, name):
        self.chans.append(name)
        return name

    def mm(self, out, lhsT, rhs, start, stop, reads, writes):
        self.P.add('pe', lambda e: e.matmul(out, lhsT, rhs, start=start, stop=stop), reads, writes)

    def dma(self, out_v, in_v, chan, out_ap=None, in_ap=None, q='sp'):
        oa = out_v.ap if out_ap is None else out_ap
        ia = in_v.ap if in_ap is None else in_ap
        return self.P.add(q, lambda e: e.dma_start(out=oa, in_=ia), [in_v], [out_v], chan=chan)

    def act_fn(self, out_ap, in_ap, func, reads, writes, **kw):
        self.P.add('act', lambda e: e.activation(out=out_ap, in_=in_ap, func=func, **kw), reads, writes)

    def tt(self, eng, out_ap, in0, in1, op, reads, writes):
        self.P.add(eng, lambda e: e.tensor_tensor(out=out_ap, in0=in0, in1=in1, op=op), reads, writes)

    def copy(self, eng, out_ap, in_ap, reads, writes):
        if eng == 'act':
            self.P.add('act', lambda e: e.copy(out=out_ap, in_=in_ap), reads, writes)
        else:
            self.P.add(eng, lambda e: e.tensor_copy(out=out_ap, in_=in_ap), reads, writes)

    def build(self):
        nc, P = self.nc, self.P
        S, NT, NL = self.S, self.NT, self.n_layers
        K1 = 1024
        self.scr_bytes = 64 * K1
        has_mix = 'mix' in self.stages
        x = self.din("x", [S, D])
        out = self.nc.dram_tensor("out", [S, D], F32, kind="ExternalOutput")
        P.arena('d_out', 1)
        outv = View(out.ap(), 'd_out', 0, 1)
        wf = {}
        for f in (1, 2):
            wf[f] = dict(g=self.din(f"ffn{f}_w_gate", [NL, D, DFF]),
                         u=self.din(f"ffn{f}_w_up", [NL, D, DFF]),
                         d=self.din(f"ffn{f}_w_down", [NL, DFF, D]))
        w_in_d = self.din("w_in", [NL, D, 10 * BW])
        w_br_d = self.din("w_branch", [NL, 3, BW, D])
        w_mg_d = self.din("w_merge_gate", [NL, 3, D, D])
        w_out_d = self.din("w_out", [NL, D, D])
        normw_d = self.din("normw", [128, (3 * NL + 1) * NCH])
        ident_d = self.din("ident", [128, 128])
        convw_d = self.din("convw", [128, NL * 4 * 3])
        cos_d = self.din("rope_cos", [S, 64])
        sin_d = self.din("rope_sin", [S, 64])
        DT_d = self.din("DT", [128, H_RET * TT])
        qdecT_d = self.din("qdecT", [128, H_RET * TT])
        kdec_d = self.din("kdec", [128, 4 * H_RET])
        gtab_d = self.din("gtab", [NL, 128, H_ATT * 256])
        fbias_d = self.din("fbias", [128, NL * H_ATT])

        units = {}
        for l in range(NL):
            for f in (1, 2):
                for u in range(11):
                    units[('gu', l, f, u)] = self.dscratch(f"s_gu_{l}_{f}_{u}", [128, 4096], BF16)
                for cp in range(4):
                    for jh in range(2):
                        units[('dn', l, f, cp, jh)] = self.dscratch(f"s_dn_{l}_{f}_{cp}_{jh}", [128, 2816], BF16)
            for i in range(4):
                units[('cv', l, i)] = self.dscratch(f"s_cv_{l}_{i}", [128, 3072], BF16)
            for blk in range(3, 10):
                units[('win', l, blk)] = self.dscratch(f"s_win_{l}_{blk}", [128, 4096], BF16)
            for cp in range(4):
                for i in range(3):
                    units[('mg', l, cp, i)] = self.dscratch(f"s_mg_{l}_{cp}_{i}", [128, 3072], BF16)
                units[('wo', l, cp)] = self.dscratch(f"s_wo_{l}_{cp}", [128, 2048], BF16)

        xT = self.sb("xT", [NCH, TT], F32)
        h = self.sb("h", [NCH, TT], BF16)
        stdb = self.sb("std", [TT], F32)
        rstd = self.sb("rstd", [TT], F32)
        normw = self.sb("normw_sb", [(3 * NL + 1) * NCH], F32)
        ident = self.sb("ident_sb", [128], F32)
        ident_bf = self.sb("ident_bf", [128], BF16)
        ones_bf = self.sb("ones_bf", [128], BF16)
        NSLOT = 4
        slots = [self.sb(f"wslot{i}", [4096], BF16) for i in range(NSLOT)]
        slot_ch = [self.chan(f"c_slot{i}") for i in range(NSLOT)]
        if has_mix:
            convw = self.sb("convw_sb", [NL * 4 * 3], F32)
            cosT = self.sb("cosT", [4, 64], F32)
            sinT = self.sb("sinT", [4, 64], F32)
            DT = self.sb("DT_sb", [H_RET, TT], F32)
            qdecT = self.sb("qdecT_sb", [H_RET, TT], F32)
            kdec = self.sb("kdec_sb", [4 * H_RET], F32)
            fbias = self.sb("fbias_sb", [NL * H_ATT], F32)
            carry = [self.sb(f"carry{l}", [4, 2], F32) for l in range(NL)]
            state_f = [self.sb(f"state_f{l}", [H_RET, 128], F32) for l in range(NL)]
            state_b = [self.sb(f"state_b{l}", [H_RET, 128], BF16) for l in range(NL)]
            kT_att = [self.sb(f"kT_att{l}", [2, 4, TT], BF16) for l in range(NL)]
            v_att = [self.sb(f"v_att{l}", [2, 4, TT], BF16) for l in range(NL)]
            gt = self.sb("gt", [H_ATT, 256], F32)
        self.scr_t = self.es.enter_context(nc.sbuf_tensor("scr", [128, self.scr_bytes // 4], F32))
        P.arena('scr', self.scr_bytes)
        NSTG = 4
        stg32 = [self.sb(f"stg32_{i}", [1024], F32) for i in range(NSTG)]
        stg32_ch = [self.chan(f"c_stg32_{i}") for i in range(NSTG)]
        slotst_ch = [self.chan(f"c_slotst{i}") for i in range(NSLOT)]
        act = self.scr(0, [NJ, TT], BF16)
        sgb = [self.scr(22 * K1 + i * 2 * K1, [TT], F32) for i in range(2)]
        sqr = [self.scr(60 * K1 + i * K1, [TT], BF16) for i in range(4)]
        outT = self.scr(0, [NCH, TT], F32)
        xo = self.scr(16 * K1, [4, D], F32)
        xs = self.scr(32 * K1, [4, D], F32)
        if has_mix:
            yconv = self.scr(0, [4, TT], BF16)
            yret = self.scr(4 * K1, [4, TT], BF16)
            yatt = self.scr(8 * K1, [4, TT], BF16)
            sgate = self.scr(12 * K1, [4, TT], F32)
            cu_sb = self.scr(20 * K1, [TT], F32)
            zb = self.scr(22 * K1, [TT + 2], F32)
            cacc = self.scr(25 * K1, [TT], F32)
            q_tok = self.scr(27 * K1, [4, BW], F32)
            k_tok = self.scr(35 * K1, [4, BW], F32)
            Kd_tok = self.scr(43 * K1, [4, BW], BF16)
            V_tok = self.scr(47 * K1, [4, BW], BF16)
            rt = [self.scr(51 * K1 + i * K1, [4, 64], F32) for i in range(2)]
            QT = [self.scr(53 * K1 + i * K1, [TT], BF16) for i in range(2)]
            QdT = [self.scr(55 * K1 + i * K1, [TT], BF16) for i in range(2)]
            KT = [self.scr(57 * K1 + i * K1, [TT], BF16) for i in range(2)]
            ATb = [self.scr(59 * K1 + i * 2560, [1280], BF16) for i in range(2)]
            rn = self.scr(20 * K1, [TT], F32)
            osq = self.scr(22 * K1, [TT], BF16)
            rgm = self.scr(23 * K1, [TT], F32)
            qm = self.scr(20 * K1, [4, 2, TT], BF16)
            stmp = [self.scr(28 * K1 + i * 2 * K1, [TT], F32) for i in range(4)]
            Vm = [self.scr(38 * K1 + i * 256, [128], BF16) for i in range(8)]
            onesm = [self.scr(40 * K1 + i * 256, [128], BF16) for i in range(2)]
            PT = [self.scr(54 * K1 + i * K1, [TT], BF16) for i in range(8)]
            rec = self.scr(36 * K1, [TT], F32)
            sig = [self.scr(34 * K1 + i * 2 * K1, [TT], F32) for i in range(2)]
            prd = [self.scr(38 * K1 + i * 2 * K1, [TT], F32) for i in range(2)]
            macc = [self.scr(42 * K1 + i * 2 * K1, [TT], F32) for i in range(2)]
            mT = self.scr(46 * K1, [NCH, TT], BF16)
        pst = self.es.enter_context(nc.psum_tensor("psum", [128, 8 * 512], F32))
        P.arena('psum', 8 * 2048)
        banks = [View(pst[:, b * 512:(b + 1) * 512], 'psum', b * 2048, (b + 1) * 2048) for b in range(8)]
        self.bank_i = 0

        self.held = set()

        def bank(hold=False):
            while (self.bank_i % 8) in self.held:
                self.bank_i += 1
            k = self.bank_i % 8
            self.bank_i += 1
            if hold:
                self.held.add(k)
            return banks[k]

        def release(bv):
            self.held.discard(bv.lo // 2048)

        def const_load(dst, src):
            self.dma(dst, src, self.chan(f"c_const{len(self.chans)}"))

        const_load(normw, normw_d)
        const_load(ident, ident_d)
        P.add('pool', lambda e: e.memset(ones_bf.ap, 1.0), [], [ones_bf])
        self.copy('dve', ident_bf.ap, ident.ap, [ident], [ident_bf])
        if has_mix:
            const_load(convw, convw_d)
            const_load(DT, DT_d.with_ap(DT_d.ap.rearrange("p (a b) -> p a b", a=H_RET)))
            const_load(qdecT, qdecT_d.with_ap(qdecT_d.ap.rearrange("p (a b) -> p a b", a=H_RET)))
            const_load(kdec, kdec_d)
            const_load(fbias, fbias_d)

        unit_src = {}
        ukey = {id(v): k for k, v in units.items()}

        def prepass_unit(unit_v, pieces):
            unit_src[ukey[id(unit_v)]] = pieces

        for l in range(NL):
            for f in (1, 2):
                if f"ffn{f}" not in self.stages:
                    continue
                g, u_, d_ = wf[f]['g'], wf[f]['u'], wf[f]['d']
                gl = g.ap[l].rearrange("(kc p) n -> p kc n", p=128)
                ul = u_.ap[l].rearrange("(kc p) n -> p kc n", p=128)
                dl = d_.ap[l].rearrange("(j p) n -> p j n", p=128)
                for u in range(11):
                    prepass_unit(units[('gu', l, f, u)],
                                 [(g, gl[:, :, u * 256:(u + 1) * 256], 0, 8, 256),
                                  (u_, ul[:, :, u * 256:(u + 1) * 256], 2048, 8, 256)])
                for cp in range(4):
                    for jh in range(2):
                        prepass_unit(units[('dn', l, f, cp, jh)],
                                     [(d_, dl[:, jh * 11:(jh + 1) * 11, cp * 256:(cp + 1) * 256], 0, 11, 256)])
            if has_mix:
                wl = w_in_d.ap[l].rearrange("(kc p) n -> p kc n", p=128)
                for i in range(4):
                    prepass_unit(units[('cv', l, i)],
                                 [(w_in_d, wl[:, :, g_ * BW + i * 128: g_ * BW + (i + 1) * 128], k_ * 1024, 8, 128)
                                  for k_, g_ in enumerate((0, 1, 2))])
                for blk in range(3, 10):
                    prepass_unit(units[('win', l, blk)], [(w_in_d, wl[:, :, blk * BW:(blk + 1) * BW], 0, 8, BW)])
                for cp in range(4):
                    for i in range(3):
                        mgl = w_mg_d.ap[l, i].rearrange("(kc p) n -> p kc n", p=128)
                        brl = w_br_d.ap[l, i].rearrange("(kk p) n -> p kk n", p=128)
                        prepass_unit(units[('mg', l, cp, i)],
                                     [(w_mg_d, mgl[:, :, cp * 256:(cp + 1) * 256], 0, 8, 256),
                                      (w_br_d, brl[:, :, cp * 256:(cp + 1) * 256], 2048, 4, 256)])
                    wol = w_out_d.ap[l].rearrange("(kc p) n -> p kc n", p=128)
                    prepass_unit(units[('wo', l, cp)], [(w_out_d, wol[:, :, cp * 256:(cp + 1) * 256], 0, 8, 256)])

        self.slot_i = 0

        self.stg_i = 0
        cast_engs = ['act', 'dve', 'pool']

        def load_unit(key, n, first):
            i = self.slot_i % NSLOT
            self.slot_i += 1
            sl = slots[i]
            uv = units[key]
            if not first:
                self.dma(sl, uv, slot_ch[i], out_ap=sl.ap[:, 0:n], in_ap=uv.ap[:, 0:n])
                return sl
            for (sv, sap, off, a, b) in unit_src[key]:
                step = max(1, 1024 // b)
                for a0 in range(0, a, step):
                    a1 = min(a, a0 + step)
                    cnt = (a1 - a0) * b
                    k = self.stg_i % NSTG
                    self.stg_i += 1
                    st = stg32[k]
                    self.dma(st, sv, stg32_ch[k], out_ap=st.ap[:, 0:cnt].rearrange("p (a b) -> p a b", a=a1 - a0),
                             in_ap=sap[:, a0:a1, :])
                    dst = sl.cols(off + a0 * b, off + a0 * b + cnt)
                    self.copy(cast_engs[self.stg_i % 3], dst.ap, st.ap[:, 0:cnt], [st], [dst])
            if NT > 1:
                self.dma(uv, sl, slotst_ch[i], out_ap=uv.ap[:, 0:n], in_ap=sl.ap[:, 0:n])
            return sl

        def unit_seq():
            seq = []
            for l in range(NL):
                for st in self.stages:
                    if st in ('ffn1', 'ffn2'):
                        f = 1 if st == 'ffn1' else 2
                        for u in range(11):
                            seq.append((('gu', l, f, u), 4096))
                        for cp in range(4):
                            for jh in range(2):
                                seq.append((('dn', l, f, cp, jh), 2816))
                    elif st == 'mix':
                        if 'conv' in self.dbg:
                            for i in range(4):
                                seq.append((('cv', l, i), 3072))
                        if 'ret' in self.dbg:
                            for blk in range(3, 7):
                                seq.append((('win', l, blk), 4096))
                        if 'att' in self.dbg:
                            for blk in range(7, 10):
                                seq.append((('win', l, blk), 4096))
                        if 'merge' in self.dbg:
                            for cp in range(4):
                                for i in range(3):
                                    seq.append((('mg', l, cp, i), 3072))
                            for cp in range(4):
                                seq.append((('wo', l, cp), 2048))
            return seq

        useq = unit_seq() * NT
        self.u_next = 0
        self.u_cons = 0
        self.loaded = {}
        PREFETCH = NSLOT - 1

        def prefetch_to(k):
            while self.u_next < min(len(useq), k + 1):
                key, n = useq[self.u_next]
                self.loaded[self.u_next] = load_unit(key, n, self.u_next < len(useq) // NT)
                self.u_next += 1

        def next_unit(expect_kind):
            k = self.u_cons
            assert useq[k][0][0] == expect_kind, (useq[k], expect_kind)
            prefetch_to(k)
            sl = self.loaded.pop(k)
            self.u_cons += 1
            return sl

        def after_unit():
            prefetch_to(self.u_cons + PREFETCH - 1)

        self.nps = None
        self.nk = 0

        def norm_chunk(c):
            if self.nk == 0:
                self.nps = bank(hold=True)
            norm_flush(keep=len(sqr) - 1)
            sqv = sqr[self.nk % len(sqr)]
            xc = xT.sub(c)
            self.act_fn(sqv.ap, xc.ap, AF.Square, [xc], [sqv])
            self.npend.append((sqv, self.nk == 0, self.nk == NCH - 1))
            self.nk += 1

        self.npend = []

        def norm_flush(keep=0):
            while len(self.npend) > keep:
                sqv, st, sp = self.npend.pop(0)
                self.mm(self.nps.ap, ones_bf.ap, sqv.ap, st, sp, [ones_bf, sqv], [self.nps])

        def rmsnorm(widx, out_bf16=True):
            assert self.nk == NCH
            norm_flush()
            ps = self.nps
            self.nk = 0
            self.act_fn(stdb.ap, ps.ap, AF.Ln, [ps], [stdb], scale=1.0 / D, bias=EPS)
            release(ps)
            self.act_fn(rstd.ap, stdb.ap, AF.Exp, [stdb], [rstd], scale=-0.5)
            for c in range(NCH):
                xc = xT.sub(c)
                dst = h.sub(c) if out_bf16 else outT.sub(c)
                wap = normw.ap[:, widx * NCH + c: widx * NCH + c + 1]
                P.add('dve', lambda e, xc=xc, dst=dst, wap=wap: e.scalar_tensor_tensor(
                    out=dst.ap, in0=xc.ap, scalar=wap, in1=rstd.ap, op0=ALU.mult, op1=ALU.mult),
                    [xc, rstd, normw], [dst])

        def proj_fm(sl, wv, ncol0, ps, kchunks=NCH, rhs_src=None):
            src = h if rhs_src is None else rhs_src
            for kc in range(kchunks):
                sk = src.sub(kc)
                self.mm(ps.ap, wv[:, kc, ncol0:ncol0 + 128], sk.ap, kc == 0, kc == kchunks - 1, [sl, sk], [ps])

        def proj_tm(sl, wv, tb, ps):
            for kc in range(NCH):
                hk = h.sub(kc)
                self.mm(ps.ap, hk.ap[:, tb * 128:(tb + 1) * 128], wv[:, kc, :], kc == 0, kc == NCH - 1,
                        [sl, hk], [ps])

        def ffn(l, f):
            rmsnorm(3 * l + (0 if f == 1 else 2))
            for u in range(11):
                sl = next_unit('gu')
                w = sl.ap.rearrange("p (g kc n) -> p g kc n", g=2, kc=8)
                for jj in range(2):
                    j = 2 * u + jj
                    pg, pu = bank(), bank()
                    proj_fm(sl, w[:, 0], jj * 128, pg)
                    proj_fm(sl, w[:, 1], jj * 128, pu)
                    sg = sgb[j % 2]
                    aj = act.sub(j)
                    self.act_fn(sg.ap, pg.ap, AF.Silu, [pg], [sg])
                    self.tt('dve', aj.ap, pu.ap, sg.ap, ALU.mult, [pu, sg], [aj])
                after_unit()
            for cp in range(4):
                pcs = [bank(), bank()]
                for jh in range(2):
                    if jh == 1:
                        norm_flush()
                    sl = next_unit('dn')
                    w = sl.ap[:, 0:2816].rearrange("p (j n) -> p j n", j=11)
                    for jj in range(11):
                        j = jh * 11 + jj
                        aj = act.sub(j)
                        for cc in range(2):
                            self.mm(pcs[cc].ap, w[:, jj, cc * 128:(cc + 1) * 128], aj.ap, j == 0, j == NJ - 1,
                                    [sl, aj], [pcs[cc]])
                    after_unit()
                for cc in range(2):
                    xc = xT.sub(cp * 2 + cc)
                    pc = pcs[cc]
                    P.add('dve', lambda e, xc=xc, pc=pc: e.scalar_tensor_tensor(
                        out=xc.ap, in0=pc.ap, scalar=0.5, in1=xc.ap, op0=ALU.mult, op1=ALU.add),
                        [pc, xc], [xc])
                    norm_chunk(cp * 2 + cc)

        def mixer(l, t):
            rmsnorm(3 * l + 1)
            self.dma(gt, gtab_d, c_gt, in_ap=gtab_d.ap[l].rearrange("p (a b) -> p a b", a=H_ATT))
            cur, prv = t % 2, (t + 1) % 2
            if 'conv' not in self.dbg:
                P.add('pool', lambda e: e.memset(yconv.ap, 0.0), [], [yconv])
            for i in (range(4) if 'conv' in self.dbg else []):
                sl = next_unit('cv')
                w = sl.ap[:, 0:3072].rearrange("p (g kc n) -> p g kc n", g=3, kc=8)
                pu, pb, pc = bank(), bank(), bank()
                proj_fm(sl, w[:, 0], 0, pu)
                proj_fm(sl, w[:, 2], 0, pc)
                proj_fm(sl, w[:, 1], 0, pb)
                after_unit()
                self.copy('act', cu_sb.ap, pu.ap, [pu], [cu_sb])
                ci = carry[l].sub(i)
                if t == 0:
                    P.add('pool', lambda e: e.memset(zb.ap[:, 0:2], 0.0), [], [zb])
                else:
                    self.copy('pool', zb.ap[:, 0:2], ci.ap, [ci], [zb])
                self.tt('dve', zb.ap[:, 2:TT + 2], pc.ap, cu_sb.ap, ALU.mult, [pc, cu_sb, zb], [zb])
                self.copy('pool', ci.ap, zb.ap[:, TT:TT + 2], [zb], [ci])
                wi = (l * 4 + i) * 3
                w0, w1, w2 = (convw.ap[:, wi + k:wi + k + 1] for k in range(3))
                P.add('dve', lambda e, w0=w0: e.tensor_scalar(
                    out=cacc.ap, in0=zb.ap[:, 0:TT], scalar1=w0, scalar2=None, op0=ALU.mult), [zb, convw], [cacc])
                P.add('dve', lambda e, w1=w1: e.scalar_tensor_tensor(
                    out=cacc.ap, in0=zb.ap[:, 1:TT + 1], scalar=w1, in1=cacc.ap, op0=ALU.mult, op1=ALU.add),
                    [zb, convw, cacc], [cacc])
                P.add('dve', lambda e, w2=w2: e.scalar_tensor_tensor(
                    out=cacc.ap, in0=zb.ap[:, 2:TT + 2], scalar=w2, in1=cacc.ap, op0=ALU.mult, op1=ALU.add),
                    [zb, convw, cacc], [cacc])
                yc = yconv.sub(i)
                self.tt('dve', yc.ap, pb.ap, cacc.ap, ALU.mult, [pb, cacc], [yc])
            def _sec_ret():
                for which, dst_tok in ((0, q_tok), (1, k_tok)):
                    sl = next_unit('win')
                    w = sl.ap.rearrange("p (kc n) -> p kc n", kc=8)
                    for tb in range(4):
                        ps = bank()
                        proj_tm(sl, w, tb, ps)
                        psv = ps.ap.rearrange("p (h two f) -> p h two f", h=H_RET, two=2)
                        x1, x2 = psv[:, :, 0, :], psv[:, :, 1, :]
                        dt_ = dst_tok.sub(tb)
                        dv = dt_.ap.rearrange("p (h two f) -> p h two f", h=H_RET, two=2)
                        cb_ = cosT.ap[:, tb, :].unsqueeze(1).broadcast_to([128, H_RET, 64])
                        sb_ = sinT.ap[:, tb, :].unsqueeze(1).broadcast_to([128, H_RET, 64])
                        t1, t2 = rt[0], rt[1]
                        self.tt('dve', t1.ap, x1, cb_, ALU.mult, [ps, cosT], [t1])
                        self.tt('dve', t2.ap, x2, sb_, ALU.mult, [ps, sinT], [t2])
                        self.tt('pool', dv[:, :, 0, :], t1.ap, t2.ap, ALU.subtract, [t1, t2], [dt_])
                        self.tt('dve', t1.ap, x1, sb_, ALU.mult, [ps, sinT], [t1])
                        self.tt('dve', t2.ap, x2, cb_, ALU.mult, [ps, cosT], [t2])
                        self.tt('pool', dv[:, :, 1, :], t1.ap, t2.ap, ALU.add, [t1, t2], [dt_])
                    after_unit()
                if self.retcut == 1:
                    for _ in range(2):
                        next_unit('win'); after_unit()
                    P.add('pool', lambda e: e.memset(yret.ap, 0.0), [], [yret])
                    return
                for tb in range(4):
                    kt_, kd_ = k_tok.sub(tb), Kd_tok.sub(tb)
                    for hd in range(H_RET):
                        sc = kdec.ap[:, tb * H_RET + hd: tb * H_RET + hd + 1]
                        P.add('act', lambda e, kt_=kt_, kd_=kd_, hd=hd, sc=sc: e.activation(
                            out=kd_.ap[:, hd * 128:(hd + 1) * 128], in_=kt_.ap[:, hd * 128:(hd + 1) * 128],
                            func=AF.Copy, scale=sc), [kt_, kdec], [kd_])
                if self.retcut == 2:
                    for _ in range(2):
                        next_unit('win'); after_unit()
                    P.add('pool', lambda e: e.memset(yret.ap, 0.0), [], [yret])
                    return
                sl = next_unit('win')
                w = sl.ap.rearrange("p (kc n) -> p kc n", kc=8)
                for tb in range(4):
                    ps = bank()
                    proj_tm(sl, w, tb, ps)
                    vt = V_tok.sub(tb)
                    self.copy('act', vt.ap, ps.ap, [ps], [vt])
                after_unit()
                if self.retcut == 3:
                    for _ in range(1):
                        next_unit('win'); after_unit()
                    P.add('pool', lambda e: e.memset(yret.ap, 0.0), [], [yret])
                    return
                sl = next_unit('win')
                w = sl.ap.rearrange("p (kc n) -> p kc n", kc=8)
                for i in range(4):
                    ps = bank()
                    proj_fm(sl, w, i * 128, ps)
                    sgi = sgate.sub(i)
                    self.act_fn(sgi.ap, ps.ap, AF.Silu, [ps], [sgi])
                after_unit()
                if self.retcut == 4:
                    for _ in range(0):
                        next_unit('win'); after_unit()
                    P.add('pool', lambda e: e.memset(yret.ap, 0.0), [], [yret])
                    return
                offs = [0, 512, 896, 1152]
                ops_ = {}

                def st_T(hd):
                    hs = slice(hd * 128, (hd + 1) * 128)
                    qt_, qd_, kt_ = QT[hd % 2], QdT[hd % 2], KT[hd % 2]
                    psq, psk = bank(), bank()
                    for tb in range(4):
                        qs, ks = q_tok.sub(tb), k_tok.sub(tb)
                        P.add('pe', lambda e, psq=psq, qs=qs, tb=tb, hs=hs: e.transpose(
                            out=psq.ap[:, tb * 128:(tb + 1) * 128], in_=qs.ap[:, hs], identity=ident.ap),
                            [qs, ident], [psq])
                        P.add('pe', lambda e, psk=psk, ks=ks, tb=tb, hs=hs: e.transpose(
                            out=psk.ap[:, tb * 128:(tb + 1) * 128], in_=ks.ap[:, hs], identity=ident.ap),
                            [ks, ident], [psk])
                    self.copy('act', qt_.ap, psq.ap, [psq], [qt_])
                    self.tt('dve', qd_.ap, psq.ap, qdecT.ap[:, hd, :], ALU.mult, [psq, qdecT], [qd_])
                    self.copy('act', kt_.ap, psk.ap, [psk], [kt_])

                def st_A(hd):
                    qt_, qd_, kt_ = QT[hd % 2], QdT[hd % 2], KT[hd % 2]
                    at = ATb[hd % 2]
                    for b in range(4):
                        n0, wd = 128 * b, TT - 128 * b
                        ps = bank()
                        self.mm(ps.ap[:, 0:wd], kt_.ap[:, b * 128:(b + 1) * 128], qt_.ap[:, n0:TT], True, True,
                                [kt_, qt_], [ps])
                        self.tt('dve', at.ap[:, offs[b]:offs[b] + wd], ps.ap[:, 0:wd], DT.ap[:, hd, 0:wd], ALU.mult,
                                [ps, DT], [at])
                    o_ps = bank(hold=True)
                    ops_[hd] = o_ps
                    if t > 0:
                        sb_h = state_b[l].sub(hd)
                        self.mm(o_ps.ap, sb_h.ap, qd_.ap, True, False, [sb_h, qd_], [o_ps])

                def st_O(hd):
                    hs = slice(hd * 128, (hd + 1) * 128)
                    at = ATb[hd % 2]
                    o_ps = ops_[hd]
                    first = (t == 0)
                    for b in range(4):
                        n0, wd = 128 * b, TT - 128 * b
                        vt = V_tok.sub(b)
                        self.mm(o_ps.ap[:, n0:TT], vt.ap[:, hs], at.ap[:, offs[b]:offs[b] + wd], first, b == 3,
                                [vt, at], [o_ps])
                        first = False
                    self.act_fn(osq.ap, o_ps.ap, AF.Square, [o_ps], [osq])
                    if t < NT - 1:
                        st_ps = bank()
                        for b in range(4):
                            kd_, vt = Kd_tok.sub(b), V_tok.sub(b)
                            self.mm(st_ps.ap[:, 0:128], kd_.ap[:, hs], vt.ap[:, hs], b == 0, b == 3, [kd_, vt], [st_ps])
                        sf, sbh = state_f[l].sub(hd), state_b[l].sub(hd)
                        if t == 0:
                            self.copy('dve', sf.ap, st_ps.ap[:, 0:128], [st_ps], [sf])
                        else:
                            P.add('dve', lambda e, sf=sf, st_ps=st_ps, hd=hd: e.scalar_tensor_tensor(
                                out=sf.ap, in0=sf.ap, scalar=TILE_DECAY[hd], in1=st_ps.ap[:, 0:128],
                                op0=ALU.mult, op1=ALU.add), [sf, st_ps], [sf])
                        self.copy('pool', sbh.ap, sf.ap, [sf], [sbh])

                def st_N(hd):
                    o_ps = ops_.pop(hd)
                    ss = bank()
                    self.mm(ss.ap, ones_bf.ap, osq.ap, True, True, [ones_bf, osq], [ss])
                    self.act_fn(rn.ap, ss.ap, AF.Ln, [ss], [rn], scale=1.0 / 128, bias=EPS)
                    self.act_fn(rn.ap, rn.ap, AF.Exp, [rn], [rn], scale=-0.5)
                    sgi = sgate.sub(hd)
                    self.tt('pool', rgm.ap, rn.ap, sgi.ap, ALU.mult, [rn, sgi], [rgm])
                    yr = yret.sub(hd)
                    self.tt('dve', yr.ap, o_ps.ap, rgm.ap, ALU.mult, [o_ps, rgm], [yr])
                    release(o_ps)

                for fn_, hd_ in ((st_T, 0), (st_T, 1), (st_A, 0), (st_T, 2), (st_O, 0), (st_A, 1), (st_T, 3),
                                 (st_N, 0), (st_O, 1), (st_A, 2), (st_N, 1), (st_O, 2), (st_A, 3), (st_N, 2),
                                 (st_O, 3), (st_N, 3)):
                    fn_(hd_)
            if 'ret' in self.dbg:
                _sec_ret()
                if self.retcut in (5, 6, 7, 8, 51, 52):
                    P.add('pool', lambda e: e.memset(yret.ap, 0.0), [], [yret])
            else:
                P.add('pool', lambda e: e.memset(yret.ap, 0.0), [], [yret])
            def _sec_att():
                sl = next_unit('win')
                w = sl.ap.rearrange("p (kc n) -> p kc n", kc=8)
                P.add('pool', lambda e: e.memset(qm.ap[64:128, :, 0, :], 0.0), [], [qm])
                P.add('pool', lambda e: e.memset(qm.ap[0:64, :, 1, :], 0.0), [], [qm])
                vmall = View(self.scr_t[:, 38 * K1 // 4: (40 * K1 + 512) // 4].bitcast(BF16), 'scr', 38 * K1, 40 * K1 + 512)
                P.add('pool', lambda e: e.memset(vmall.ap, 0.0), [], [vmall])
                P.add('pool', lambda e: e.memset(onesm[0].ap[:, 0:64], 1.0), [], [onesm[0]])
                P.add('pool', lambda e: e.memset(onesm[1].ap[:, 64:128], 1.0), [], [onesm[1]])
                for i in range(4):
                    ps = bank()
                    proj_fm(sl, w, i * 128, ps)
                    qi = qm.sub(i)
                    P.add('act', lambda e, qi=qi, ps=ps: e.mul(out=qi.ap[0:64, 0, :], in_=ps.ap[0:64, :], mul=0.125),
                          [ps], [qi])
                    P.add('act', lambda e, qi=qi, ps=ps: e.mul(out=qi.ap[64:128, 1, :], in_=ps.ap[64:128, :], mul=0.125),
                          [ps], [qi])
                after_unit()
                sl = next_unit('win')
                w = sl.ap.rearrange("p (kc n) -> p kc n", kc=8)
                kcur = kT_att[l].sub(cur)
                for i in range(4):
                    ps = bank()
                    proj_fm(sl, w, i * 128, ps)
                    ki = kcur.sub(i)
                    self.copy('dve' if i % 2 else 'act', ki.ap, ps.ap, [ps], [ki])
                after_unit()
                sl = next_unit('win')
                w = sl.ap.rearrange("p (kc n) -> p kc n", kc=8)
                vcur = v_att[l].sub(cur)
                for tb in range(4):
                    ps = bank()
                    proj_tm(sl, w, tb, ps)
                    vi = vcur.sub(tb)
                    self.copy('dve' if tb % 2 else 'act', vi.ap, ps.ap, [ps], [vi])
                after_unit()
                blocks = [4, 5, 6, 7] + ([0, 1, 2, 3] if t > 0 else [])
                LA = 5
                units_ = [(hp, bi, b, hh) for hp in range(4) for bi, b in enumerate(blocks) for hh in range(2)]
                acc = {}
                st_ = {}

                def stage_A(ui):
                    hp, bi, b, hh = units_[ui]
                    qh = qm.sub(hp).sub(hh)
                    n0 = 64 * max(0, 2 * b - 8)
                    n1 = 64 * (min(7, 2 * b + 1) + 1)
                    wd = n1 - n0
                    half = cur if b >= 4 else prv
                    bb = b % 4
                    kh = kT_att[l].sub(half).sub(hp)
                    head = 2 * hp + hh
                    r0 = hh * 64
                    s_ps = bank()
                    self.mm(s_ps.ap[:, 0:wd], kh.ap[:, bb * 128:(bb + 1) * 128],
                            qh.ap[:, n0:n1], True, True, [kh, qh], [s_ps])
                    vm = Vm[(ui // 2 % 4) * 2 + hh]
                    vh_ = v_att[l].sub(half).sub(bb)
                    P.add('pool', lambda e, vm=vm, vh_=vh_, head=head, r0=r0: e.tensor_copy(
                        out=vm.ap[:, r0:r0 + 64], in_=vh_.ap[:, head * 64:(head + 1) * 64]), [vh_], [vm])
                    pt = PT[ui % len(PT)]
                    fb = fbias.ap[:, l * H_ATT + head: l * H_ATT + head + 1]
                    t0 = n0 + 512 - 128 * b
                    nw = max(0, min(256 - t0, wd))
                    if nw > 0:
                        tmp = stmp[ui % len(stmp)]
                        self.tt('dve', tmp.ap[:, 0:nw], s_ps.ap[:, 0:nw], gt.ap[:, head, t0:t0 + nw], ALU.add,
                                [s_ps, gt], [tmp])
                        self.act_fn(pt.ap[:, 0:nw], tmp.ap[:, 0:nw], AF.Exp, [tmp], [pt])
                    if wd > nw:
                        self.act_fn(pt.ap[:, nw:wd], s_ps.ap[:, nw:wd], AF.Exp, [s_ps, fbias], [pt], bias=fb)
                    if b < 4:
                        P.add('pool', lambda e, pt=pt, wd=wd: e.memset(pt.ap[0:64, wd - 64:wd], 0.0), [], [pt])
                    st_[ui] = (pt, n0, n1, wd, bb, half, head, r0, vm)

                def stage_C(ui):
                    hp, bi, b, hh = units_[ui]
                    pt, n0, n1, wd, bb, half, head, r0, vm = st_.pop(ui)
                    if hp not in acc:
                        acc[hp] = (bank(hold=True), bank(hold=True))
                    o_ps, den_ps = acc[hp]
                    first_ = (bi == 0 and hh == 0)
                    last_ = (bi == len(blocks) - 1 and hh == 1)
                    self.mm(o_ps.ap[:, n0:n1], vm.ap, pt.ap[:, 0:wd], first_, last_, [vm, pt], [o_ps])
                    om = onesm[hh]
                    self.mm(den_ps.ap[:, n0:n1], om.ap, pt.ap[:, 0:wd], first_, last_, [om, pt], [den_ps])
                    if bi == len(blocks) - 1 and hh == 1:
                        self.act_fn(rec.ap, den_ps.ap, AF.Ln, [den_ps], [rec])
                        self.act_fn(rec.ap, rec.ap, AF.Exp, [rec], [rec], scale=-1.0)
                        ya = yatt.sub(hp)
                        self.tt('dve', ya.ap, o_ps.ap, rec.ap, ALU.mult, [o_ps, rec], [ya])
                        release(o_ps)
                        release(den_ps)

                nu = len(units_)
                for i in range(nu + LA):
                    if i < nu:
                        stage_A(i)
                    if i - LA >= 0:
                        stage_C(i - LA)
            if 'att' in self.dbg:
                _sec_att()
            else:
                P.add('pool', lambda e: e.memset(yatt.ap, 0.0), [], [yatt])
            def _sec_merge():
                ys = (yconv, yret, yatt)
                for cp in range(4):
                    for i in range(3):
                        sl = next_unit('mg')
                        wg = sl.ap[:, 0:2048].rearrange("p (kc n) -> p kc n", kc=8)
                        wb = sl.ap[:, 2048:3072].rearrange("p (kk n) -> p kk n", kk=4)
                        for cc in range(2):
                            pg, pb = bank(), bank()
                            proj_fm(sl, wg, cc * 128, pg)
                            proj_fm(sl, wb, cc * 128, pb, kchunks=4, rhs_src=ys[i])
                            sg_, pr_, ma_ = sig[cc], prd[cc], macc[cc]
                            self.act_fn(sg_.ap, pg.ap, AF.Sigmoid, [pg], [sg_])
                            if i == 0:
                                self.tt('dve', ma_.ap, pb.ap, sg_.ap, ALU.mult, [pb, sg_], [ma_])
                            else:
                                self.tt('dve', pr_.ap, pb.ap, sg_.ap, ALU.mult, [pb, sg_], [pr_])
                                if i == 1:
                                    self.tt('pool', ma_.ap, ma_.ap, pr_.ap, ALU.add, [ma_, pr_], [ma_])
                                else:
                                    mc = mT.sub(cp * 2 + cc)
                                    self.tt('pool', mc.ap, ma_.ap, pr_.ap, ALU.add, [ma_, pr_], [mc])
                        after_unit()
                for cp in range(4):
                    sl = next_unit('wo')
                    w = sl.ap[:, 0:2048].rearrange("p (kc n) -> p kc n", kc=8)
                    for cc in range(2):
                        ps = bank()
                        proj_fm(sl, w, cc * 128, ps, rhs_src=mT)
                        norm_flush(keep=1)
                        xc = xT.sub(cp * 2 + cc)
                        self.tt('dve', xc.ap, ps.ap, xc.ap, ALU.add, [ps, xc], [xc])
                        norm_chunk(cp * 2 + cc)
                    after_unit()
            if 'merge' in self.dbg:
                _sec_merge()

        c_xin = self.chan("c_xin")
        c_out = self.chan("c_out")
        c_gt = self.chan("c_gt")
        c_rope = [self.chan("c_cos"), self.chan("c_sin")]

        def load_x(t):
            self.dma(xs, x, c_xin, in_ap=x.ap[t * TT:(t + 1) * TT].rearrange("(tb p) d -> p tb d", p=128))

        def xload(t):
            if has_mix:
                self.dma(cosT, cos_d, c_rope[0], in_ap=cos_d.ap[t * TT:(t + 1) * TT].rearrange("(tb p) f -> p tb f", p=128))
                self.dma(sinT, sin_d, c_rope[1], in_ap=sin_d.ap[t * TT:(t + 1) * TT].rearrange("(tb p) f -> p tb f", p=128))
            for c in range(NCH):
                ps = bank()
                for tb in range(4):
                    P.add('pe', lambda e, ps=ps, tb=tb, c=c: e.transpose(
                        out=ps.ap[:, tb * 128:(tb + 1) * 128], in_=xs.ap[:, tb, c * 128:(c + 1) * 128],
                        identity=ident.ap), [xs, ident], [ps])
                xc = xT.sub(c)
                self.copy('dve' if c % 2 else 'act', xc.ap, ps.ap, [ps], [xc])
                norm_flush(keep=1)
                norm_chunk(c)
            norm_flush()

        load_x(0)
        xload(0)
        for t in range(NT):
            for l in range(NL):
                for si, st in enumerate(self.stages):
                    if l == NL - 1 and si == len(self.stages) - 1 and t + 1 < NT:
                        load_x(t + 1)
                    if st == 'ffn1':
                        ffn(l, 1)
                    elif st == 'ffn2':
                        ffn(l, 2)
                    elif st == 'mix':
                        mixer(l, t)
            rmsnorm(3 * NL, out_bf16=False)
            if t + 1 < NT:
                xload(t + 1)
            for tb in range(4):
                for half in range(2):
                    ps = bank()
                    for cc in range(4):
                        oc = outT.sub(half * 4 + cc)
                        P.add('pe', lambda e, ps=ps, cc=cc, oc=oc, tb=tb: e.transpose(
                            out=ps.ap[:, cc * 128:(cc + 1) * 128], in_=oc.ap[:, tb * 128:(tb + 1) * 128],
                            identity=ident.ap), [oc, ident], [ps])
                    dst = xo.ap[:, tb, half * 512:(half + 1) * 512]
                    self.copy('dve' if half else 'act', dst, ps.ap, [ps], [xo])
            self.dma(outv, xo, c_out,
                     out_ap=out.ap()[t * TT:(t + 1) * TT].rearrange("(tb p) d -> p tb d", p=128))
        P.add('sp', None, [outv], [])

    def finish(self):
        nc, P = self.nc, self.P
        sems = {e: self.es.enter_context(nc.semaphore("sem_" + e)) for e in Prog.ENGS}
        chan_sems = {c: self.es.enter_context(nc.semaphore(c)) for c in self.chans}
        block = self.es.enter_context(nc.Block())
        P.emit(block, sems, chan_sems)
        self.es.close()
        return nc


def build_program(S, n_layers=L, stages=('ffn1', 'mix', 'ffn2')):
    b = Builder(S, n_layers, stages)
    b.build()
    return b.finish()


def host_inputs(inp, S, n_layers=L):
    nw = []
    for l in range(n_layers):
        nw += [inp["ffn1_norm"][l], inp["mix_norm"][l], inp["ffn2_norm"][l]]
    nw.append(inp["final_norm"])
    normw = _chunked_vec(np.stack(nw)).reshape(128, -1)
    cw = np.asarray(inp["conv_w"], np.float32)[:n_layers]
    convw = cw.reshape(n_layers, 3, 4, 128).transpose(3, 0, 2, 1).reshape(128, -1)
    tabs, _ = _const_tables(S)
    G, fb = _gather_bias(np.asarray(inp["rel_bias"], np.float32)[:n_layers])
    shared = {
        "normw": np.ascontiguousarray(normw, np.float32),
        "ident": np.eye(128, dtype=np.float32),
        "convw": np.ascontiguousarray(convw, np.float32),
        "gtab": G, "fbias": fb,
    }
    shared.update(tabs)
    for k in ("ffn1_w_gate", "ffn1_w_up", "ffn1_w_down", "ffn2_w_gate", "ffn2_w_up", "ffn2_w_down",
              "w_in", "w_branch", "w_merge_gate", "w_out"):
        shared[k] = np.ascontiguousarray(np.asarray(inp[k], np.float32)[:n_layers])
    return shared


_NC_CACHE = {}


def kernel(**inputs):
    x = np.asarray(inputs["x"], np.float32)
    B, S, _ = x.shape
    key = (S,)
    if key not in _NC_CACHE:
        _NC_CACHE[key] = build_program(S)
    nc = _NC_CACHE[key]
    shared = host_inputs(inputs, S)
    in_maps = []
    for b in range(B):
        m = dict(shared)
        m["x"] = np.ascontiguousarray(x[b])
        in_maps.append(m)
    res = run_bass_kernel_spmd(nc, in_maps, core_ids=list(range(B)))
    return np.stack([np.asarray(r["out"], np.float32) for r in res.results], axis=0)
```
